# Optimizing a Trainium2 kernel written in Bass

```python
import math
import jax, jax.numpy as jnp
from jax import lax
import numpy as np

D_MODEL = 1024
BATCH = 2
SEQ = 8192
DEPTH = 2

CHUNK = 64
CONV_WIDTH = 512
CONV_GROUPS = 8
CONV_TAPS = 3
N_HEADS = 8
HEAD_DIM = 64
ATTN_WIDTH = N_HEADS * HEAD_DIM
D_FF = 4 * D_MODEL
Q_BLOCK = 128
N_MOD = 6
EPS = 1e-6
IN_COLS = 3 * CONV_WIDTH + 3 * ATTN_WIDTH + 2 * D_MODEL

kernel_name = "hybrid_shortconv_stickbreaking_block"


def rms_norm(x, g):
    xf = x.astype(jnp.float32)
    y = xf * lax.rsqrt(jnp.mean(xf * xf, axis=-1, keepdims=True) + EPS)
    return (y * g.astype(jnp.float32)).astype(x.dtype)


def modulate(h, shift, scale):
    return h * (1.0 + scale[:, None, :]) + shift[:, None, :]


def short_conv_branch(b_gate, c_gate, u, conv_w):
    s = u.shape[1]
    v = c_gate * u
    vp = jnp.pad(v, ((0, 0), (CONV_TAPS - 1, 0), (0, 0)))
    y = sum(conv_w[k] * vp[:, k:k + s, :] for k in range(CONV_TAPS))
    return b_gate * y


def stick_breaking_attention(q, k, v):
    b, s, h, dh = q.shape
    qh = jnp.transpose(q, (0, 2, 1, 3))
    kh = jnp.transpose(k, (0, 2, 1, 3))
    vh = jnp.transpose(v, (0, 2, 1, 3))
    inv_sqrt = 1.0 / math.sqrt(dh)
    key_pos = jnp.arange(s)
    n_blocks = s // Q_BLOCK

    def block(i):
        start = i * Q_BLOCK
        q_blk = lax.dynamic_slice_in_dim(qh, start, Q_BLOCK, axis=2)
        z = jnp.einsum('bhqd,bhkd->bhqk', q_blk, kh).astype(jnp.float32) * inv_sqrt
        q_pos = start + jnp.arange(Q_BLOCK)
        mask = key_pos[None, :] < q_pos[:, None]
        log_1m_beta = jnp.where(mask, jax.nn.log_sigmoid(-z), 0.0)
        suffix = lax.cumsum(log_1m_beta, axis=3, reverse=True) - log_1m_beta
        a = jnp.where(mask, jnp.exp(jax.nn.log_sigmoid(z) + suffix), 0.0)
        return jnp.einsum('bhqk,bhkd->bhqd', a, vh.astype(jnp.float32)).astype(q.dtype)

    o = lax.map(block, jnp.arange(n_blocks))
    o = jnp.transpose(o, (1, 0, 3, 2, 4))
    return o.reshape(b, s, h * dh)


def setup_inputs(seed: int = 0) -> dict:
    key = jax.random.key(seed)
    ks = jax.random.split(key, 16)
    f32 = jnp.float32
    L, D = DEPTH, D_MODEL

    def nrm(k, shape, fan_in):
        return jax.random.normal(k, shape, f32) * (fan_in ** -0.5)

    def gain(k):
        return 1.0 + 0.05 * jax.random.normal(k, (L, D), f32)

    return {
        "x": jax.random.normal(ks[0], (BATCH, SEQ, D), f32),
        "c": jax.random.normal(ks[1], (BATCH, D), f32),
        "w_ada": nrm(ks[2], (L, D, N_MOD * D), D) * 0.5,
        "b_ada": 0.01 * jax.random.normal(ks[3], (L, N_MOD * D), f32),
        "g_pre_mix": gain(ks[4]),
        "g_post_mix": gain(ks[5]),
        "g_pre_mlp": gain(ks[6]),
        "g_post_mlp": gain(ks[7]),
        "w_in": nrm(ks[8], (L, D, IN_COLS), D),
        "conv_w": nrm(ks[9], (L, CONV_TAPS, CONV_WIDTH), CONV_TAPS),
        "w_proj_conv": nrm(ks[10], (L, CONV_WIDTH, D), CONV_WIDTH),
        "w_proj_attn": nrm(ks[11], (L, ATTN_WIDTH, D), ATTN_WIDTH),
        "w_out": nrm(ks[12], (L, D, D), D),
        "w_mlp_in": nrm(ks[13], (L, D, D_FF), D),
        "w_mlp_out": nrm(ks[14], (L, D_FF, D), D_FF),
    }


def reference(x, c, w_ada, b_ada, g_pre_mix, g_post_mix, g_pre_mlp, g_post_mlp,
              w_in, conv_w, w_proj_conv, w_proj_attn, w_out, w_mlp_in, w_mlp_out):
    b, s, d = x.shape
    split_pts = np.cumsum([CONV_WIDTH, CONV_WIDTH, CONV_WIDTH,
                           ATTN_WIDTH, ATTN_WIDTH, ATTN_WIDTH, D_MODEL])
    for l in range(DEPTH):
        mod = c @ w_ada[l] + b_ada[l]
        sh1, sc1, gt1, sh2, sc2, gt2 = jnp.split(mod, N_MOD, axis=-1)

        h = modulate(rms_norm(x, g_pre_mix[l]), sh1, sc1)
        proj = h @ w_in[l]
        bg, cg, u, q, k, v, ga, gb = jnp.split(proj, split_pts, axis=-1)
        y_conv = short_conv_branch(bg, cg, u, conv_w[l]) @ w_proj_conv[l]
        o = stick_breaking_attention(q.reshape(b, s, N_HEADS, HEAD_DIM),
                                     k.reshape(b, s, N_HEADS, HEAD_DIM),
                                     v.reshape(b, s, N_HEADS, HEAD_DIM))
        y_attn = o @ w_proj_attn[l]
        merged = jax.nn.sigmoid(ga) * y_conv + jax.nn.sigmoid(gb) * y_attn
        mix_out = merged @ w_out[l]
        x = x + gt1[:, None, :] * rms_norm(mix_out, g_post_mix[l])

        h2 = modulate(rms_norm(x, g_pre_mlp[l]), sh2, sc2)
        ff = jnp.square(jax.nn.relu(h2 @ w_mlp_in[l])) @ w_mlp_out[l]
        x = x + gt2[:, None, :] * rms_norm(ff, g_post_mlp[l])
    return x
```

```python
import numpy as np
from contextlib import ExitStack
import ml_dtypes
import concourse.bass as bass
import concourse.mybir as mybir
from concourse.bass_utils import run_bass_kernel_spmd

F32 = mybir.dt.float32
BF16 = mybir.dt.bfloat16
AF = mybir.ActivationFunctionType
ALU = mybir.AluOpType

D = 1024
NT = 2048
NB = 16
DFF = 4096
EPS = 1e-6
NCORES = 8
FUSED = True


class Res:
    __slots__ = ("name", "lw", "rd")

    def __init__(self, name=""):
        self.name = name
        self.lw = None
        self.rd = []


class Sched:
    ENGS = ("sync", "scalar", "vector", "gpsimd", "tensor")
    NDMA = 40
    NHW = 24

    def __init__(self, nc, es):
        self.nc = nc
        self._es = es
        self.ops = {e: [] for e in self.ENGS}
        self.epoch = 0
        self.cnt = {e: 0 for e in self.ENGS}
        self.esems = [{e: es.enter_context(nc.semaphore("s0_" + e)) for e in self.ENGS}]
        self.dsem = [es.enter_context(nc.semaphore("d%d" % i)) for i in range(self.NDMA)]
        self.dcnt = [0] * self.NDMA
        self.drr = {"sync": 0, "gpsimd": 0, "scalar": 0}
        self.waited = {e: {} for e in self.ENGS}
        self.ccsem = None

    def _ekey(self, eng):
        return ("eng", eng, self.epoch)

    def _deps(self, eng, reads, writes):
        deps = []
        for r in reads:
            if r.lw is not None:
                deps.append(r.lw)
        for w in writes:
            if w.lw is not None:
                deps.append(w.lw)
            deps.extend(w.rd)
        wd = self.waited[eng]
        best = {}
        for (k, v) in deps:
            if k[0] == "eng":
                if k[2] < self.epoch:
                    continue
                if k[1] == "tensor" and eng == "tensor":
                    continue
            if wd.get(k, 0) >= v:
                continue
            wd[k] = v
            best[k] = max(best.get(k, 0), v)
        return list(best.items())

    def _commit(self, tok, reads, writes):
        for r in reads:
            r.rd.append(tok)
        for w in writes:
            w.lw = tok
            w.rd = []

    def op(self, eng, fn, reads=(), writes=()):
        waits = self._deps(eng, reads, writes)
        self.cnt[eng] += 1
        tok = (self._ekey(eng), self.cnt[eng])
        self.ops[eng].append((fn, waits, self.esems[self.epoch][eng], 1))
        self._commit(tok, reads, writes)
        return tok

    def dma(self, eng, fn, reads=(), writes=()):
        waits = self._deps(eng, reads, writes)
        if eng == "gpsimd":
            n = self.NDMA - self.NHW
            i = self.NHW + self.drr[eng]
            self.drr[eng] = (self.drr[eng] + 1) % n
        else:
            i = self.drr["sync"]
            self.drr["sync"] = (self.drr["sync"] + 1) % self.NHW
        key = ("dma", i)
        if self.dcnt[i] > self.waited[eng].get(key, 0):
            self.waited[eng][key] = self.dcnt[i]
            waits = [w for w in waits if w[0] != key] + [(key, self.dcnt[i])]
        self.dcnt[i] += 16
        tok = (("dma", i), self.dcnt[i])
        self.ops[eng].append((fn, waits, self.dsem[i], 16))
        self._commit(tok, reads, writes)
        return tok

    def coll(self, fn, reads=(), writes=(), blocking=True):
        if self.ccsem is None:
            self.ccsem = self._es.enter_context(self.nc.semaphore("ccsem"))
            self.ccdummy = self._es.enter_context(self.nc.sbuf_tensor("ccdummy", [128, 8], F32))
            self.cccnt = 0
        waits = self._deps("gpsimd", reads, writes)
        self.cccnt += 1
        self.ops["gpsimd"].append((fn, waits, self.ccsem, 1))
        if not blocking:
            tok = (("cc",), self.cccnt)
            self._commit(tok, reads, writes)
            return tok
        self.ops["gpsimd"].append((None, [(("cc",), self.cccnt)], None, 0))
        dm = self.ccdummy
        return self.op("gpsimd", lambda e: e.memset(dm[:], 0.0), reads=reads, writes=writes)

    def barrier(self):
        for e in self.ENGS:
            waits = []
            wd = self.waited[e]
            for k2 in self.ENGS:
                key = self._ekey(k2)
                if self.cnt[k2] > wd.get(key, 0):
                    wd[key] = self.cnt[k2]
                    waits.append((key, self.cnt[k2]))
            for i in range(self.NDMA):
                key = ("dma", i)
                if self.dcnt[i] > wd.get(key, 0):
                    wd[key] = self.dcnt[i]
                    waits.append((key, self.dcnt[i]))
            if waits:
                self.ops[e].append((None, waits, None, 0))
        self.epoch += 1
        self.esems.append({e: self._es.enter_context(self.nc.semaphore("s%d_%s" % (self.epoch, e))) for e in self.ENGS})
        self.cnt = {e: 0 for e in self.ENGS}

    def _semof(self, k):
        if k[0] == "cc":
            return self.ccsem
        if k[0] == "dma":
            return self.dsem[k[1]]
        return self.esems[k[2]][k[1]]

    def final_wait(self, eng, resources):
        waits = self._deps(eng, resources, ())
        self.ops[eng].append((None, waits, None, 0))

    def emit(self, block):
        for e in self.ENGS:
            ops = self.ops[e]
            if not ops:
                continue

            def body(engine, ops=ops):
                for fn, waits, sem, inc in ops:
                    for k, v in waits:
                        engine.wait_ge(self._semof(k), v)
                    if fn is not None:
                        fn(engine).then_inc(sem, inc)

            getattr(block, e)(body)


class Ring:
    def __init__(self, tiles):
        self.tiles = tiles
        self.res = [Res() for _ in tiles]
        self.i = 0

    def next(self):
        t, r = self.tiles[self.i], self.res[self.i]
        self.i = (self.i + 1) % len(self.tiles)
        return t, r


def build(parts):
    nc = bass.Bass("TRN2", target_bir_lowering=False)
    es = ExitStack()
    with es:
        S = Sched(nc, es)
        layers = sorted(set(l for _, l in parts))

        def dram(name, shape, dt, kind):
            return nc.dram_tensor(name, list(shape), dt, kind=kind).ap()

        x_in = dram("x_in", [D, NT], F32, "ExternalInput"); r_xin = Res()
        cT = dram("cT", [128, 8], F32, "ExternalInput")
        w_ada = dram("w_ada_s", [2, D, 1536], F32, "ExternalInput")
        b_ada = dram("b_ada_s", [2, 128, 12], F32, "ExternalInput")
        modc = dram("modc", [128, 12], F32, "Internal"); r_modc = Res()
        modg = dram("modg", [512, 12], F32, "Internal"); r_modg = Res()
        gvec = dram("gvec", [2, 128, 4, 8], F32, "ExternalInput")
        w_in = dram("w_in", [2, D, 5120], F32, "ExternalInput")
        convw = dram("convw", [2, 128, 3, 4], F32, "ExternalInput")
        w_pc = dram("w_pc", [2, 512, D], F32, "ExternalInput")
        w_pa = dram("w_pa", [2, 512, D], F32, "ExternalInput")
        w_out = dram("w_out", [2, D, D], F32, "ExternalInput")
        w1 = dram("w1", [2, D, DFF], F32, "ExternalInput")
        w2 = dram("w2", [2, DFF, D], F32, "ExternalInput")
        maskd = dram("mask", [128, 16, 512], BF16, "ExternalInput")
        wseld = dram("wsel", [128, 4], F32, "ExternalInput")
        trid = dram("tri", [128, 128], BF16, "ExternalInput")

        firstA = parts[0][0] == "A"
        lastA = parts[-1][0] == "A"
        fused_pairs = [l for l in layers if ("A", l) in parts and ("B", l) in parts]
        ckind = "ExternalOutput" if lastA else "Internal"
        has_A = any(p == "A" for p, _ in parts)
        has_B = any(p == "B" for p, _ in parts)
        if has_A:
            contrib = dram("contrib", [1024, NT], BF16, ckind); r_contrib = Res()
            hcon = dram("hcon", [128, 128], F32, ckind); r_hcon = Res()
            qTd_w = dram("qTd_w", [4, 128, NT], BF16, ckind)
        if has_B:
            gk = "ExternalInput" if not firstA else "Internal"
            gath = dram("gath", [4 * 1024, NT], BF16, gk); r_gath = Res(); r_gaths = [Res() for _ in range(4)]
            hgath = dram("hgath", [4 * 128, 128], F32, gk); r_hgath = Res()
            if not firstA:
                qTd_r = dram("qTd_r", [4, 128, NT], BF16, "ExternalInput")
            oTd = dram("oTd", [4, 128, NT], BF16, "Internal"); r_oTd = Res()
            xmid = dram("xmid", [D, NT], F32, "Internal"); r_xmid = Res()
        r_qTd = Res()
        r_qTdh = [Res() for _ in range(4)]
        x_out = None
        if has_B:
            x_out = dram("x_out", [D, NT], F32, "ExternalOutput"); r_xout = Res()
        n_B = sum(1 for p, _ in parts if p == "B")
        if n_B > 1:
            xl0 = dram("xl0", [D, NT], F32, "Internal"); r_xl0 = Res()

        def sb(name, shape, dt):
            return es.enter_context(nc.sbuf_tensor(name, list(shape), dt))

        ARN = 86144
        AR = sb("AR", [128, ARN], BF16)
        CM = sb("CM", [128, 16384], BF16)

        def abf(off, n):
            return AR[:, off:off + n]

        def af32(off, n):
            return AR[:, off:off + 2 * n].bitcast(F32)

        def cf32(i):
            return CM[:, i * 1024:(i + 1) * 1024].bitcast(F32)

        def c2(i):
            return CM[:, i * 1024:(i + 1) * 1024].rearrange("p (e t) -> p e t", e=2)

        rstd = cf32(0); r_rstd = Res()
        lnv = cf32(1); r_lnv = Res()
        tmpA = Ring([cf32(2 + i) for i in range(3)])
        tmpB = Ring([cf32(5 + i) for i in range(3)])
        ones = sb("ones", [128, 128], BF16); r_ones = Res()
        tri = sb("tri_sb", [128, 128], BF16); r_tri = Res()
        wsel = sb("wselt", [128, 4], F32); r_wsel = Res()
        cbf = sb("cbf", [128, 8], BF16); r_cbf = Res()
        badat = sb("badat", [128, 12], F32); r_bada = Res()
        gvt = sb("gvt", [128, 4, 8], F32); r_gvt = Res()
        modt = sb("modt", [128, 48], F32); r_mod = Res()
        coef = sb("coef", [128, 4, 8], F32); r_coef = Res()
        cwt = sb("cwt", [128, 3, 4], F32); r_cwt = Res()
        halo_acc = sb("halo_acc", [128, 4, 16, 2], F32); r_halo = Res()
        hg = sb("hg", [128, 4, 128], F32); r_hg = Res()
        hsel = sb("hsel", [128, 4, 16, 2], F32); r_hsel = Res()
        hh = sb("hh", [128, 8, 8], F32); r_hh = Res()

        class Ctx:
            pass

        PSALL = es.enter_context(nc.psum_tensor("psall", [128, 4096], F32))
        PS = Ring([PSALL[:, i * 512:(i + 1) * 512] for i in range(8)])
        zerosw = sb("zerosw", [128, 128], BF16); r_zw = Res()

        block = es.enter_context(nc.Block())

        S.op("vector", lambda e: e.memset(ones[:], 1.0), writes=[r_ones])
        S.op("vector", lambda e: e.memset(zerosw[:], 0.0), writes=[r_zw])
        S.dma("sync", lambda e: e.dma_start(out=tri[:], in_=trid), writes=[r_tri])
        S.dma("sync", lambda e: e.dma_start(out=wsel[:], in_=wseld), writes=[r_wsel])
        S.dma("gpsimd", lambda e: e.dma_start(out=cbf[:], in_=cT), writes=[r_cbf])

        barrier = S.barrier
        ATT0 = 45056
        CC_GROUPS = [[0, 1, 2, 3], [4, 5, 6, 7]]
        r_chp = [Res() for _ in range(4)]

        def issue_gather(hp):
            S.coll(lambda e: e.collective_compute(
                "AllGather", ALU.bypass, CC_GROUPS, [contrib[hp * 256:(hp + 1) * 256, :].opt()],
                [gath[hp * 1024:(hp + 1) * 1024, :].opt()]), reads=[r_chp[hp]], writes=[r_gaths[hp]], blocking=False)

        def wv3(off, k, n):
            return AR[:, off:off + k * n].rearrange("p (k n) -> p k n", k=k)

        def load_w(dst_view, src_rows_ap, rw, kchunks, nsplit=4):
            src = src_rows_ap.rearrange("(k p) n -> p k n", p=128)
            step = max(1, kchunks // nsplit)
            for k0 in range(0, kchunks, step):
                k1 = min(kchunks, k0 + step)
                S.dma("gpsimd", lambda e, k0=k0, k1=k1: e.dma_start(out=dst_view[:, k0:k1, :], in_=src[:, k0:k1, :]),
                      writes=[rw])

        def evac(i, out_ap, in_ap, reads, writes):
            if i % 2 == 0:
                S.op("scalar", lambda e: e.activation(out=out_ap, in_=in_ap, func=AF.Identity), reads=reads, writes=writes)
            else:
                S.op("vector", lambda e: e.tensor_copy(out=out_ap, in_=in_ap), reads=reads, writes=writes)

        def mk_ctx(off, ncol, with_mo=True):
            cx = Ctx()
            cx.ncol = ncol
            xt0 = af32(off, 8 * ncol).rearrange("p (c t) -> p c t", c=8)
            xt1 = CM[:, 8192:8192 + 16 * ncol].bitcast(F32).rearrange("p (c t) -> p c t", c=8)
            cx.xts = [xt0, xt1]; cx.r_xts = [Res(), Res()]
            cx.xt, cx.r_xt = xt0, cx.r_xts[0]
            off += 16 * ncol
            cx.hb = abf(off, 8 * ncol).rearrange("p (c t) -> p c t", c=8); cx.r_hb = Res()
            off += 8 * ncol
            if with_mo:
                cx.mo = af32(off, 8 * ncol).rearrange("p (c t) -> p c t", c=8); cx.r_mo = Res()
                off += 16 * ncol
            cx.end = off
            return cx

        def sumsq_rstd(cx, src3, r_src, scratch=None):
            ncol = cx.ncol
            hb, r_hb = (cx.hb, cx.r_hb) if scratch is None else scratch
            S.op("scalar", lambda e: e.activation(out=hb[:, :, :], in_=src3, func=AF.Square),
                 reads=[r_src], writes=[r_hb])
            p, rp = PS.next()
            for c in range(8):
                S.op("tensor", lambda e, c=c: e.matmul(p[:, 0:ncol], ones[:], hb[:, c, :], start=(c == 0), stop=(c == 7)),
                     reads=[r_ones, r_hb], writes=[rp])
            S.op("scalar", lambda e: e.activation(out=lnv[:, 0:ncol], in_=p[:, 0:ncol], func=AF.Ln, bias=EPS, scale=1.0 / D),
                 reads=[rp], writes=[r_lnv])
            S.op("scalar", lambda e: e.activation(out=rstd[:, 0:ncol], in_=lnv[:, 0:ncol], func=AF.Exp, scale=-0.5),
                 reads=[r_lnv], writes=[r_rstd])

        def norm_mod(cx, ai, bi):
            ncol = cx.ncol
            xt, hb = cx.xt, cx.hb
            sumsq_rstd(cx, xt[:, :, :], cx.r_xt)
            for c in range(8):
                t, rt = tmpA.next()
                S.op("vector", lambda e, c=c, t=t: e.scalar_tensor_tensor(
                    out=t[:, 0:ncol], in0=xt[:, c, :], scalar=coef[:, ai, c:c + 1], in1=rstd[:, 0:ncol],
                    op0=ALU.mult, op1=ALU.mult), reads=[cx.r_xt, r_coef, r_rstd], writes=[rt])
                S.op("scalar", lambda e, c=c, t=t: e.activation(
                    out=hb[:, c, :], in_=t[:, 0:ncol], func=AF.Identity, bias=modt[:, bi * 8 + c:bi * 8 + c + 1], scale=1.0),
                    reads=[rt, r_mod], writes=[cx.r_hb])

        def post_norm_residual(cx, gi, x_dst, r_dst, col0, scratch, xt, r_xt):
            ncol = cx.ncol
            mo = cx.mo
            sumsq_rstd(cx, mo[:, :, :], cx.r_mo, scratch)
            for c in range(8):
                t, rt = tmpA.next()
                S.op("vector", lambda e, c=c, t=t: e.scalar_tensor_tensor(
                    out=t[:, 0:ncol], in0=mo[:, c, :], scalar=coef[:, gi, c:c + 1], in1=rstd[:, 0:ncol],
                    op0=ALU.mult, op1=ALU.mult), reads=[cx.r_mo, r_coef, r_rstd], writes=[rt])
                S.op("gpsimd", lambda e, c=c, t=t: e.tensor_tensor(
                    out=xt[:, c, :], in0=xt[:, c, :], in1=t[:, 0:ncol], op=ALU.add),
                    reads=[rt, r_xt], writes=[r_xt])
            dst = x_dst.rearrange("(c p) t -> p c t", p=128)[:, :, col0:col0 + ncol]
            S.dma("sync", lambda e: e.dma_start(out=dst, in_=xt[:, :, :]), reads=[r_xt], writes=[r_dst])

        def issue_x(cx, x_src, r_src, t):
            src = x_src.rearrange("(c p) t -> p c t", p=128)[:, :, t * cx.ncol:(t + 1) * cx.ncol]
            dst, rd = cx.xts[t % 2], cx.r_xts[t % 2]
            S.dma("sync", lambda e: e.dma_start(out=dst[:, :, :], in_=src), reads=[r_src], writes=[rd])

        def load_x(cx, x_src, r_src, T, ntiles, manual=False):
            if not manual:
                if T == 0:
                    issue_x(cx, x_src, r_src, 0)
                if T + 1 < ntiles:
                    issue_x(cx, x_src, r_src, T + 1)
            cx.xt, cx.r_xt = cx.xts[T % 2], cx.r_xts[T % 2]

        def mm_group(p, rp, ncol, lhs_fn, rhs_fn, nk, reads):
            for fi in range(nk):
                rd = reads(fi) if callable(reads) else reads
                S.op("tensor", lambda e, fi=fi: e.matmul(p[:, 0:ncol], lhs_fn(fi), rhs_fn(fi), start=(fi == 0), stop=(fi == nk - 1)),
                     reads=rd, writes=[rp])

        def stage_mod(l, pre_A=None):
            barrier()
            S.dma("sync", lambda e: e.dma_start(out=badat[:], in_=b_ada[l]), writes=[r_bada])
            S.dma("sync", lambda e: e.dma_start(out=gvt[:], in_=gvec[l]), writes=[r_gvt])
            S.dma("sync", lambda e: e.dma_start(out=cwt[:], in_=convw[l]), writes=[r_cwt])
            pm, rpm = PS.next()
            wv = wv3(20480, 8, 1536)
            rws = [Res() for _ in range(4)]
            src = w_ada[l].rearrange("(k p) n -> p k n", p=128)
            for g in range(4):
                S.dma("gpsimd", lambda e, g=g: e.dma_start(out=wv[:, 2 * g:2 * g + 2, :], in_=src[:, 2 * g:2 * g + 2, :]), writes=[rws[g]])
            if pre_A is not None:
                pre_A()
            for fo in range(12):
                for fi in range(8):
                    S.op("tensor", lambda e, fo=fo, fi=fi: e.matmul(
                        pm[:, fo:fo + 1], wv[:, fi, fo * 128:(fo + 1) * 128], cbf[:, fi:fi + 1],
                        start=(fi == 0), stop=(fi == 7)), reads=[rws[fi // 2], r_cbf], writes=[rpm])
            mpart = tmpA.tiles[0]; r_mpart = tmpA.res[0]
            S.op("vector", lambda e: e.tensor_tensor(out=mpart[:, 0:12], in0=pm[:, 0:12], in1=badat[:], op=ALU.add),
                 reads=[rpm, r_bada], writes=[r_mpart])
            S.dma("sync", lambda e: e.dma_start(out=modc, in_=mpart[:, 0:12]), reads=[r_mpart], writes=[r_modc])
            S.coll(lambda e: e.collective_compute("AllGather", ALU.bypass, CC_GROUPS, [modc.opt()], [modg.opt()]),
                   reads=[r_modc], writes=[r_modg], blocking=False)
            S.dma("sync", lambda e: e.dma_start(out=modt[:, :].rearrange("p (r f) -> p r f", r=4),
                                                in_=modg.rearrange("(r p) f -> p r f", p=128)), reads=[r_modg], writes=[r_mod])
            S.op("vector", lambda e: e.scalar_tensor_tensor(out=coef[:, 0, :], in0=modt[:, 8:16], scalar=1.0, in1=gvt[:, 0, :],
                                                            op0=ALU.add, op1=ALU.mult), reads=[r_mod, r_gvt], writes=[r_coef])
            S.op("vector", lambda e: e.tensor_tensor(out=coef[:, 1, :], in0=modt[:, 16:24], in1=gvt[:, 1, :], op=ALU.mult),
                 reads=[r_mod, r_gvt], writes=[r_coef])
            S.op("vector", lambda e: e.scalar_tensor_tensor(out=coef[:, 2, :], in0=modt[:, 32:40], scalar=1.0, in1=gvt[:, 2, :],
                                                            op0=ALU.add, op1=ALU.mult), reads=[r_mod, r_gvt], writes=[r_coef])
            S.op("vector", lambda e: e.tensor_tensor(out=coef[:, 3, :], in0=modt[:, 40:48], in1=gvt[:, 3, :], op=ALU.mult),
                 reads=[r_mod, r_gvt], writes=[r_coef])

        def make_pre_A(l, x_src, r_src):
            P = Ctx()
            P.wqkv = wv3(0, 8, 1536); P.rwq = Res()
            P.wcu = wv3(12288, 8, 1024); P.rwc = Res()
            P.xt1 = CM[:, 8192:16384].bitcast(F32).rearrange("p (c t) -> p c t", c=8); P.r_x1 = Res()
            P.xt0 = af32(ATT0, 4096).rearrange("p (c t) -> p c t", c=8); P.r_x0 = Res()
            P.cxs = Ctx(); P.cxs.ncol = 512
            P.cxs.xts = [P.xt1, P.xt0]; P.cxs.r_xts = [P.r_x1, P.r_x0]

            def go():
                load_w(P.wcu, w_in[l][:, 512:1536], P.rwc, 8)
                load_w(P.wqkv, w_in[l][:, 1536:3072], P.rwq, 8)
                issue_x(P.cxs, x_src, r_src, 0)
                issue_x(P.cxs, x_src, r_src, 1)
            P.go = go
            return P

        def stage_A(l, x_src, r_src, do_gather, P):
            barrier()
            wqkv, rwq, wcu, rwc = P.wqkv, P.rwq, P.wcu, P.rwc
            xt1, r_x1 = P.xt1, P.r_x1
            hbT = [abf(20480 + T * 4096, 4096).rearrange("p (c t) -> p c t", c=8) for T in range(4)]
            r_hbT = [Res() for _ in range(4)]
            stg = Ring([abf(36864 + i * 1536, 1536).rearrange("p (c t) -> p c t", c=3) for i in range(4)])
            assert 36864 + 4 * 1536 <= ATT0
            conH = contrib.rearrange("(h s) c -> s h c", h=4)
            xt0, r_x0 = P.xt0, P.r_x0
            ATB.alias = [r_x0]
            cxs = P.cxs
            for T in range(4):
                cx = Ctx(); cx.ncol = 512
                cx.xts = cxs.xts; cx.r_xts = cxs.r_xts; cx.xt, cx.r_xt = cxs.xts[T % 2], cxs.r_xts[T % 2]
                cx.hb, cx.r_hb = hbT[T], r_hbT[T]
                norm_mod(cx, 0, 0)
                if T + 2 < 4:
                    issue_x(cxs, x_src, r_src, T + 2)
                hb = cx.hb
                p, rp = PS.next()
                hview = hb.rearrange("p c (b t) -> p c b t", t=128)
                for fo in range(8):
                    for fi in range(8):
                        S.op("tensor", lambda e, fo=fo, fi=fi, p=p, hview=hview: e.matmul(
                            p[:, fo * 8:(fo + 1) * 8].rearrange("p (b t) -> p b t", t=2), wcu[:, fi, fo * 128:(fo + 1) * 128],
                            hview[:, fi, :, 126:128], start=(fi == 0), stop=(fi == 7)), reads=[rwc, cx.r_hb], writes=[rp])
                S.op("scalar", lambda e, p=p: e.activation(out=hh[:, :, :].rearrange("p a b -> p (a b)"), in_=p[:, 0:64], func=AF.Identity),
                     reads=[rp], writes=[r_hh])
                S.op("vector", lambda e, T=T: e.tensor_tensor(
                    out=halo_acc[:, :, T * 4:(T + 1) * 4, :], in0=hh[:, 0:4, :].rearrange("p c (b t) -> p c b t", t=2),
                    in1=hh[:, 4:8, :].rearrange("p c (b t) -> p c b t", t=2), op=ALU.mult), reads=[r_hh], writes=[r_halo])
            S.dma("sync", lambda e: e.dma_start(out=hcon, in_=halo_acc[:, :, :, :].rearrange("p c i t -> p (c i t)")),
                  reads=[r_halo], writes=[r_hcon])
            for hp in range(4):
                for T in range(4):
                    hb, r_hb = hbT[T], r_hbT[T]
                    sg, rsg = stg.next()
                    for which in range(2):
                        fo = which * 4 + hp
                        p, rp = PS.next()
                        mm_group(p, rp, 512, lambda fi, fo=fo: wqkv[:, fi, fo * 128:(fo + 1) * 128], lambda fi, hb=hb: hb[:, fi, :], 8, [rwq, r_hb])
                        evac(which, sg[:, which, :], p[:, :], [rp], [rsg])
                    p, rp = PS.next()
                    for blk in range(4):
                        for fi in range(8):
                            S.op("tensor", lambda e, blk=blk, fi=fi, p=p, hb=hb, hp=hp: e.matmul(
                                p[:, blk * 128:(blk + 1) * 128], hb[:, fi, blk * 128:(blk + 1) * 128],
                                wqkv[:, fi, 1024 + hp * 128:1024 + (hp + 1) * 128], start=(fi == 0), stop=(fi == 7)),
                                reads=[rwq, r_hb], writes=[rp])
                    evac(1, sg[:, 2, :], p[:, :], [rp], [rsg])
                    S.dma("sync", lambda e, T=T, sg=sg, hp=hp: e.dma_start(out=qTd_w[hp, :, T * 512:(T + 1) * 512], in_=sg[:, 0, :]),
                          reads=[rsg], writes=[r_qTdh[hp]])
                    S.dma("sync", lambda e, T=T, sg=sg, hp=hp: e.dma_start(out=conH[0:128, hp, T * 512:(T + 1) * 512], in_=sg[:, 1, :]),
                          reads=[rsg], writes=[r_chp[hp], r_contrib])
                    S.dma("sync", lambda e, T=T, sg=sg, hp=hp: e.dma_start(
                        out=conH[128:256, hp, T * 512:(T + 1) * 512], in_=sg[:, 2, :]),
                        reads=[rsg], writes=[r_chp[hp], r_contrib])
                if do_gather and hp == 0:
                    issue_gather(0)
                if do_gather and hp == 1:
                    attn_prefetch(qTd_w, "gpsimd")
            if do_gather:
                S.coll(lambda e: e.collective_compute("AllGather", ALU.bypass, CC_GROUPS, [hcon.opt()], [hgath.opt()]),
                       reads=[r_hcon], writes=[r_hgath], blocking=False)

        def load_post_weights(l):
            W = Ctx()
            W.wbcu = wv3(0, 8, 1536); W.wgt = wv3(12288, 8, 2048)
            W.wpc = wv3(28672, 4, 1024); W.wpa = wv3(32768, 4, 1024); W.wo = wv3(36864, 8, 1024)
            W.rw0, W.rw1, W.rw2, W.rw3 = Res(), Res(), Res(), Res()
            todo = []

            def add(dst_view, src_rows_ap, rw, kchunks):
                src = src_rows_ap.rearrange("(k p) n -> p k n", p=128)
                step = max(1, kchunks // 4)
                for k0 in range(0, kchunks, step):
                    k1 = min(kchunks, k0 + step)
                    todo.append(lambda k0=k0, k1=k1: S.dma(
                        "gpsimd", lambda e: e.dma_start(out=dst_view[:, k0:k1, :], in_=src[:, k0:k1, :]), writes=[rw]))
            add(W.wbcu, w_in[l][:, 0:1536], W.rw0, 8)
            add(W.wgt, w_in[l][:, 3072:5120], W.rw1, 8)
            add(W.wpc, w_pc[l], W.rw2, 4)
            add(W.wpa, w_pa[l], W.rw2, 4)
            add(W.wo, w_out[l], W.rw3, 8)
            return W, todo

        ATB = Ctx()
        ATB.KT = abf(ATT0, 8192).rearrange("p (r t) -> p r t", r=4); ATB.r_KT = Res()
        ATB.Vs = [abf(ATT0 + 8192 + i * 8192, 8192).rearrange("p (r i c) -> p r i c", r=4, i=16) for i in range(2)]
        ATB.r_V = [Res(), Res()]
        ATB.maskt = abf(ATT0 + 24576, 8192).rearrange("p (k t) -> p k t", k=16); ATB.r_mask = Res()
        ATB.qpt = abf(ATT0 + 32768, 2048); ATB.r_qpt = Res()
        ATB.r_KTp = [Res() for _ in range(4)]
        ATB.r_qp = [Res() for _ in range(4)]

        def attn_load_V(hp, q="sync"):
            G4 = gath.rearrange("(h r s) c -> h r s c", h=4, r=4)
            pb = hp % 2
            for r in range(4):
                S.dma(q, lambda e, r=r: e.dma_start(
                    out=ATB.Vs[pb][:, r, :, :], in_=G4[hp, r, 128:256, :].rearrange("p (i c) -> p i c", c=128)),
                    reads=[r_gath, r_gaths[hp]], writes=[ATB.r_V[pb]])

        def attn_load_KTpart(hp, j, q="sync"):
            G4 = gath.rearrange("(h r s) c -> h r s c", h=4, r=4)
            S.dma(q, lambda e: e.dma_start(
                out=ATB.KT[:, :, j * 512:(j + 1) * 512], in_=G4[hp, :, 0:128, j * 512:(j + 1) * 512].rearrange("r p t -> p r t")),
                reads=[r_gath, r_gaths[hp]], writes=[ATB.r_KTp[j]] + getattr(ATB, "alias", []))

        def attn_load_qpart(hp, j, qsrc, q="sync"):
            S.dma(q, lambda e: e.dma_start(out=ATB.qpt[:, j * 512:(j + 1) * 512], in_=qsrc[hp][:, j * 512:(j + 1) * 512]),
                  reads=[r_qTd, r_qTdh[hp]], writes=[ATB.r_qp[j]])

        def attn_load_KQ(hp, qsrc, q="sync"):
            for j in range(4):
                attn_load_KTpart(hp, j, q)
                attn_load_qpart(hp, j, qsrc, q)

        def attn_prefetch(qsrc, q="sync"):
            S.dma(q, lambda e: e.dma_start(out=ATB.maskt, in_=maskd), writes=[ATB.r_mask])
            attn_load_KQ(0, qsrc, q)
            attn_load_V(0, q)
            ATB.prefetched = True

        def stage_attn(l, qsrc, fusedA):
            barrier()
            W, wtodo = load_post_weights(l)

            G4 = gath.rearrange("(h r s) c -> h r s c", h=4, r=4)

            KT, r_KT, Vs, r_V = ATB.KT, ATB.r_KT, ATB.Vs, ATB.r_V
            maskt, r_mask, qpt, r_qpt = ATB.maskt, ATB.r_mask, ATB.qpt, ATB.r_qpt
            ost = [abf(ATT0 + 34816 + i * 2048, 2048) for i in range(2)]; r_ost = [Res(), Res()]
            assert ATT0 + 38912 <= ARN
            if not getattr(ATB, "prefetched", False):
                attn_prefetch(qsrc)
            ATB.prefetched = False
            load_V = attn_load_V

            def load_KQ(hp):
                attn_load_KQ(hp, qsrc)

            steps = []
            for hp in range(4):
                for I in range(4):
                    kbs = list(range(16 * I + 15, -1, -1))
                    for n, kb in enumerate(kbs):
                        band = kb >= 16 * I
                        n0 = 128 * ((kb - 16 * I) // 4) if band else 0
                        steps.append(dict(hp=hp, I=I, kb=kb, band=band, n0=n0, first=(n == 0), last=(n == len(kbs) - 1)))

            def ps2(b):
                return PSALL[:, b * 512:(b + 2) * 512].rearrange("p (e t) -> p e t", e=2)

            ZP = Ring([ps2(0), ps2(2)])
            CP = Ring([ps2(4)])
            OB = ps2(6); r_OB = Res()
            Ering = Ring([c2(i) for i in range(0, 4)])
            SPring = Ring([c2(i) for i in range(4, 8)])
            Aring = Ring([c2(i) for i in range(8, 12)])
            Dring = Ring([c2(i) for i in range(12, 14)])
            ACring = Ring([c2(i) for i in range(14, 16)])
            st = {}
            cur_acc = [None]

            def Zpart(k):
                s = steps[k]
                hp, I, kb, n0 = s["hp"], s["I"], s["kb"], s["n0"]
                if I == 1 and s["first"] and hp + 1 < 4:
                    if fusedA:
                        issue_gather(hp + 1)
                    load_V(hp + 1)
                z, rz = ZP.next()
                r, i = kb % 4, kb // 4
                for e_ in range(2):
                    S.op("tensor", lambda e, e_=e_: e.matmul(
                        z[:, e_, n0:512], KT[64 * e_:64 * e_ + 64, r, i * 128:(i + 1) * 128],
                        qpt[64 * e_:64 * e_ + 64, I * 512 + n0:(I + 1) * 512], start=True, stop=True),
                        reads=[ATB.r_KTp[kb // 16], ATB.r_qp[I]], writes=[rz])
                if I == 3 and hp + 1 < 4:
                    if s["first"]:
                        for j in range(3):
                            attn_load_qpart(hp + 1, j, qsrc)
                    if kb % 16 == 0:
                        attn_load_KTpart(hp + 1, kb // 16)
                    if s["last"]:
                        attn_load_qpart(hp + 1, 3, qsrc)
                E, rE = Ering.next()
                S.op("scalar", lambda e: e.activation(out=E[:, :, n0:512], in_=z[:, :, n0:512], func=AF.Exp, scale=0.125),
                     reads=[rz], writes=[rE])
                if s["band"]:
                    kr = kb - 16 * I
                    for e_ in range(2):
                        S.op("vector", lambda e, e_=e_: e.tensor_tensor(out=E[:, e_, n0:512], in0=E[:, e_, n0:512],
                                                                         in1=maskt[:, kr, n0:512], op=ALU.mult),
                             reads=[rE, r_mask], writes=[rE])
                st[k] = dict(E=E, rE=rE)

            def SPpart(k):
                s = steps[k]; b = st[k]; n0 = s["n0"]
                SP, rSP = SPring.next()
                S.op("scalar", lambda e: e.activation(out=SP[:, :, n0:512], in_=b["E"][:, :, n0:512], func=AF.Ln, bias=1.0, scale=1.0),
                     reads=[b["rE"]], writes=[rSP])
                b["SP"] = SP; b["rSP"] = rSP

            def Cpart(k):
                s = steps[k]; b = st[k]; n0 = s["n0"]; first = s["first"]
                c, rc = CP.next()
                if first:
                    acc, racc = ACring.next()
                    S.op("gpsimd", lambda e: e.memset(acc[:, :, :], 0.0), writes=[racc])
                    cur_acc[0] = (acc, racc)
                acc, racc = cur_acc[0]
                for e_ in range(2):
                    S.op("tensor", lambda e, e_=e_: e.matmul(c[:, e_, n0:512], tri[:], b["SP"][:, e_, n0:512], start=True, stop=first),
                         reads=[r_tri, b["rSP"]], writes=[rc])
                    if not first:
                        S.op("tensor", lambda e, e_=e_: e.matmul(c[:, e_, n0:512], ones[:], acc[:, e_, n0:512], start=False, stop=True),
                             reads=[r_ones, racc], writes=[rc])
                Dt, rD = Dring.next()
                S.op("scalar", lambda e: e.activation(out=Dt[:, :, n0:512], in_=c[:, :, n0:512], func=AF.Exp, scale=-1.0),
                     reads=[rc], writes=[rD])
                b["D"] = Dt; b["rD"] = rD

            def Rest(k):
                s = steps[k]; b = st[k]; n0 = s["n0"]
                acc, racc = cur_acc[0]
                if not s["last"]:
                    nacc, rnacc = ACring.next()
                    if n0 > 0:
                        S.op("gpsimd", lambda e: e.memset(nacc[:, :, 0:n0], 0.0), writes=[rnacc])
                    S.op("gpsimd", lambda e: e.tensor_tensor(out=nacc[:, :, n0:512], in0=acc[:, :, n0:512], in1=b["SP"][:, :, n0:512],
                                                             op=ALU.add), reads=[racc, b["rSP"]], writes=[rnacc])
                    cur_acc[0] = (nacc, rnacc)
                A, rA = Aring.next()
                S.op("vector", lambda e: e.tensor_tensor(out=A[:, :, n0:512], in0=b["E"][:, :, n0:512], in1=b["D"][:, :, n0:512], op=ALU.mult),
                     reads=[b["rE"], b["rD"]], writes=[rA])
                b["A"] = A; b["rA"] = rA

            def AVpart(k):
                s = steps[k]; b = st.pop(k); n0 = s["n0"]; hp = s["hp"]; pb = hp % 2
                kb = s["kb"]; r, i = kb % 4, kb // 4
                if s["first"]:
                    for e_ in range(2):
                        S.op("tensor", lambda e, e_=e_: e.matmul(OB[:, e_, :], zerosw[:], maskt[:, 0, :], start=True, stop=False),
                             reads=[r_zw, r_mask], writes=[r_OB])
                last = s["last"]
                for e_ in range(2):
                    S.op("tensor", lambda e, e_=e_: e.matmul(OB[:, e_, n0:512], Vs[pb][:, r, i, :], b["A"][:, e_, n0:512], start=False, stop=last),
                         reads=[r_V[pb], b["rA"]], writes=[r_OB])
                if last:
                    I = s["I"]
                    for e_ in range(2):
                        S.op("scalar", lambda e, e_=e_: e.activation(
                            out=ost[pb][64 * e_:64 * e_ + 64, I * 512:(I + 1) * 512], in_=OB[64 * e_:64 * e_ + 64, e_, :], func=AF.Identity),
                            reads=[r_OB], writes=[r_ost[pb]])
                    if I == 3:
                        S.dma("sync", lambda e: e.dma_start(out=oTd[hp], in_=ost[pb]), reads=[r_ost[pb]], writes=[r_oTd])

            n = len(steps)
            Zpart(0); SPpart(0); Zpart(1); SPpart(1)
            for k in range(n):
                if k >= 48 and k % 8 == 0 and wtodo:
                    wtodo.pop(0)()
                if k + 2 < n:
                    Zpart(k + 2)
                Cpart(k)
                if k + 2 < n:
                    SPpart(k + 2)
                Rest(k)
                if k >= 2:
                    AVpart(k - 2)
            AVpart(n - 2); AVpart(n - 1)
            while wtodo:
                wtodo.pop(0)()
            return W

        def stage_post(l, x_src, r_src, x_dst, r_dst, W):
            barrier()
            wbcu, wgt, wpc, wpa, wo = W.wbcu, W.wgt, W.wpc, W.wpa, W.wo
            rw0, rw1, rw2, rw3 = W.rw0, W.rw1, W.rw2, W.rw3
            cx = mk_ctx(45056, 512)
            hb, mo = cx.hb, cx.mo
            off = cx.end
            vcp = af32(off, 4160).rearrange("p (c b t) -> p c b t", c=4, b=4); r_vcp = Res(); off += 8320
            bgt = af32(off, 2048).rearrange("p (c t) -> p c t", c=4); r_bgt = Res(); off += 4096
            ycv = abf(off, 2048).rearrange("p (c t) -> p c t", c=4); r_ycv = Res(); off += 2048
            otile = abf(off, 2048).rearrange("p (c t) -> p c t", c=4); r_otile = Res(); off += 2048
            mrg = abf(off, 4096).rearrange("p (c t) -> p c t", c=8); r_mrg = Res(); off += 4096
            assert off <= ARN
            S.dma("sync", lambda e: e.dma_start(out=hg[:], in_=hgath.rearrange("(r p) n -> p r n", p=128)),
                  reads=[r_hgath], writes=[r_hg])
            hgv = hg[:, :, :].rearrange("p r (c i t) -> p r c i t", c=4, i=16)
            S.op("vector", lambda e: e.tensor_scalar(out=hsel[:], in0=hgv[:, 0], scalar1=wsel[:, 0:1], scalar2=None, op0=ALU.mult),
                 reads=[r_hg, r_wsel], writes=[r_hsel])
            for r in (1, 2):
                S.op("vector", lambda e, r=r: e.scalar_tensor_tensor(out=hsel[:], in0=hgv[:, r], scalar=wsel[:, r:r + 1], in1=hsel[:],
                                                                     op0=ALU.mult, op1=ALU.add), reads=[r_hg, r_wsel, r_hsel], writes=[r_hsel])
            S.op("vector", lambda e: e.scalar_tensor_tensor(out=hsel[:, :, 1:16, :], in0=hgv[:, 3, :, 0:15, :], scalar=wsel[:, 3:4],
                                                            in1=hsel[:, :, 1:16, :], op0=ALU.mult, op1=ALU.add),
                 reads=[r_hg, r_wsel, r_hsel], writes=[r_hsel])
            oTv = oTd.rearrange("h p t -> p h t")
            def prologue(T):
                load_x(cx, x_src, r_src, T, 4, manual=True)
                norm_mod(cx, 0, 0)

            issue_x(cx, x_src, r_src, 0)
            issue_x(cx, x_src, r_src, 1)
            prologue(0)
            for T in range(4):
                xt_T, r_xt_T = cx.xt, cx.r_xt
                S.dma("sync", lambda e, T=T: e.dma_start(out=otile, in_=oTv[:, :, T * 512:(T + 1) * 512]),
                      reads=[r_oTd], writes=[r_otile])
                S.op("gpsimd", lambda e, T=T: e.tensor_copy(out=vcp[:, :, :, 0:2], in_=hsel[:, :, T * 4:(T + 1) * 4, :]),
                     reads=[r_hsel], writes=[r_vcp])
                for c in range(4):
                    for which in (1, 2, 0):
                        fo = which * 4 + c
                        p, rp = PS.next()
                        mm_group(p, rp, 512, lambda fi, fo=fo: wbcu[:, fi, fo * 128:(fo + 1) * 128], lambda fi: hb[:, fi, :], 8, [rw0, cx.r_hb])
                        if which == 1:
                            S.op("scalar", lambda e, c=c, p=p: e.activation(out=mo[:, c, :], in_=p[:, :], func=AF.Identity),
                                 reads=[rp], writes=[cx.r_mo])
                        elif which == 2:
                            S.op("vector", lambda e, c=c, p=p: e.tensor_tensor(
                                out=vcp[:, c, :, 2:130], in0=mo[:, c, :].rearrange("p (b t) -> p b t", t=128),
                                in1=p[:, :].rearrange("p (b t) -> p b t", t=128), op=ALU.mult),
                                reads=[rp, cx.r_mo], writes=[r_vcp])
                        else:
                            S.op("scalar", lambda e, c=c, p=p: e.activation(out=bgt[:, c, :], in_=p[:, :], func=AF.Identity),
                                 reads=[rp], writes=[r_bgt])
                    t, rt = tmpA.next()
                    tv = t[:, :].rearrange("p (b t) -> p b t", t=128)
                    S.op("scalar", lambda e, c=c, tv=tv: e.activation(out=tv, in_=vcp[:, c, :, 2:130], func=AF.Identity, scale=cwt[:, 2, c:c + 1]),
                         reads=[r_vcp, r_cwt], writes=[rt])
                    S.op("vector", lambda e, c=c, tv=tv: e.scalar_tensor_tensor(out=tv, in0=vcp[:, c, :, 1:129], scalar=cwt[:, 1, c:c + 1], in1=tv,
                                                                                op0=ALU.mult, op1=ALU.add), reads=[r_vcp, r_cwt, rt], writes=[rt])
                    S.op("vector", lambda e, c=c, tv=tv: e.scalar_tensor_tensor(out=tv, in0=vcp[:, c, :, 0:128], scalar=cwt[:, 0, c:c + 1], in1=tv,
                                                                                op0=ALU.mult, op1=ALU.add), reads=[r_vcp, r_cwt, rt], writes=[rt])
                    S.op("vector", lambda e, c=c, t=t: e.tensor_tensor(out=ycv[:, c, :], in0=t[:, :], in1=bgt[:, c, :], op=ALU.mult),
                         reads=[rt, r_bgt], writes=[r_ycv])
                for fo in range(8):
                    pya, rya = PS.next()
                    mm_group(pya, rya, 512, lambda ci, fo=fo: wpa[:, ci, fo * 128:(fo + 1) * 128], lambda ci: otile[:, ci, :], 4, [rw2, r_otile])
                    pga, rga = PS.next()
                    mm_group(pga, rga, 512, lambda fi, fo=fo: wgt[:, fi, fo * 128:(fo + 1) * 128], lambda fi: hb[:, fi, :], 8, [rw1, cx.r_hb])
                    pgb, rgb = PS.next()
                    mm_group(pgb, rgb, 512, lambda fi, fo=fo: wgt[:, fi, 1024 + fo * 128:1024 + (fo + 1) * 128], lambda fi: hb[:, fi, :], 8,
                             [rw1, cx.r_hb])
                    pyc, ryc = PS.next()
                    mm_group(pyc, ryc, 512, lambda ci, fo=fo: wpc[:, ci, fo * 128:(fo + 1) * 128], lambda ci: ycv[:, ci, :], 4, [rw2, r_ycv])
                    sa, rsa = tmpB.next()
                    S.op("scalar", lambda e, p=pga, sa=sa: e.activation(out=sa[:, :], in_=p[:, :], func=AF.Sigmoid), reads=[rga], writes=[rsa])
                    sg, rsg = tmpB.next()
                    S.op("scalar", lambda e, p=pgb, sg=sg: e.activation(out=sg[:, :], in_=p[:, :], func=AF.Sigmoid), reads=[rgb], writes=[rsg])
                    S.op("vector", lambda e, p=pya, sg=sg: e.tensor_tensor(out=sg[:, :], in0=sg[:, :], in1=p[:, :], op=ALU.mult),
                         reads=[rsg, rya], writes=[rsg])
                    S.op("vector", lambda e, p=pyc, sa=sa: e.tensor_tensor(out=sa[:, :], in0=sa[:, :], in1=p[:, :], op=ALU.mult),
                         reads=[rsa, ryc], writes=[rsa])
                    S.op("gpsimd", lambda e, fo=fo, sa=sa, sg=sg: e.tensor_tensor(out=mrg[:, fo, :], in0=sa[:, :], in1=sg[:, :], op=ALU.add),
                         reads=[rsa, rsg], writes=[r_mrg])
                for fo in range(8):
                    if fo == 2 and T + 1 < 4:
                        prologue(T + 1)
                    p, rp = PS.next()
                    mm_group(p, rp, 512, lambda fi, fo=fo: wo[:, fi, fo * 128:(fo + 1) * 128], lambda fi: mrg[:, fi, :], 8, [rw3, r_mrg])
                    evac(fo, mo[:, fo, :], p[:, :], [rp], [cx.r_mo])
                post_norm_residual(cx, 1, x_dst, r_dst, T * 512, (mrg, r_mrg), xt_T, r_xt_T)
                if T + 2 < 4:
                    issue_x(cx, x_src, r_src, T + 2)

        def stage_mlp(l, x_src, r_src, x_dst, r_dst):
            barrier()
            W1v = wv3(0, 8, 4096); W2v = wv3(32768, 32, 1024)
            rw1 = [Res() for _ in range(8)]
            rw2 = [Res() for _ in range(8)]
            for g in range(8):
                S.dma("gpsimd", lambda e, g=g: e.dma_start(
                    out=W1v[:, :, g * 512:(g + 1) * 512], in_=w1[l][:, g * 512:(g + 1) * 512].rearrange("(k p) n -> p k n", p=128)),
                    writes=[rw1[g]])
            for g in range(8):
                S.dma("gpsimd", lambda e, g=g: e.dma_start(
                    out=W2v[:, g * 4:(g + 1) * 4, :], in_=w2[l][g * 512:(g + 1) * 512, :].rearrange("(k p) n -> p k n", p=128)),
                    writes=[rw2[g]])
            NC_ = 256
            cx = mk_ctx(65536, NC_)
            hb, mo = cx.hb, cx.mo
            ff1 = abf(cx.end, 32 * NC_).rearrange("p (k n) -> p k n", k=32); r_ff1 = Res()
            assert cx.end + 32 * NC_ <= ARN
            def prologue(T):
                load_x(cx, x_src, r_src, T, NT // NC_, manual=True)
                norm_mod(cx, 2, 3)

            issue_x(cx, x_src, r_src, 0)
            issue_x(cx, x_src, r_src, 1)
            prologue(0)
            for T in range(NT // NC_):
                xt_T, r_xt_T = cx.xt, cx.r_xt
                for fo in range(32):
                    p, rp = PS.next()
                    mm_group(p, rp, NC_, lambda fi, fo=fo: W1v[:, fi, fo * 128:(fo + 1) * 128], lambda fi: hb[:, fi, :], 8, [rw1[fo // 4], cx.r_hb])
                    t, rt = tmpB.next()
                    S.op("scalar", lambda e, p=p, t=t: e.activation(out=t[:, 0:NC_], in_=p[:, 0:NC_], func=AF.Relu), reads=[rp], writes=[rt])
                    eng = "vector" if fo % 2 == 0 else "gpsimd"
                    S.op(eng, lambda e, fo=fo, t=t: e.tensor_tensor(out=ff1[:, fo, :], in0=t[:, 0:NC_], in1=t[:, 0:NC_], op=ALU.mult),
                         reads=[rt], writes=[r_ff1])
                for fo in range(8):
                    if fo == 3 and T + 1 < NT // NC_:
                        prologue(T + 1)
                    p, rp = PS.next()
                    mm_group(p, rp, NC_, lambda fi, fo=fo: W2v[:, fi, fo * 128:(fo + 1) * 128], lambda fi: ff1[:, fi, :], 32, lambda fi: [rw2[fi // 4], r_ff1])
                    evac(fo, mo[:, fo, :], p[:, 0:NC_], [rp], [cx.r_mo])
                post_norm_residual(cx, 3, x_dst, r_dst, T * NC_, (ff1[:, 0:8, :], r_ff1), xt_T, r_xt_T)
                if T + 2 < NT // NC_:
                    issue_x(cx, x_src, r_src, T + 2)

        cur_x, r_cur = x_in, r_xin
        nB_done = 0
        last_mod = None
        finals = []
        for (kind, l) in parts:
            P = make_pre_A(l, cur_x, r_cur) if kind == "A" else None
            if last_mod != l:
                stage_mod(l, P.go if P is not None else None)
                last_mod = l
            elif P is not None:
                P.go()
            if kind == "A":
                stage_A(l, cur_x, r_cur, ("B", l) in parts, P)
                finals += [r_contrib, r_hcon, r_qTd] + r_qTdh
            else:
                fusedA = ("A", l) in parts
                qsrc = qTd_w if fusedA else qTd_r
                W = stage_attn(l, qsrc, fusedA)
                stage_post(l, cur_x, r_cur, xmid, r_xmid, W)
                nB_done += 1
                if n_B > 1 and nB_done < n_B:
                    dst, rdst = xl0, r_xl0
                else:
                    dst, rdst = x_out, r_xout
                stage_mlp(l, xmid, r_xmid, dst, rdst)
                cur_x, r_cur = dst, rdst
                finals.append(rdst)
        S.final_wait("sync", finals)
        S.emit(block)
    return nc


_PROG_CACHE = {}


def _prog(parts):
    key = tuple(parts)
    if key not in _PROG_CACHE:
        _PROG_CACHE[key] = build(list(parts))
    return _PROG_CACHE[key]


def _core_tokens(j):
    return [4 * i + j for i in range(NB)]


def _prep_static(c, w_ada, b_ada, g_pre_mix, g_post_mix, g_pre_mlp, g_post_mlp, w_in, conv_w,
                 w_proj_conv, w_proj_attn, w_out, w_mlp_in, w_mlp_out):
    f = lambda a: np.ascontiguousarray(np.asarray(a, dtype=np.float32))
    shared = {
        "gvec": f(np.stack([np.asarray(g).reshape(2, 8, 128) for g in (g_pre_mix, g_post_mix, g_pre_mlp, g_post_mlp)], axis=1)
                  .transpose(0, 3, 1, 2)),
        "w_in": f(w_in),
        "convw": f(np.asarray(conv_w).reshape(2, 3, 4, 128).transpose(0, 3, 1, 2)),
        "w_pc": f(w_proj_conv), "w_pa": f(w_proj_attn), "w_out": f(w_out),
        "w1": f(w_mlp_in), "w2": f(w_mlp_out),
        "tri": (np.arange(128)[:, None] >= np.arange(128)[None, :]).astype(ml_dtypes.bfloat16),
    }
    per_core = []
    b_ada_l = np.asarray(b_ada).reshape(2, 48, 128).transpose(0, 2, 1)
    s_idx = np.arange(128)[:, None]
    t_idx = np.arange(128)[None, :]
    for core in range(NCORES):
        b, j = core // 4, core % 4
        mask = np.zeros((128, 16, 4, 128), dtype=np.float32)
        for kr in range(16):
            for m in range(4):
                q = 4 * m + j
                if kr < q:
                    mask[:, kr, m, :] = 1.0
                elif kr == q:
                    mask[:, kr, m, :] = (t_idx > s_idx).astype(np.float32)
        wsel = np.zeros((128, 4), dtype=np.float32)
        wsel[:, (j - 1) % 4] = 1.0
        d = dict(shared)
        d["mask"] = mask.reshape(128, 16, 512).astype(ml_dtypes.bfloat16)
        d["wsel"] = wsel
        d["cT"] = f(np.asarray(c)[b].reshape(8, 128).T)
        d["w_ada_s"] = f(np.asarray(w_ada)[:, :, j * 1536:(j + 1) * 1536])
        d["b_ada_s"] = f(b_ada_l[:, :, j * 12:(j + 1) * 12])
        per_core.append(d)
    return per_core


def _shard_x(x):
    x = np.asarray(x, dtype=np.float32)
    out = []
    for core in range(NCORES):
        b, j = core // 4, core % 4
        xb = x[b].reshape(64, 128, D)[j::4].reshape(NT, D)
        out.append(np.ascontiguousarray(xb.T))
    return out


def _unshard_x(xs):
    out = np.zeros((2, 64, 128, D), dtype=np.float32)
    for core in range(NCORES):
        b, j = core // 4, core % 4
        out[b, j::4] = np.asarray(xs[core]).T.reshape(NB, 128, D)
    return out.reshape(2, 8192, D)


def _gather(res, name, nq=1):
    outs = []
    for core in range(NCORES):
        b = core // 4
        parts = []
        for q in range(nq):
            for r in range(4):
                a = np.asarray(res[b * 4 + r][name])
                n = a.shape[0] // nq
                parts.append(a[q * n:(q + 1) * n])
        outs.append(np.ascontiguousarray(np.concatenate(parts, axis=0)))
    return outs


def kernel(x, c, w_ada, b_ada, g_pre_mix, g_post_mix, g_pre_mlp, g_post_mlp,
           w_in, conv_w, w_proj_conv, w_proj_attn, w_out, w_mlp_in, w_mlp_out):
    static = _prep_static(c, w_ada, b_ada, g_pre_mix, g_post_mix, g_pre_mlp, g_post_mlp, w_in, conv_w,
                          w_proj_conv, w_proj_attn, w_out, w_mlp_in, w_mlp_out)
    xs = _shard_x(x)
    cores = list(range(NCORES))
    if FUSED:
        nc = _prog((("A", 0), ("B", 0), ("A", 1), ("B", 1)))
        in_maps = [dict(static[i], x_in=xs[i]) for i in cores]
        res = run_bass_kernel_spmd(nc, in_maps, core_ids=cores).results
        return _unshard_x([res[i]["x_out"] for i in cores])
    nc1 = _prog((("A", 0),))
    r1 = run_bass_kernel_spmd(nc1, [dict(static[i], x_in=xs[i]) for i in cores], core_ids=cores).results
    g1, h1 = _gather(r1, "contrib", 4), _gather(r1, "hcon")
    nc2 = _prog((("B", 0), ("A", 1)))
    r2 = run_bass_kernel_spmd(nc2, [dict(static[i], x_in=xs[i], gath=g1[i], hgath=h1[i], qTd_r=np.asarray(r1[i]["qTd_w"]))
                                    for i in cores], core_ids=cores).results
    g2, h2 = _gather(r2, "contrib", 4), _gather(r2, "hcon")
    nc3 = _prog((("B", 1),))
    r3 = run_bass_kernel_spmd(nc3, [dict(static[i], x_in=np.asarray(r2[i]["x_out"]), gath=g2[i], hgath=h2[i],
                                         qTd_r=np.asarray(r2[i]["qTd_w"])) for i in cores], core_ids=cores).results
    return _unshard_x([r3[i]["x_out"] for i in cores])
```

```python
import numpy as np
from contextlib import ExitStack
import ml_dtypes
import concourse.bass as bass
import concourse.mybir as mybir
from concourse.bass_utils import run_bass_kernel_spmd

F32 = mybir.dt.float32
BF16 = mybir.dt.bfloat16
AF = mybir.ActivationFunctionType
ALU = mybir.AluOpType

D = 1024
NT = 2048
NB = 16
DFF = 4096
EPS = 1e-6
NCORES = 8
FUSED = True


class Res:
    __slots__ = ("name", "lw", "rd")

    def __init__(self, name=""):
        self.name = name
        self.lw = None
        self.rd = []


class Sched:
    ENGS = ("sync", "scalar", "vector", "gpsimd", "tensor")
    NDMA = 40
    NHW = 24

    def __init__(self, nc, es):
        self.nc = nc
        self._es = es
        self.ops = {e: [] for e in self.ENGS}
        self.epoch = 0
        self.cnt = {e: 0 for e in self.ENGS}
        self.esems = [{e: es.enter_context(nc.semaphore("s0_" + e)) for e in self.ENGS}]
        self.dsem = [es.enter_context(nc.semaphore("d%d" % i)) for i in range(self.NDMA)]
        self.dcnt = [0] * self.NDMA
        self.drr = {"sync": 0, "gpsimd": 0, "scalar": 0}
        self.waited = {e: {} for e in self.ENGS}
        self.ccsem = None

    def _ekey(self, eng):
        return ("eng", eng, self.epoch)

    def _deps(self, eng, reads, writes):
        deps = []
        for r in reads:
            if r.lw is not None:
                deps.append(r.lw)
        for w in writes:
            if w.lw is not None:
                deps.append(w.lw)
            deps.extend(w.rd)
        wd = self.waited[eng]
        best = {}
        for (k, v) in deps:
            if k[0] == "eng":
                if k[2] < self.epoch:
                    continue
                if k[1] == "tensor" and eng == "tensor":
                    continue
            if wd.get(k, 0) >= v:
                continue
            wd[k] = v
            best[k] = max(best.get(k, 0), v)
        return list(best.items())

    def _commit(self, tok, reads, writes):
        for r in reads:
            r.rd.append(tok)
        for w in writes:
            w.lw = tok
            w.rd = []

    def op(self, eng, fn, reads=(), writes=()):
        waits = self._deps(eng, reads, writes)
        self.cnt[eng] += 1
        tok = (self._ekey(eng), self.cnt[eng])
        self.ops[eng].append((fn, waits, self.esems[self.epoch][eng], 1))
        self._commit(tok, reads, writes)
        return tok

    def dma(self, eng, fn, reads=(), writes=()):
        waits = self._deps(eng, reads, writes)
        if eng == "gpsimd":
            n = self.NDMA - self.NHW
            i = self.NHW + self.drr[eng]
            self.drr[eng] = (self.drr[eng] + 1) % n
        else:
            i = self.drr["sync"]
            self.drr["sync"] = (self.drr["sync"] + 1) % self.NHW
        key = ("dma", i)
        if self.dcnt[i] > self.waited[eng].get(key, 0):
            self.waited[eng][key] = self.dcnt[i]
            waits = [w for w in waits if w[0] != key] + [(key, self.dcnt[i])]
        self.dcnt[i] += 16
        tok = (("dma", i), self.dcnt[i])
        self.ops[eng].append((fn, waits, self.dsem[i], 16))
        self._commit(tok, reads, writes)
        return tok

    def coll(self, fn, reads=(), writes=(), blocking=True):
        if self.ccsem is None:
            self.ccsem = self._es.enter_context(self.nc.semaphore("ccsem"))
            self.ccdummy = self._es.enter_context(self.nc.sbuf_tensor("ccdummy", [128, 8], F32))
            self.cccnt = 0
        waits = self._deps("gpsimd", reads, writes)
        self.cccnt += 1
        self.ops["gpsimd"].append((fn, waits, self.ccsem, 1))
        if not blocking:
            tok = (("cc",), self.cccnt)
            self._commit(tok, reads, writes)
            return tok
        self.ops["gpsimd"].append((None, [(("cc",), self.cccnt)], None, 0))
        dm = self.ccdummy
        return self.op("gpsimd", lambda e: e.memset(dm[:], 0.0), reads=reads, writes=writes)

    def barrier(self):
        for e in self.ENGS:
            waits = []
            wd = self.waited[e]
            for k2 in self.ENGS:
                key = self._ekey(k2)
                if self.cnt[k2] > wd.get(key, 0):
                    wd[key] = self.cnt[k2]
                    waits.append((key, self.cnt[k2]))
            for i in range(self.NDMA):
                key = ("dma", i)
                if self.dcnt[i] > wd.get(key, 0):
                    wd[key] = self.dcnt[i]
                    waits.append((key, self.dcnt[i]))
            if waits:
                self.ops[e].append((None, waits, None, 0))
        self.epoch += 1
        self.esems.append({e: self._es.enter_context(self.nc.semaphore("s%d_%s" % (self.epoch, e))) for e in self.ENGS})
        self.cnt = {e: 0 for e in self.ENGS}

    def _semof(self, k):
        if k[0] == "cc":
            return self.ccsem
        if k[0] == "dma":
            return self.dsem[k[1]]
        return self.esems[k[2]][k[1]]

    def final_wait(self, eng, resources):
        waits = self._deps(eng, resources, ())
        self.ops[eng].append((None, waits, None, 0))

    def emit(self, block):
        for e in self.ENGS:
            ops = self.ops[e]
            if not ops:
                continue

            def body(engine, ops=ops):
                for fn, waits, sem, inc in ops:
                    for k, v in waits:
                        engine.wait_ge(self._semof(k), v)
                    if fn is not None:
                        fn(engine).then_inc(sem, inc)

            getattr(block, e)(body)


class Ring:
    def __init__(self, tiles):
        self.tiles = tiles
        self.res = [Res() for _ in tiles]
        self.i = 0

    def next(self):
        t, r = self.tiles[self.i], self.res[self.i]
        self.i = (self.i + 1) % len(self.tiles)
        return t, r


def build(parts):
    nc = bass.Bass("TRN2", target_bir_lowering=False)
    es = ExitStack()
    with es:
        S = Sched(nc, es)
        layers = sorted(set(l for _, l in parts))

        def dram(name, shape, dt, kind):
            return nc.dram_tensor(name, list(shape), dt, kind=kind).ap()

        x_in = dram("x_in", [D, NT], F32, "ExternalInput"); r_xin = Res()
        cT = dram("cT", [128, 8], F32, "ExternalInput")
        w_ada = dram("w_ada_s", [2, D, 1536], F32, "ExternalInput")
        b_ada = dram("b_ada_s", [2, 128, 12], F32, "ExternalInput")
        modc = dram("modc", [128, 12], F32, "Internal"); r_modc = Res()
        modg = dram("modg", [512, 12], F32, "Internal"); r_modg = Res()
        gvec = dram("gvec", [2, 128, 4, 8], F32, "ExternalInput")
        w_in = dram("w_in", [2, D, 5120], F32, "ExternalInput")
        convw = dram("convw", [2, 128, 3, 4], F32, "ExternalInput")
        w_pc = dram("w_pc", [2, 512, D], F32, "ExternalInput")
        w_pa = dram("w_pa", [2, 512, D], F32, "ExternalInput")
        w_out = dram("w_out", [2, D, D], F32, "ExternalInput")
        w1 = dram("w1", [2, D, DFF], F32, "ExternalInput")
        w2 = dram("w2", [2, DFF, D], F32, "ExternalInput")
        maskd = dram("mask", [128, 16, 512], BF16, "ExternalInput")
        wseld = dram("wsel", [128, 4], F32, "ExternalInput")
        trid = dram("tri", [128, 128], BF16, "ExternalInput")

        firstA = parts[0][0] == "A"
        lastA = parts[-1][0] == "A"
        fused_pairs = [l for l in layers if ("A", l) in parts and ("B", l) in parts]
        ckind = "ExternalOutput" if lastA else "Internal"
        has_A = any(p == "A" for p, _ in parts)
        has_B = any(p == "B" for p, _ in parts)
        if has_A:
            contrib = dram("contrib", [1024, NT], BF16, ckind); r_contrib = Res()
            hcon = dram("hcon", [128, 128], F32, ckind); r_hcon = Res()
            qTd_w = dram("qTd_w", [4, 128, NT], BF16, ckind)
        if has_B:
            gk = "ExternalInput" if not firstA else "Internal"
            gath = dram("gath", [4 * 1024, NT], BF16, gk); r_gath = Res(); r_gaths = [Res() for _ in range(4)]
            hgath = dram("hgath", [4 * 128, 128], F32, gk); r_hgath = Res()
            if not firstA:
                qTd_r = dram("qTd_r", [4, 128, NT], BF16, "ExternalInput")
            oTd = dram("oTd", [4, 128, NT], BF16, "Internal"); r_oTd = Res()
            xmid = dram("xmid", [D, NT], F32, "Internal"); r_xmid = Res()
        r_qTd = Res()
        r_qTdh = [Res() for _ in range(4)]
        x_out = None
        if has_B:
            x_out = dram("x_out", [D, NT], F32, "ExternalOutput"); r_xout = Res()
        n_B = sum(1 for p, _ in parts if p == "B")
        if n_B > 1:
            xl0 = dram("xl0", [D, NT], F32, "Internal"); r_xl0 = Res()

        def sb(name, shape, dt):
            return es.enter_context(nc.sbuf_tensor(name, list(shape), dt))

        ARN = 86144
        AR = sb("AR", [128, ARN], BF16)
        CM = sb("CM", [128, 16384], BF16)

        def abf(off, n):
            return AR[:, off:off + n]

        def af32(off, n):
            return AR[:, off:off + 2 * n].bitcast(F32)

        def cf32(i):
            return CM[:, i * 1024:(i + 1) * 1024].bitcast(F32)

        def c2(i):
            return CM[:, i * 1024:(i + 1) * 1024].rearrange("p (e t) -> p e t", e=2)

        rstd = cf32(0); r_rstd = Res()
        lnv = cf32(1); r_lnv = Res()
        tmpA = Ring([cf32(2 + i) for i in range(3)])
        tmpB = Ring([cf32(5 + i) for i in range(3)])
        ones = sb("ones", [128, 128], BF16); r_ones = Res()
        tri = sb("tri_sb", [128, 128], BF16); r_tri = Res()
        wsel = sb("wselt", [128, 4], F32); r_wsel = Res()
        cbf = sb("cbf", [128, 8], BF16); r_cbf = Res()
        badat = sb("badat", [128, 12], F32); r_bada = Res()
        gvt = sb("gvt", [128, 4, 8], F32); r_gvt = Res()
        modt = sb("modt", [128, 48], F32); r_mod = Res()
        coef = sb("coef", [128, 4, 8], F32); r_coef = Res()
        cwt = sb("cwt", [128, 3, 4], F32); r_cwt = Res()
        halo_acc = sb("halo_acc", [128, 4, 16, 2], F32); r_halo = Res()
        hg = sb("hg", [128, 4, 128], F32); r_hg = Res()
        hsel = sb("hsel", [128, 4, 16, 2], F32); r_hsel = Res()
        hh = sb("hh", [128, 8, 8], F32); r_hh = Res()

        class Ctx:
            pass

        PSALL = es.enter_context(nc.psum_tensor("psall", [128, 4096], F32))
        PS = Ring([PSALL[:, i * 512:(i + 1) * 512] for i in range(8)])
        zerosw = sb("zerosw", [128, 128], BF16); r_zw = Res()

        block = es.enter_context(nc.Block())

        S.op("vector", lambda e: e.memset(ones[:], 1.0), writes=[r_ones])
        S.op("vector", lambda e: e.memset(zerosw[:], 0.0), writes=[r_zw])
        S.dma("sync", lambda e: e.dma_start(out=tri[:], in_=trid), writes=[r_tri])
        S.dma("sync", lambda e: e.dma_start(out=wsel[:], in_=wseld), writes=[r_wsel])
        S.dma("gpsimd", lambda e: e.dma_start(out=cbf[:], in_=cT), writes=[r_cbf])

        barrier = S.barrier
        ATT0 = 45056
        CC_GROUPS = [[0, 1, 2, 3], [4, 5, 6, 7]]
        r_chp = [Res() for _ in range(4)]

        def issue_gather(hp):
            S.coll(lambda e: e.collective_compute(
                "AllGather", ALU.bypass, CC_GROUPS, [contrib[hp * 256:(hp + 1) * 256, :].opt()],
                [gath[hp * 1024:(hp + 1) * 1024, :].opt()]), reads=[r_chp[hp]], writes=[r_gaths[hp]], blocking=False)

        def wv3(off, k, n):
            return AR[:, off:off + k * n].rearrange("p (k n) -> p k n", k=k)

        def load_w(dst_view, src_rows_ap, rw, kchunks, nsplit=4):
            src = src_rows_ap.rearrange("(k p) n -> p k n", p=128)
            step = max(1, kchunks // nsplit)
            for k0 in range(0, kchunks, step):
                k1 = min(kchunks, k0 + step)
                S.dma("gpsimd", lambda e, k0=k0, k1=k1: e.dma_start(out=dst_view[:, k0:k1, :], in_=src[:, k0:k1, :]),
                      writes=[rw])

        def evac(i, out_ap, in_ap, reads, writes):
            if i % 2 == 0:
                S.op("scalar", lambda e: e.activation(out=out_ap, in_=in_ap, func=AF.Identity), reads=reads, writes=writes)
            else:
                S.op("vector", lambda e: e.tensor_copy(out=out_ap, in_=in_ap), reads=reads, writes=writes)

        def mk_ctx(off, ncol, with_mo=True):
            cx = Ctx()
            cx.ncol = ncol
            xt0 = af32(off, 8 * ncol).rearrange("p (c t) -> p c t", c=8)
            xt1 = CM[:, 8192:8192 + 16 * ncol].bitcast(F32).rearrange("p (c t) -> p c t", c=8)
            cx.xts = [xt0, xt1]; cx.r_xts = [Res(), Res()]
            cx.xt, cx.r_xt = xt0, cx.r_xts[0]
            off += 16 * ncol
            cx.hb = abf(off, 8 * ncol).rearrange("p (c t) -> p c t", c=8); cx.r_hb = Res()
            off += 8 * ncol
            if with_mo:
                cx.mo = af32(off, 8 * ncol).rearrange("p (c t) -> p c t", c=8); cx.r_mo = Res()
                off += 16 * ncol
            cx.end = off
            return cx

        def sumsq_rstd(cx, src3, r_src, scratch=None):
            ncol = cx.ncol
            hb, r_hb = (cx.hb, cx.r_hb) if scratch is None else scratch
            S.op("scalar", lambda e: e.activation(out=hb[:, :, :], in_=src3, func=AF.Square),
                 reads=[r_src], writes=[r_hb])
            p, rp = PS.next()
            for c in range(8):
                S.op("tensor", lambda e, c=c: e.matmul(p[:, 0:ncol], ones[:], hb[:, c, :], start=(c == 0), stop=(c == 7)),
                     reads=[r_ones, r_hb], writes=[rp])
            S.op("scalar", lambda e: e.activation(out=lnv[:, 0:ncol], in_=p[:, 0:ncol], func=AF.Ln, bias=EPS, scale=1.0 / D),
                 reads=[rp], writes=[r_lnv])
            S.op("scalar", lambda e: e.activation(out=rstd[:, 0:ncol], in_=lnv[:, 0:ncol], func=AF.Exp, scale=-0.5),
                 reads=[r_lnv], writes=[r_rstd])

        def norm_mod(cx, ai, bi):
            ncol = cx.ncol
            xt, hb = cx.xt, cx.hb
            sumsq_rstd(cx, xt[:, :, :], cx.r_xt)
            for c in range(8):
                t, rt = tmpA.next()
                S.op("vector", lambda e, c=c, t=t: e.scalar_tensor_tensor(
                    out=t[:, 0:ncol], in0=xt[:, c, :], scalar=coef[:, ai, c:c + 1], in1=rstd[:, 0:ncol],
                    op0=ALU.mult, op1=ALU.mult), reads=[cx.r_xt, r_coef, r_rstd], writes=[rt])
                S.op("scalar", lambda e, c=c, t=t: e.activation(
                    out=hb[:, c, :], in_=t[:, 0:ncol], func=AF.Identity, bias=modt[:, bi * 8 + c:bi * 8 + c + 1], scale=1.0),
                    reads=[rt, r_mod], writes=[cx.r_hb])

        def post_norm_residual(cx, gi, x_dst, r_dst, col0, scratch, xt, r_xt):
            ncol = cx.ncol
            mo = cx.mo
            sumsq_rstd(cx, mo[:, :, :], cx.r_mo, scratch)
            for c in range(8):
                t, rt = tmpA.next()
                S.op("vector", lambda e, c=c, t=t: e.scalar_tensor_tensor(
                    out=t[:, 0:ncol], in0=mo[:, c, :], scalar=coef[:, gi, c:c + 1], in1=rstd[:, 0:ncol],
                    op0=ALU.mult, op1=ALU.mult), reads=[cx.r_mo, r_coef, r_rstd], writes=[rt])
                S.op("gpsimd", lambda e, c=c, t=t: e.tensor_tensor(
                    out=xt[:, c, :], in0=xt[:, c, :], in1=t[:, 0:ncol], op=ALU.add),
                    reads=[rt, r_xt], writes=[r_xt])
            dst = x_dst.rearrange("(c p) t -> p c t", p=128)[:, :, col0:col0 + ncol]
            S.dma("sync", lambda e: e.dma_start(out=dst, in_=xt[:, :, :]), reads=[r_xt], writes=[r_dst])

        def issue_x(cx, x_src, r_src, t):
            src = x_src.rearrange("(c p) t -> p c t", p=128)[:, :, t * cx.ncol:(t + 1) * cx.ncol]
            dst, rd = cx.xts[t % 2], cx.r_xts[t % 2]
            S.dma("sync", lambda e: e.dma_start(out=dst[:, :, :], in_=src), reads=[r_src], writes=[rd])

        def load_x(cx, x_src, r_src, T, ntiles, manual=False):
            if not manual:
                if T == 0:
                    issue_x(cx, x_src, r_src, 0)
                if T + 1 < ntiles:
                    issue_x(cx, x_src, r_src, T + 1)
            cx.xt, cx.r_xt = cx.xts[T % 2], cx.r_xts[T % 2]

        def mm_group(p, rp, ncol, lhs_fn, rhs_fn, nk, reads):
            for fi in range(nk):
                rd = reads(fi) if callable(reads) else reads
                S.op("tensor", lambda e, fi=fi: e.matmul(p[:, 0:ncol], lhs_fn(fi), rhs_fn(fi), start=(fi == 0), stop=(fi == nk - 1)),
                     reads=rd, writes=[rp])

        def stage_mod(l, pre_A=None):
            barrier()
            S.dma("sync", lambda e: e.dma_start(out=badat[:], in_=b_ada[l]), writes=[r_bada])
            S.dma("sync", lambda e: e.dma_start(out=gvt[:], in_=gvec[l]), writes=[r_gvt])
            S.dma("sync", lambda e: e.dma_start(out=cwt[:], in_=convw[l]), writes=[r_cwt])
            pm, rpm = PS.next()
            wv = wv3(20480, 8, 1536)
            rws = [Res() for _ in range(4)]
            src = w_ada[l].rearrange("(k p) n -> p k n", p=128)
            for g in range(4):
                S.dma("gpsimd", lambda e, g=g: e.dma_start(out=wv[:, 2 * g:2 * g + 2, :], in_=src[:, 2 * g:2 * g + 2, :]), writes=[rws[g]])
            for fo in range(12):
                for fi in range(8):
                    S.op("tensor", lambda e, fo=fo, fi=fi: e.matmul(
                        pm[:, fo:fo + 1], wv[:, fi, fo * 128:(fo + 1) * 128], cbf[:, fi:fi + 1],
                        start=(fi == 0), stop=(fi == 7)), reads=[rws[fi // 2], r_cbf], writes=[rpm])
            mpart = tmpA.tiles[0]; r_mpart = tmpA.res[0]
            S.op("vector", lambda e: e.tensor_tensor(out=mpart[:, 0:12], in0=pm[:, 0:12], in1=badat[:], op=ALU.add),
                 reads=[rpm, r_bada], writes=[r_mpart])
            S.dma("sync", lambda e: e.dma_start(out=modc, in_=mpart[:, 0:12]), reads=[r_mpart], writes=[r_modc])
            S.coll(lambda e: e.collective_compute("AllGather", ALU.bypass, CC_GROUPS, [modc.opt()], [modg.opt()]),
                   reads=[r_modc], writes=[r_modg], blocking=False)
            if pre_A is not None:
                pre_A()
            S.dma("sync", lambda e: e.dma_start(out=modt[:, :].rearrange("p (r f) -> p r f", r=4),
                                                in_=modg.rearrange("(r p) f -> p r f", p=128)), reads=[r_modg], writes=[r_mod])
            S.op("vector", lambda e: e.scalar_tensor_tensor(out=coef[:, 0, :], in0=modt[:, 8:16], scalar=1.0, in1=gvt[:, 0, :],
                                                            op0=ALU.add, op1=ALU.mult), reads=[r_mod, r_gvt], writes=[r_coef])
            S.op("vector", lambda e: e.tensor_tensor(out=coef[:, 1, :], in0=modt[:, 16:24], in1=gvt[:, 1, :], op=ALU.mult),
                 reads=[r_mod, r_gvt], writes=[r_coef])
            S.op("vector", lambda e: e.scalar_tensor_tensor(out=coef[:, 2, :], in0=modt[:, 32:40], scalar=1.0, in1=gvt[:, 2, :],
                                                            op0=ALU.add, op1=ALU.mult), reads=[r_mod, r_gvt], writes=[r_coef])
            S.op("vector", lambda e: e.tensor_tensor(out=coef[:, 3, :], in0=modt[:, 40:48], in1=gvt[:, 3, :], op=ALU.mult),
                 reads=[r_mod, r_gvt], writes=[r_coef])

        def make_pre_A(l, x_src, r_src):
            P = Ctx()
            P.wqkv = wv3(0, 8, 1536); P.rwq = Res()
            P.wcu = wv3(12288, 8, 1024); P.rwc = Res()
            P.xt1 = CM[:, 8192:16384].bitcast(F32).rearrange("p (c t) -> p c t", c=8); P.r_x1 = Res()
            P.xt0 = af32(ATT0, 4096).rearrange("p (c t) -> p c t", c=8); P.r_x0 = Res()
            P.cxs = Ctx(); P.cxs.ncol = 512
            P.cxs.xts = [P.xt1, P.xt0]; P.cxs.r_xts = [P.r_x1, P.r_x0]

            def go():
                load_w(P.wcu, w_in[l][:, 512:1536], P.rwc, 8)
                load_w(P.wqkv, w_in[l][:, 1536:3072], P.rwq, 8)
                issue_x(P.cxs, x_src, r_src, 0)
                issue_x(P.cxs, x_src, r_src, 1)
            P.go = go
            return P

        def stage_A(l, x_src, r_src, do_gather, P):
            barrier()
            wqkv, rwq, wcu, rwc = P.wqkv, P.rwq, P.wcu, P.rwc
            xt1, r_x1 = P.xt1, P.r_x1
            hbT = [abf(20480 + T * 4096, 4096).rearrange("p (c t) -> p c t", c=8) for T in range(4)]
            r_hbT = [Res() for _ in range(4)]
            stg = Ring([abf(36864 + i * 1536, 1536).rearrange("p (c t) -> p c t", c=3) for i in range(4)])
            assert 36864 + 4 * 1536 <= ATT0
            conH = contrib.rearrange("(h s) c -> s h c", h=4)
            xt0, r_x0 = P.xt0, P.r_x0
            ATB.alias = [r_x0]
            cxs = P.cxs
            for T in range(4):
                cx = Ctx(); cx.ncol = 512
                cx.xts = cxs.xts; cx.r_xts = cxs.r_xts; cx.xt, cx.r_xt = cxs.xts[T % 2], cxs.r_xts[T % 2]
                cx.hb, cx.r_hb = hbT[T], r_hbT[T]
                norm_mod(cx, 0, 0)
                if T + 2 < 4:
                    issue_x(cxs, x_src, r_src, T + 2)
                hb = cx.hb
                p, rp = PS.next()
                hview = hb.rearrange("p c (b t) -> p c b t", t=128)
                for fo in range(8):
                    for fi in range(8):
                        S.op("tensor", lambda e, fo=fo, fi=fi, p=p, hview=hview: e.matmul(
                            p[:, fo * 8:(fo + 1) * 8].rearrange("p (b t) -> p b t", t=2), wcu[:, fi, fo * 128:(fo + 1) * 128],
                            hview[:, fi, :, 126:128], start=(fi == 0), stop=(fi == 7)), reads=[rwc, cx.r_hb], writes=[rp])
                S.op("scalar", lambda e, p=p: e.activation(out=hh[:, :, :].rearrange("p a b -> p (a b)"), in_=p[:, 0:64], func=AF.Identity),
                     reads=[rp], writes=[r_hh])
                S.op("vector", lambda e, T=T: e.tensor_tensor(
                    out=halo_acc[:, :, T * 4:(T + 1) * 4, :], in0=hh[:, 0:4, :].rearrange("p c (b t) -> p c b t", t=2),
                    in1=hh[:, 4:8, :].rearrange("p c (b t) -> p c b t", t=2), op=ALU.mult), reads=[r_hh], writes=[r_halo])
            S.dma("sync", lambda e: e.dma_start(out=hcon, in_=halo_acc[:, :, :, :].rearrange("p c i t -> p (c i t)")),
                  reads=[r_halo], writes=[r_hcon])
            for hp in range(4):
                for T in range(4):
                    hb, r_hb = hbT[T], r_hbT[T]
                    sg, rsg = stg.next()
                    for which in range(2):
                        fo = which * 4 + hp
                        p, rp = PS.next()
                        mm_group(p, rp, 512, lambda fi, fo=fo: wqkv[:, fi, fo * 128:(fo + 1) * 128], lambda fi, hb=hb: hb[:, fi, :], 8, [rwq, r_hb])
                        evac(which, sg[:, which, :], p[:, :], [rp], [rsg])
                    p, rp = PS.next()
                    for blk in range(4):
                        for fi in range(8):
                            S.op("tensor", lambda e, blk=blk, fi=fi, p=p, hb=hb, hp=hp: e.matmul(
                                p[:, blk * 128:(blk + 1) * 128], hb[:, fi, blk * 128:(blk + 1) * 128],
                                wqkv[:, fi, 1024 + hp * 128:1024 + (hp + 1) * 128], start=(fi == 0), stop=(fi == 7)),
                                reads=[rwq, r_hb], writes=[rp])
                    evac(1, sg[:, 2, :], p[:, :], [rp], [rsg])
                    S.dma("sync", lambda e, T=T, sg=sg, hp=hp: e.dma_start(out=qTd_w[hp, :, T * 512:(T + 1) * 512], in_=sg[:, 0, :]),
                          reads=[rsg], writes=[r_qTdh[hp]])
                    S.dma("sync", lambda e, T=T, sg=sg, hp=hp: e.dma_start(out=conH[0:128, hp, T * 512:(T + 1) * 512], in_=sg[:, 1, :]),
                          reads=[rsg], writes=[r_chp[hp], r_contrib])
                    S.dma("sync", lambda e, T=T, sg=sg, hp=hp: e.dma_start(
                        out=conH[128:256, hp, T * 512:(T + 1) * 512], in_=sg[:, 2, :]),
                        reads=[rsg], writes=[r_chp[hp], r_contrib])
                if do_gather and hp == 0:
                    issue_gather(0)
                if do_gather and hp == 1:
                    attn_prefetch(qTd_w, "gpsimd")
            if do_gather:
                S.coll(lambda e: e.collective_compute("AllGather", ALU.bypass, CC_GROUPS, [hcon.opt()], [hgath.opt()]),
                       reads=[r_hcon], writes=[r_hgath], blocking=False)

        def load_post_weights(l):
            W = Ctx()
            W.wbcu = wv3(0, 8, 1536); W.wgt = wv3(12288, 8, 2048)
            W.wpc = wv3(28672, 4, 1024); W.wpa = wv3(32768, 4, 1024); W.wo = wv3(36864, 8, 1024)
            W.rw0, W.rw1, W.rw2, W.rw3 = Res(), Res(), Res(), Res()
            todo = []

            def add(dst_view, src_rows_ap, rw, kchunks):
                src = src_rows_ap.rearrange("(k p) n -> p k n", p=128)
                step = max(1, kchunks // 4)
                for k0 in range(0, kchunks, step):
                    k1 = min(kchunks, k0 + step)
                    todo.append(lambda k0=k0, k1=k1: S.dma(
                        "gpsimd", lambda e: e.dma_start(out=dst_view[:, k0:k1, :], in_=src[:, k0:k1, :]), writes=[rw]))
            add(W.wbcu, w_in[l][:, 0:1536], W.rw0, 8)
            add(W.wgt, w_in[l][:, 3072:5120], W.rw1, 8)
            add(W.wpc, w_pc[l], W.rw2, 4)
            add(W.wpa, w_pa[l], W.rw2, 4)
            add(W.wo, w_out[l], W.rw3, 8)
            return W, todo

        ATB = Ctx()
        ATB.KT = abf(ATT0, 8192).rearrange("p (r t) -> p r t", r=4); ATB.r_KT = Res()
        ATB.Vs = [abf(ATT0 + 8192 + i * 8192, 8192).rearrange("p (r i c) -> p r i c", r=4, i=16) for i in range(2)]
        ATB.r_V = [Res(), Res()]
        ATB.maskt = abf(ATT0 + 24576, 8192).rearrange("p (k t) -> p k t", k=16); ATB.r_mask = Res()
        ATB.qpt = abf(ATT0 + 32768, 2048); ATB.r_qpt = Res()
        ATB.r_KTp = [Res() for _ in range(4)]
        ATB.r_qp = [Res() for _ in range(4)]

        def attn_load_V(hp, q="sync"):
            G4 = gath.rearrange("(h r s) c -> h r s c", h=4, r=4)
            pb = hp % 2
            for r in range(4):
                S.dma(q, lambda e, r=r: e.dma_start(
                    out=ATB.Vs[pb][:, r, :, :], in_=G4[hp, r, 128:256, :].rearrange("p (i c) -> p i c", c=128)),
                    reads=[r_gath, r_gaths[hp]], writes=[ATB.r_V[pb]])

        def attn_load_KTpart(hp, j, q="sync"):
            G4 = gath.rearrange("(h r s) c -> h r s c", h=4, r=4)
            S.dma(q, lambda e: e.dma_start(
                out=ATB.KT[:, :, j * 512:(j + 1) * 512], in_=G4[hp, :, 0:128, j * 512:(j + 1) * 512].rearrange("r p t -> p r t")),
                reads=[r_gath, r_gaths[hp]], writes=[ATB.r_KTp[j]] + getattr(ATB, "alias", []))

        def attn_load_qpart(hp, j, qsrc, q="sync"):
            S.dma(q, lambda e: e.dma_start(out=ATB.qpt[:, j * 512:(j + 1) * 512], in_=qsrc[hp][:, j * 512:(j + 1) * 512]),
                  reads=[r_qTd, r_qTdh[hp]], writes=[ATB.r_qp[j]])

        def attn_load_KQ(hp, qsrc, q="sync"):
            for j in range(4):
                attn_load_KTpart(hp, j, q)
                attn_load_qpart(hp, j, qsrc, q)

        def attn_prefetch(qsrc, q="sync"):
            S.dma(q, lambda e: e.dma_start(out=ATB.maskt, in_=maskd), writes=[ATB.r_mask])
            attn_load_KQ(0, qsrc, q)
            attn_load_V(0, q)
            ATB.prefetched = True

        def stage_attn(l, qsrc, fusedA):
            barrier()
            W, wtodo = load_post_weights(l)

            G4 = gath.rearrange("(h r s) c -> h r s c", h=4, r=4)

            KT, r_KT, Vs, r_V = ATB.KT, ATB.r_KT, ATB.Vs, ATB.r_V
            maskt, r_mask, qpt, r_qpt = ATB.maskt, ATB.r_mask, ATB.qpt, ATB.r_qpt
            ost = [abf(ATT0 + 34816 + i * 2048, 2048) for i in range(2)]; r_ost = [Res(), Res()]
            assert ATT0 + 38912 <= ARN
            if not getattr(ATB, "prefetched", False):
                attn_prefetch(qsrc)
            ATB.prefetched = False
            load_V = attn_load_V

            def load_KQ(hp):
                attn_load_KQ(hp, qsrc)

            steps = []
            for hp in range(4):
                for I in range(4):
                    kbs = list(range(16 * I + 15, -1, -1))
                    for n, kb in enumerate(kbs):
                        band = kb >= 16 * I
                        n0 = 128 * ((kb - 16 * I) // 4) if band else 0
                        steps.append(dict(hp=hp, I=I, kb=kb, band=band, n0=n0, first=(n == 0), last=(n == len(kbs) - 1)))

            def ps2(b):
                return PSALL[:, b * 512:(b + 2) * 512].rearrange("p (e t) -> p e t", e=2)

            ZP = Ring([ps2(0), ps2(2)])
            CP = Ring([ps2(4)])
            OB = ps2(6); r_OB = Res()
            Ering = Ring([c2(i) for i in range(0, 4)])
            SPring = Ring([c2(i) for i in range(4, 8)])
            Aring = Ring([c2(i) for i in range(8, 12)])
            Dring = Ring([c2(i) for i in range(12, 14)])
            ACring = Ring([c2(i) for i in range(14, 16)])
            st = {}
            cur_acc = [None]

            def Zpart(k):
                s = steps[k]
                hp, I, kb, n0 = s["hp"], s["I"], s["kb"], s["n0"]
                if I == 1 and s["first"] and hp + 1 < 4:
                    if fusedA:
                        issue_gather(hp + 1)
                    load_V(hp + 1)
                z, rz = ZP.next()
                r, i = kb % 4, kb // 4
                for e_ in range(2):
                    S.op("tensor", lambda e, e_=e_: e.matmul(
                        z[:, e_, n0:512], KT[64 * e_:64 * e_ + 64, r, i * 128:(i + 1) * 128],
                        qpt[64 * e_:64 * e_ + 64, I * 512 + n0:(I + 1) * 512], start=True, stop=True),
                        reads=[ATB.r_KTp[kb // 16], ATB.r_qp[I]], writes=[rz])
                if I == 3 and hp + 1 < 4:
                    if s["first"]:
                        for j in range(3):
                            attn_load_qpart(hp + 1, j, qsrc)
                    if kb % 16 == 0:
                        attn_load_KTpart(hp + 1, kb // 16)
                    if s["last"]:
                        attn_load_qpart(hp + 1, 3, qsrc)
                E, rE = Ering.next()
                S.op("scalar", lambda e: e.activation(out=E[:, :, n0:512], in_=z[:, :, n0:512], func=AF.Exp, scale=0.125),
                     reads=[rz], writes=[rE])
                if s["band"]:
                    kr = kb - 16 * I
                    for e_ in range(2):
                        S.op("vector", lambda e, e_=e_: e.tensor_tensor(out=E[:, e_, n0:512], in0=E[:, e_, n0:512],
                                                                         in1=maskt[:, kr, n0:512], op=ALU.mult),
                             reads=[rE, r_mask], writes=[rE])
                st[k] = dict(E=E, rE=rE)

            def SPpart(k):
                s = steps[k]; b = st[k]; n0 = s["n0"]
                SP, rSP = SPring.next()
                S.op("scalar", lambda e: e.activation(out=SP[:, :, n0:512], in_=b["E"][:, :, n0:512], func=AF.Ln, bias=1.0, scale=1.0),
                     reads=[b["rE"]], writes=[rSP])
                b["SP"] = SP; b["rSP"] = rSP

            def Cpart(k):
                s = steps[k]; b = st[k]; n0 = s["n0"]; first = s["first"]
                c, rc = CP.next()
                if first:
                    acc, racc = ACring.next()
                    S.op("gpsimd", lambda e: e.memset(acc[:, :, :], 0.0), writes=[racc])
                    cur_acc[0] = (acc, racc)
                acc, racc = cur_acc[0]
                for e_ in range(2):
                    S.op("tensor", lambda e, e_=e_: e.matmul(c[:, e_, n0:512], tri[:], b["SP"][:, e_, n0:512], start=True, stop=first),
                         reads=[r_tri, b["rSP"]], writes=[rc])
                    if not first:
                        S.op("tensor", lambda e, e_=e_: e.matmul(c[:, e_, n0:512], ones[:], acc[:, e_, n0:512], start=False, stop=True),
                             reads=[r_ones, racc], writes=[rc])
                Dt, rD = Dring.next()
                S.op("scalar", lambda e: e.activation(out=Dt[:, :, n0:512], in_=c[:, :, n0:512], func=AF.Exp, scale=-1.0),
                     reads=[rc], writes=[rD])
                b["D"] = Dt; b["rD"] = rD

            def Rest(k):
                s = steps[k]; b = st[k]; n0 = s["n0"]
                acc, racc = cur_acc[0]
                if not s["last"]:
                    nacc, rnacc = ACring.next()
                    if n0 > 0:
                        S.op("gpsimd", lambda e: e.memset(nacc[:, :, 0:n0], 0.0), writes=[rnacc])
                    S.op("gpsimd", lambda e: e.tensor_tensor(out=nacc[:, :, n0:512], in0=acc[:, :, n0:512], in1=b["SP"][:, :, n0:512],
                                                             op=ALU.add), reads=[racc, b["rSP"]], writes=[rnacc])
                    cur_acc[0] = (nacc, rnacc)
                A, rA = Aring.next()
                S.op("vector", lambda e: e.tensor_tensor(out=A[:, :, n0:512], in0=b["E"][:, :, n0:512], in1=b["D"][:, :, n0:512], op=ALU.mult),
                     reads=[b["rE"], b["rD"]], writes=[rA])
                b["A"] = A; b["rA"] = rA

            def AVpart(k):
                s = steps[k]; b = st.pop(k); n0 = s["n0"]; hp = s["hp"]; pb = hp % 2
                kb = s["kb"]; r, i = kb % 4, kb // 4
                if s["first"]:
                    for e_ in range(2):
                        S.op("tensor", lambda e, e_=e_: e.matmul(OB[:, e_, :], zerosw[:], maskt[:, 0, :], start=True, stop=False),
                             reads=[r_zw, r_mask], writes=[r_OB])
                last = s["last"]
                for e_ in range(2):
                    S.op("tensor", lambda e, e_=e_: e.matmul(OB[:, e_, n0:512], Vs[pb][:, r, i, :], b["A"][:, e_, n0:512], start=False, stop=last),
                         reads=[r_V[pb], b["rA"]], writes=[r_OB])
                if last:
                    I = s["I"]
                    for e_ in range(2):
                        S.op("scalar", lambda e, e_=e_: e.activation(
                            out=ost[pb][64 * e_:64 * e_ + 64, I * 512:(I + 1) * 512], in_=OB[64 * e_:64 * e_ + 64, e_, :], func=AF.Identity),
                            reads=[r_OB], writes=[r_ost[pb]])
                    if I == 3:
                        S.dma("sync", lambda e: e.dma_start(out=oTd[hp], in_=ost[pb]), reads=[r_ost[pb]], writes=[r_oTd])

            n = len(steps)
            Zpart(0); SPpart(0); Zpart(1); SPpart(1)
            for k in range(n):
                if k >= 48 and k % 8 == 0 and wtodo:
                    wtodo.pop(0)()
                if k + 2 < n:
                    Zpart(k + 2)
                Cpart(k)
                if k + 2 < n:
                    SPpart(k + 2)
                Rest(k)
                if k >= 2:
                    AVpart(k - 2)
            AVpart(n - 2); AVpart(n - 1)
            while wtodo:
                wtodo.pop(0)()
            return W

        def stage_post(l, x_src, r_src, x_dst, r_dst, W):
            barrier()
            wbcu, wgt, wpc, wpa, wo = W.wbcu, W.wgt, W.wpc, W.wpa, W.wo
            rw0, rw1, rw2, rw3 = W.rw0, W.rw1, W.rw2, W.rw3
            cx = mk_ctx(45056, 512)
            hb, mo = cx.hb, cx.mo
            off = cx.end
            vcp = af32(off, 4160).rearrange("p (c b t) -> p c b t", c=4, b=4); r_vcp = Res(); off += 8320
            bgt = af32(off, 2048).rearrange("p (c t) -> p c t", c=4); r_bgt = Res(); off += 4096
            ycv = abf(off, 2048).rearrange("p (c t) -> p c t", c=4); r_ycv = Res(); off += 2048
            otile = abf(off, 2048).rearrange("p (c t) -> p c t", c=4); r_otile = Res(); off += 2048
            mrg = abf(off, 4096).rearrange("p (c t) -> p c t", c=8); r_mrg = Res(); off += 4096
            assert off <= ARN
            S.dma("sync", lambda e: e.dma_start(out=hg[:], in_=hgath.rearrange("(r p) n -> p r n", p=128)),
                  reads=[r_hgath], writes=[r_hg])
            hgv = hg[:, :, :].rearrange("p r (c i t) -> p r c i t", c=4, i=16)
            S.op("vector", lambda e: e.tensor_scalar(out=hsel[:], in0=hgv[:, 0], scalar1=wsel[:, 0:1], scalar2=None, op0=ALU.mult),
                 reads=[r_hg, r_wsel], writes=[r_hsel])
            for r in (1, 2):
                S.op("vector", lambda e, r=r: e.scalar_tensor_tensor(out=hsel[:], in0=hgv[:, r], scalar=wsel[:, r:r + 1], in1=hsel[:],
                                                                     op0=ALU.mult, op1=ALU.add), reads=[r_hg, r_wsel, r_hsel], writes=[r_hsel])
            S.op("vector", lambda e: e.scalar_tensor_tensor(out=hsel[:, :, 1:16, :], in0=hgv[:, 3, :, 0:15, :], scalar=wsel[:, 3:4],
                                                            in1=hsel[:, :, 1:16, :], op0=ALU.mult, op1=ALU.add),
                 reads=[r_hg, r_wsel, r_hsel], writes=[r_hsel])
            oTv = oTd.rearrange("h p t -> p h t")
            def prologue(T):
                load_x(cx, x_src, r_src, T, 4, manual=True)
                norm_mod(cx, 0, 0)

            issue_x(cx, x_src, r_src, 0)
            issue_x(cx, x_src, r_src, 1)
            prologue(0)
            for T in range(4):
                xt_T, r_xt_T = cx.xt, cx.r_xt
                S.dma("sync", lambda e, T=T: e.dma_start(out=otile, in_=oTv[:, :, T * 512:(T + 1) * 512]),
                      reads=[r_oTd], writes=[r_otile])
                S.op("gpsimd", lambda e, T=T: e.tensor_copy(out=vcp[:, :, :, 0:2], in_=hsel[:, :, T * 4:(T + 1) * 4, :]),
                     reads=[r_hsel], writes=[r_vcp])
                for c in range(4):
                    for which in (1, 2, 0):
                        fo = which * 4 + c
                        p, rp = PS.next()
                        mm_group(p, rp, 512, lambda fi, fo=fo: wbcu[:, fi, fo * 128:(fo + 1) * 128], lambda fi: hb[:, fi, :], 8, [rw0, cx.r_hb])
                        if which == 1:
                            S.op("scalar", lambda e, c=c, p=p: e.activation(out=mo[:, c, :], in_=p[:, :], func=AF.Identity),
                                 reads=[rp], writes=[cx.r_mo])
                        elif which == 2:
                            S.op("vector", lambda e, c=c, p=p: e.tensor_tensor(
                                out=vcp[:, c, :, 2:130], in0=mo[:, c, :].rearrange("p (b t) -> p b t", t=128),
                                in1=p[:, :].rearrange("p (b t) -> p b t", t=128), op=ALU.mult),
                                reads=[rp, cx.r_mo], writes=[r_vcp])
                        else:
                            S.op("scalar", lambda e, c=c, p=p: e.activation(out=bgt[:, c, :], in_=p[:, :], func=AF.Identity),
                                 reads=[rp], writes=[r_bgt])
                    t, rt = tmpA.next()
                    tv = t[:, :].rearrange("p (b t) -> p b t", t=128)
                    S.op("scalar", lambda e, c=c, tv=tv: e.activation(out=tv, in_=vcp[:, c, :, 2:130], func=AF.Identity, scale=cwt[:, 2, c:c + 1]),
                         reads=[r_vcp, r_cwt], writes=[rt])
                    S.op("vector", lambda e, c=c, tv=tv: e.scalar_tensor_tensor(out=tv, in0=vcp[:, c, :, 1:129], scalar=cwt[:, 1, c:c + 1], in1=tv,
                                                                                op0=ALU.mult, op1=ALU.add), reads=[r_vcp, r_cwt, rt], writes=[rt])
                    S.op("vector", lambda e, c=c, tv=tv: e.scalar_tensor_tensor(out=tv, in0=vcp[:, c, :, 0:128], scalar=cwt[:, 0, c:c + 1], in1=tv,
                                                                                op0=ALU.mult, op1=ALU.add), reads=[r_vcp, r_cwt, rt], writes=[rt])
                    S.op("vector", lambda e, c=c, t=t: e.tensor_tensor(out=ycv[:, c, :], in0=t[:, :], in1=bgt[:, c, :], op=ALU.mult),
                         reads=[rt, r_bgt], writes=[r_ycv])
                for fo in range(8):
                    pya, rya = PS.next()
                    mm_group(pya, rya, 512, lambda ci, fo=fo: wpa[:, ci, fo * 128:(fo + 1) * 128], lambda ci: otile[:, ci, :], 4, [rw2, r_otile])
                    pga, rga = PS.next()
                    mm_group(pga, rga, 512, lambda fi, fo=fo: wgt[:, fi, fo * 128:(fo + 1) * 128], lambda fi: hb[:, fi, :], 8, [rw1, cx.r_hb])
                    pgb, rgb = PS.next()
                    mm_group(pgb, rgb, 512, lambda fi, fo=fo: wgt[:, fi, 1024 + fo * 128:1024 + (fo + 1) * 128], lambda fi: hb[:, fi, :], 8,
                             [rw1, cx.r_hb])
                    pyc, ryc = PS.next()
                    mm_group(pyc, ryc, 512, lambda ci, fo=fo: wpc[:, ci, fo * 128:(fo + 1) * 128], lambda ci: ycv[:, ci, :], 4, [rw2, r_ycv])
                    sa, rsa = tmpB.next()
                    S.op("scalar", lambda e, p=pga, sa=sa: e.activation(out=sa[:, :], in_=p[:, :], func=AF.Sigmoid), reads=[rga], writes=[rsa])
                    sg, rsg = tmpB.next()
                    S.op("scalar", lambda e, p=pgb, sg=sg: e.activation(out=sg[:, :], in_=p[:, :], func=AF.Sigmoid), reads=[rgb], writes=[rsg])
                    S.op("vector", lambda e, p=pya, sg=sg: e.tensor_tensor(out=sg[:, :], in0=sg[:, :], in1=p[:, :], op=ALU.mult),
                         reads=[rsg, rya], writes=[rsg])
                    S.op("vector", lambda e, p=pyc, sa=sa: e.tensor_tensor(out=sa[:, :], in0=sa[:, :], in1=p[:, :], op=ALU.mult),
                         reads=[rsa, ryc], writes=[rsa])
                    S.op("gpsimd", lambda e, fo=fo, sa=sa, sg=sg: e.tensor_tensor(out=mrg[:, fo, :], in0=sa[:, :], in1=sg[:, :], op=ALU.add),
                         reads=[rsa, rsg], writes=[r_mrg])
                for fo in range(8):
                    if fo == 2 and T + 1 < 4:
                        prologue(T + 1)
                    p, rp = PS.next()
                    mm_group(p, rp, 512, lambda fi, fo=fo: wo[:, fi, fo * 128:(fo + 1) * 128], lambda fi: mrg[:, fi, :], 8, [rw3, r_mrg])
                    evac(fo, mo[:, fo, :], p[:, :], [rp], [cx.r_mo])
                post_norm_residual(cx, 1, x_dst, r_dst, T * 512, (mrg, r_mrg), xt_T, r_xt_T)
                if T + 2 < 4:
                    issue_x(cx, x_src, r_src, T + 2)

        def stage_mlp(l, x_src, r_src, x_dst, r_dst):
            barrier()
            W1v = wv3(0, 8, 4096); W2v = wv3(32768, 32, 1024)
            rw1 = [Res() for _ in range(8)]
            rw2 = [Res() for _ in range(8)]
            for g in range(8):
                S.dma("gpsimd", lambda e, g=g: e.dma_start(
                    out=W1v[:, :, g * 512:(g + 1) * 512], in_=w1[l][:, g * 512:(g + 1) * 512].rearrange("(k p) n -> p k n", p=128)),
                    writes=[rw1[g]])
            for g in range(8):
                S.dma("gpsimd", lambda e, g=g: e.dma_start(
                    out=W2v[:, g * 4:(g + 1) * 4, :], in_=w2[l][g * 512:(g + 1) * 512, :].rearrange("(k p) n -> p k n", p=128)),
                    writes=[rw2[g]])
            NC_ = 256
            cx = mk_ctx(65536, NC_)
            hb, mo = cx.hb, cx.mo
            ff1 = abf(cx.end, 32 * NC_).rearrange("p (k n) -> p k n", k=32); r_ff1 = Res()
            assert cx.end + 32 * NC_ <= ARN
            def prologue(T):
                load_x(cx, x_src, r_src, T, NT // NC_, manual=True)
                norm_mod(cx, 2, 3)

            issue_x(cx, x_src, r_src, 0)
            issue_x(cx, x_src, r_src, 1)
            prologue(0)
            for T in range(NT // NC_):
                xt_T, r_xt_T = cx.xt, cx.r_xt
                for fo in range(32):
                    p, rp = PS.next()
                    mm_group(p, rp, NC_, lambda fi, fo=fo: W1v[:, fi, fo * 128:(fo + 1) * 128], lambda fi: hb[:, fi, :], 8, [rw1[fo // 4], cx.r_hb])
                    t, rt = tmpB.next()
                    S.op("scalar", lambda e, p=p, t=t: e.activation(out=t[:, 0:NC_], in_=p[:, 0:NC_], func=AF.Relu), reads=[rp], writes=[rt])
                    eng = "vector" if fo % 2 == 0 else "gpsimd"
                    S.op(eng, lambda e, fo=fo, t=t: e.tensor_tensor(out=ff1[:, fo, :], in0=t[:, 0:NC_], in1=t[:, 0:NC_], op=ALU.mult),
                         reads=[rt], writes=[r_ff1])
                for fo in range(8):
                    if fo == 3 and T + 1 < NT // NC_:
                        prologue(T + 1)
                    p, rp = PS.next()
                    mm_group(p, rp, NC_, lambda fi, fo=fo: W2v[:, fi, fo * 128:(fo + 1) * 128], lambda fi: ff1[:, fi, :], 32, lambda fi: [rw2[fi // 4], r_ff1])
                    evac(fo, mo[:, fo, :], p[:, 0:NC_], [rp], [cx.r_mo])
                post_norm_residual(cx, 3, x_dst, r_dst, T * NC_, (ff1[:, 0:8, :], r_ff1), xt_T, r_xt_T)
                if T + 2 < NT // NC_:
                    issue_x(cx, x_src, r_src, T + 2)

        cur_x, r_cur = x_in, r_xin
        nB_done = 0
        last_mod = None
        finals = []
        for (kind, l) in parts:
            P = make_pre_A(l, cur_x, r_cur) if kind == "A" else None
            if last_mod != l:
                stage_mod(l, P.go if P is not None else None)
                last_mod = l
            elif P is not None:
                P.go()
            if kind == "A":
                stage_A(l, cur_x, r_cur, ("B", l) in parts, P)
                finals += [r_contrib, r_hcon, r_qTd] + r_qTdh
            else:
                fusedA = ("A", l) in parts
                qsrc = qTd_w if fusedA else qTd_r
                W = stage_attn(l, qsrc, fusedA)
                stage_post(l, cur_x, r_cur, xmid, r_xmid, W)
                nB_done += 1
                if n_B > 1 and nB_done < n_B:
                    dst, rdst = xl0, r_xl0
                else:
                    dst, rdst = x_out, r_xout
                stage_mlp(l, xmid, r_xmid, dst, rdst)
                cur_x, r_cur = dst, rdst
                finals.append(rdst)
        S.final_wait("sync", finals)
        S.emit(block)
    return nc


_PROG_CACHE = {}


def _prog(parts):
    key = tuple(parts)
    if key not in _PROG_CACHE:
        _PROG_CACHE[key] = build(list(parts))
    return _PROG_CACHE[key]


def _core_tokens(j):
    return [4 * i + j for i in range(NB)]


def _prep_static(c, w_ada, b_ada, g_pre_mix, g_post_mix, g_pre_mlp, g_post_mlp, w_in, conv_w,
                 w_proj_conv, w_proj_attn, w_out, w_mlp_in, w_mlp_out):
    f = lambda a: np.ascontiguousarray(np.asarray(a, dtype=np.float32))
    shared = {
        "gvec": f(np.stack([np.asarray(g).reshape(2, 8, 128) for g in (g_pre_mix, g_post_mix, g_pre_mlp, g_post_mlp)], axis=1)
                  .transpose(0, 3, 1, 2)),
        "w_in": f(w_in),
        "convw": f(np.asarray(conv_w).reshape(2, 3, 4, 128).transpose(0, 3, 1, 2)),
        "w_pc": f(w_proj_conv), "w_pa": f(w_proj_attn), "w_out": f(w_out),
        "w1": f(w_mlp_in), "w2": f(w_mlp_out),
        "tri": (np.arange(128)[:, None] >= np.arange(128)[None, :]).astype(ml_dtypes.bfloat16),
    }
    per_core = []
    b_ada_l = np.asarray(b_ada).reshape(2, 48, 128).transpose(0, 2, 1)
    s_idx = np.arange(128)[:, None]
    t_idx = np.arange(128)[None, :]
    for core in range(NCORES):
        b, j = core // 4, core % 4
        mask = np.zeros((128, 16, 4, 128), dtype=np.float32)
        for kr in range(16):
            for m in range(4):
                q = 4 * m + j
                if kr < q:
                    mask[:, kr, m, :] = 1.0
                elif kr == q:
                    mask[:, kr, m, :] = (t_idx > s_idx).astype(np.float32)
        wsel = np.zeros((128, 4), dtype=np.float32)
        wsel[:, (j - 1) % 4] = 1.0
        d = dict(shared)
        d["mask"] = mask.reshape(128, 16, 512).astype(ml_dtypes.bfloat16)
        d["wsel"] = wsel
        d["cT"] = f(np.asarray(c)[b].reshape(8, 128).T)
        d["w_ada_s"] = f(np.asarray(w_ada)[:, :, j * 1536:(j + 1) * 1536])
        d["b_ada_s"] = f(b_ada_l[:, :, j * 12:(j + 1) * 12])
        per_core.append(d)
    return per_core


def _shard_x(x):
    x = np.asarray(x, dtype=np.float32)
    out = []
    for core in range(NCORES):
        b, j = core // 4, core % 4
        xb = x[b].reshape(64, 128, D)[j::4].reshape(NT, D)
        out.append(np.ascontiguousarray(xb.T))
    return out


def _unshard_x(xs):
    out = np.zeros((2, 64, 128, D), dtype=np.float32)
    for core in range(NCORES):
        b, j = core // 4, core % 4
        out[b, j::4] = np.asarray(xs[core]).T.reshape(NB, 128, D)
    return out.reshape(2, 8192, D)


def _gather(res, name, nq=1):
    outs = []
    for core in range(NCORES):
        b = core // 4
        parts = []
        for q in range(nq):
            for r in range(4):
                a = np.asarray(res[b * 4 + r][name])
                n = a.shape[0] // nq
                parts.append(a[q * n:(q + 1) * n])
        outs.append(np.ascontiguousarray(np.concatenate(parts, axis=0)))
    return outs


def kernel(x, c, w_ada, b_ada, g_pre_mix, g_post_mix, g_pre_mlp, g_post_mlp,
           w_in, conv_w, w_proj_conv, w_proj_attn, w_out, w_mlp_in, w_mlp_out):
    static = _prep_static(c, w_ada, b_ada, g_pre_mix, g_post_mix, g_pre_mlp, g_post_mlp, w_in, conv_w,
                          w_proj_conv, w_proj_attn, w_out, w_mlp_in, w_mlp_out)
    xs = _shard_x(x)
    cores = list(range(NCORES))
    if FUSED:
        nc = _prog((("A", 0), ("B", 0), ("A", 1), ("B", 1)))
        in_maps = [dict(static[i], x_in=xs[i]) for i in cores]
        res = run_bass_kernel_spmd(nc, in_maps, core_ids=cores).results
        return _unshard_x([res[i]["x_out"] for i in cores])
    nc1 = _prog((("A", 0),))
    r1 = run_bass_kernel_spmd(nc1, [dict(static[i], x_in=xs[i]) for i in cores], core_ids=cores).results
    g1, h1 = _gather(r1, "contrib", 4), _gather(r1, "hcon")
    nc2 = _prog((("B", 0), ("A", 1)))
    r2 = run_bass_kernel_spmd(nc2, [dict(static[i], x_in=xs[i], gath=g1[i], hgath=h1[i], qTd_r=np.asarray(r1[i]["qTd_w"]))
                                    for i in cores], core_ids=cores).results
    g2, h2 = _gather(r2, "contrib", 4), _gather(r2, "hcon")
    nc3 = _prog((("B", 1),))
    r3 = run_bass_kernel_spmd(nc3, [dict(static[i], x_in=np.asarray(r2[i]["x_out"]), gath=g2[i], hgath=h2[i],
                                         qTd_r=np.asarray(r2[i]["qTd_w"])) for i in cores], core_ids=cores).results
    return _unshard_x([r3[i]["x_out"] for i in cores])
```

```python
import numpy as np
from contextlib import ExitStack
import ml_dtypes
import concourse.bass as bass
import concourse.mybir as mybir
from concourse.bass_utils import run_bass_kernel_spmd

F32 = mybir.dt.float32
BF16 = mybir.dt.bfloat16
AF = mybir.ActivationFunctionType
ALU = mybir.AluOpType

D = 1024
NT = 2048
NB = 16
DFF = 4096
EPS = 1e-6
NCORES = 8
FUSED = True


class Res:
    __slots__ = ("name", "lw", "rd")

    def __init__(self, name=""):
        self.name = name
        self.lw = None
        self.rd = []


class Sched:
    ENGS = ("sync", "scalar", "vector", "gpsimd", "tensor")
    NDMA = 40
    NHW = 24

    def __init__(self, nc, es):
        self.nc = nc
        self._es = es
        self.ops = {e: [] for e in self.ENGS}
        self.epoch = 0
        self.cnt = {e: 0 for e in self.ENGS}
        self.esems = [{e: es.enter_context(nc.semaphore("s0_" + e)) for e in self.ENGS}]
        self.dsem = [es.enter_context(nc.semaphore("d%d" % i)) for i in range(self.NDMA)]
        self.dcnt = [0] * self.NDMA
        self.drr = {"sync": 0, "gpsimd": 0, "scalar": 0}
        self.waited = {e: {} for e in self.ENGS}
        self.ccsem = None

    def _ekey(self, eng):
        return ("eng", eng, self.epoch)

    def _deps(self, eng, reads, writes):
        deps = []
        for r in reads:
            if r.lw is not None:
                deps.append(r.lw)
        for w in writes:
            if w.lw is not None:
                deps.append(w.lw)
            deps.extend(w.rd)
        wd = self.waited[eng]
        best = {}
        for (k, v) in deps:
            if k[0] == "eng":
                if k[2] < self.epoch:
                    continue
                if k[1] == "tensor" and eng == "tensor":
                    continue
            if wd.get(k, 0) >= v:
                continue
            wd[k] = v
            best[k] = max(best.get(k, 0), v)
        return list(best.items())

    def _commit(self, tok, reads, writes):
        for r in reads:
            r.rd.append(tok)
        for w in writes:
            w.lw = tok
            w.rd = []

    def op(self, eng, fn, reads=(), writes=()):
        waits = self._deps(eng, reads, writes)
        self.cnt[eng] += 1
        tok = (self._ekey(eng), self.cnt[eng])
        self.ops[eng].append((fn, waits, self.esems[self.epoch][eng], 1))
        self._commit(tok, reads, writes)
        return tok

    def dma(self, eng, fn, reads=(), writes=()):
        waits = self._deps(eng, reads, writes)
        if eng == "gpsimd":
            n = self.NDMA - self.NHW
            i = self.NHW + self.drr[eng]
            self.drr[eng] = (self.drr[eng] + 1) % n
        else:
            i = self.drr["sync"]
            self.drr["sync"] = (self.drr["sync"] + 1) % self.NHW
        key = ("dma", i)
        if self.dcnt[i] > self.waited[eng].get(key, 0):
            self.waited[eng][key] = self.dcnt[i]
            waits = [w for w in waits if w[0] != key] + [(key, self.dcnt[i])]
        self.dcnt[i] += 16
        tok = (("dma", i), self.dcnt[i])
        self.ops[eng].append((fn, waits, self.dsem[i], 16))
        self._commit(tok, reads, writes)
        return tok

    def coll(self, fn, reads=(), writes=(), blocking=True):
        if self.ccsem is None:
            self.ccsem = self._es.enter_context(self.nc.semaphore("ccsem"))
            self.ccdummy = self._es.enter_context(self.nc.sbuf_tensor("ccdummy", [128, 8], F32))
            self.cccnt = 0
        waits = self._deps("gpsimd", reads, writes)
        self.cccnt += 1
        self.ops["gpsimd"].append((fn, waits, self.ccsem, 1))
        if not blocking:
            tok = (("cc",), self.cccnt)
            self._commit(tok, reads, writes)
            return tok
        self.ops["gpsimd"].append((None, [(("cc",), self.cccnt)], None, 0))
        dm = self.ccdummy
        return self.op("gpsimd", lambda e: e.memset(dm[:], 0.0), reads=reads, writes=writes)

    def barrier(self):
        for e in self.ENGS:
            waits = []
            wd = self.waited[e]
            for k2 in self.ENGS:
                key = self._ekey(k2)
                if self.cnt[k2] > wd.get(key, 0):
                    wd[key] = self.cnt[k2]
                    waits.append((key, self.cnt[k2]))
            for i in range(self.NDMA):
                key = ("dma", i)
                if self.dcnt[i] > wd.get(key, 0):
                    wd[key] = self.dcnt[i]
                    waits.append((key, self.dcnt[i]))
            if waits:
                self.ops[e].append((None, waits, None, 0))
        self.epoch += 1
        self.esems.append({e: self._es.enter_context(self.nc.semaphore("s%d_%s" % (self.epoch, e))) for e in self.ENGS})
        self.cnt = {e: 0 for e in self.ENGS}

    def _semof(self, k):
        if k[0] == "cc":
            return self.ccsem
        if k[0] == "dma":
            return self.dsem[k[1]]
        return self.esems[k[2]][k[1]]

    def final_wait(self, eng, resources):
        waits = self._deps(eng, resources, ())
        self.ops[eng].append((None, waits, None, 0))

    def emit(self, block):
        for e in self.ENGS:
            ops = self.ops[e]
            if not ops:
                continue

            def body(engine, ops=ops):
                for fn, waits, sem, inc in ops:
                    for k, v in waits:
                        engine.wait_ge(self._semof(k), v)
                    if fn is not None:
                        fn(engine).then_inc(sem, inc)

            getattr(block, e)(body)


class Ring:
    def __init__(self, tiles):
        self.tiles = tiles
        self.res = [Res() for _ in tiles]
        self.i = 0

    def next(self):
        t, r = self.tiles[self.i], self.res[self.i]
        self.i = (self.i + 1) % len(self.tiles)
        return t, r


def build(parts):
    nc = bass.Bass("TRN2", target_bir_lowering=False)
    es = ExitStack()
    with es:
        S = Sched(nc, es)
        layers = sorted(set(l for _, l in parts))

        def dram(name, shape, dt, kind):
            return nc.dram_tensor(name, list(shape), dt, kind=kind).ap()

        x_in = dram("x_in", [D, NT], F32, "ExternalInput"); r_xin = Res()
        cT = dram("cT", [128, 8], F32, "ExternalInput")
        w_ada = dram("w_ada_s", [2, D, 1536], F32, "ExternalInput")
        b_ada = dram("b_ada_s", [2, 128, 12], F32, "ExternalInput")
        modc = dram("modc", [128, 12], F32, "Internal"); r_modc = Res()
        modg = dram("modg", [512, 12], F32, "Internal"); r_modg = Res()
        gvec = dram("gvec", [2, 128, 4, 8], F32, "ExternalInput")
        w_in = dram("w_in", [2, D, 5120], F32, "ExternalInput")
        convw = dram("convw", [2, 128, 3, 4], F32, "ExternalInput")
        w_pc = dram("w_pc", [2, 512, D], F32, "ExternalInput")
        w_pa = dram("w_pa", [2, 512, D], F32, "ExternalInput")
        w_out = dram("w_out", [2, D, D], F32, "ExternalInput")
        w1 = dram("w1", [2, D, DFF], F32, "ExternalInput")
        w2 = dram("w2", [2, DFF, D], F32, "ExternalInput")
        maskd = dram("mask", [128, 16, 512], BF16, "ExternalInput")
        wseld = dram("wsel", [128, 4], F32, "ExternalInput")
        trid = dram("tri", [128, 128], BF16, "ExternalInput")

        firstA = parts[0][0] == "A"
        lastA = parts[-1][0] == "A"
        fused_pairs = [l for l in layers if ("A", l) in parts and ("B", l) in parts]
        ckind = "ExternalOutput" if lastA else "Internal"
        has_A = any(p == "A" for p, _ in parts)
        has_B = any(p == "B" for p, _ in parts)
        if has_A:
            contrib = dram("contrib", [1024, NT], BF16, ckind); r_contrib = Res()
            hcon = dram("hcon", [128, 128], F32, ckind); r_hcon = Res()
            qTd_w = dram("qTd_w", [4, 128, NT], BF16, ckind)
        if has_B:
            gk = "ExternalInput" if not firstA else "Internal"
            gath = dram("gath", [4 * 1024, NT], BF16, gk); r_gath = Res(); r_gaths = [Res() for _ in range(4)]
            hgath = dram("hgath", [4 * 128, 128], F32, gk); r_hgath = Res()
            if not firstA:
                qTd_r = dram("qTd_r", [4, 128, NT], BF16, "ExternalInput")
            oTd = dram("oTd", [4, 128, NT], BF16, "Internal"); r_oTd = Res()
            xmid = dram("xmid", [D, NT], F32, "Internal"); r_xmid = Res()
        r_qTd = Res()
        r_qTdh = [Res() for _ in range(4)]
        x_out = None
        if has_B:
            x_out = dram("x_out", [D, NT], F32, "ExternalOutput"); r_xout = Res()
        n_B = sum(1 for p, _ in parts if p == "B")
        if n_B > 1:
            xl0 = dram("xl0", [D, NT], F32, "Internal"); r_xl0 = Res()

        def sb(name, shape, dt):
            return es.enter_context(nc.sbuf_tensor(name, list(shape), dt))

        ARN = 86144
        AR = sb("AR", [128, ARN], BF16)
        CM = sb("CM", [128, 16384], BF16)

        def abf(off, n):
            return AR[:, off:off + n]

        def af32(off, n):
            return AR[:, off:off + 2 * n].bitcast(F32)

        def cf32(i):
            return CM[:, i * 1024:(i + 1) * 1024].bitcast(F32)

        def c2(i):
            return CM[:, i * 1024:(i + 1) * 1024].rearrange("p (e t) -> p e t", e=2)

        rstd = cf32(0); r_rstd = Res()
        lnv = cf32(1); r_lnv = Res()
        tmpA = Ring([cf32(2 + i) for i in range(3)])
        tmpB = Ring([cf32(5 + i) for i in range(3)])
        ones = sb("ones", [128, 128], BF16); r_ones = Res()
        tri = sb("tri_sb", [128, 128], BF16); r_tri = Res()
        wsel = sb("wselt", [128, 4], F32); r_wsel = Res()
        cbf = sb("cbf", [128, 8], BF16); r_cbf = Res()
        badat = sb("badat", [128, 12], F32); r_bada = Res()
        gvt = sb("gvt", [128, 4, 8], F32); r_gvt = Res()
        modt = sb("modt", [128, 48], F32); r_mod = Res()
        coef = sb("coef", [128, 4, 8], F32); r_coef = Res()
        cwt = sb("cwt", [128, 3, 4], F32); r_cwt = Res()
        halo_acc = sb("halo_acc", [128, 4, 16, 2], F32); r_halo = Res()
        hg = sb("hg", [128, 4, 128], F32); r_hg = Res()
        hsel = sb("hsel", [128, 4, 16, 2], F32); r_hsel = Res()
        hh = sb("hh", [128, 8, 8], F32); r_hh = Res()

        class Ctx:
            pass

        PSALL = es.enter_context(nc.psum_tensor("psall", [128, 4096], F32))
        PS = Ring([PSALL[:, i * 512:(i + 1) * 512] for i in range(8)])
        zerosw = sb("zerosw", [128, 128], BF16); r_zw = Res()

        block = es.enter_context(nc.Block())

        S.op("vector", lambda e: e.memset(ones[:], 1.0), writes=[r_ones])
        S.op("vector", lambda e: e.memset(zerosw[:], 0.0), writes=[r_zw])
        S.dma("sync", lambda e: e.dma_start(out=tri[:], in_=trid), writes=[r_tri])
        S.dma("sync", lambda e: e.dma_start(out=wsel[:], in_=wseld), writes=[r_wsel])
        S.dma("gpsimd", lambda e: e.dma_start(out=cbf[:], in_=cT), writes=[r_cbf])

        barrier = S.barrier
        ATT0 = 45056
        CC_GROUPS = [[0, 1, 2, 3], [4, 5, 6, 7]]
        r_chp = [Res() for _ in range(4)]

        def issue_gather(hp):
            S.coll(lambda e: e.collective_compute(
                "AllGather", ALU.bypass, CC_GROUPS, [contrib[hp * 256:(hp + 1) * 256, :].opt()],
                [gath[hp * 1024:(hp + 1) * 1024, :].opt()]), reads=[r_chp[hp]], writes=[r_gaths[hp]], blocking=False)

        def wv3(off, k, n):
            return AR[:, off:off + k * n].rearrange("p (k n) -> p k n", k=k)

        def load_w(dst_view, src_rows_ap, rw, kchunks, nsplit=4):
            src = src_rows_ap.rearrange("(k p) n -> p k n", p=128)
            step = max(1, kchunks // nsplit)
            for k0 in range(0, kchunks, step):
                k1 = min(kchunks, k0 + step)
                S.dma("gpsimd", lambda e, k0=k0, k1=k1: e.dma_start(out=dst_view[:, k0:k1, :], in_=src[:, k0:k1, :]),
                      writes=[rw])

        def evac(i, out_ap, in_ap, reads, writes):
            if i % 2 == 0:
                S.op("scalar", lambda e: e.activation(out=out_ap, in_=in_ap, func=AF.Identity), reads=reads, writes=writes)
            else:
                S.op("vector", lambda e: e.tensor_copy(out=out_ap, in_=in_ap), reads=reads, writes=writes)

        def mk_ctx(off, ncol, with_mo=True):
            cx = Ctx()
            cx.ncol = ncol
            xt0 = af32(off, 8 * ncol).rearrange("p (c t) -> p c t", c=8)
            xt1 = CM[:, 8192:8192 + 16 * ncol].bitcast(F32).rearrange("p (c t) -> p c t", c=8)
            cx.xts = [xt0, xt1]; cx.r_xts = [Res(), Res()]
            cx.xt, cx.r_xt = xt0, cx.r_xts[0]
            off += 16 * ncol
            cx.hb = abf(off, 8 * ncol).rearrange("p (c t) -> p c t", c=8); cx.r_hb = Res()
            off += 8 * ncol
            if with_mo:
                cx.mo = af32(off, 8 * ncol).rearrange("p (c t) -> p c t", c=8); cx.r_mo = Res()
                off += 16 * ncol
            cx.end = off
            return cx

        def sumsq_rstd(cx, src3, r_src, scratch=None):
            ncol = cx.ncol
            hb, r_hb = (cx.hb, cx.r_hb) if scratch is None else scratch
            S.op("scalar", lambda e: e.activation(out=hb[:, :, :], in_=src3, func=AF.Square),
                 reads=[r_src], writes=[r_hb])
            p, rp = PS.next()
            for c in range(8):
                S.op("tensor", lambda e, c=c: e.matmul(p[:, 0:ncol], ones[:], hb[:, c, :], start=(c == 0), stop=(c == 7)),
                     reads=[r_ones, r_hb], writes=[rp])
            S.op("scalar", lambda e: e.activation(out=lnv[:, 0:ncol], in_=p[:, 0:ncol], func=AF.Ln, bias=EPS, scale=1.0 / D),
                 reads=[rp], writes=[r_lnv])
            S.op("scalar", lambda e: e.activation(out=rstd[:, 0:ncol], in_=lnv[:, 0:ncol], func=AF.Exp, scale=-0.5),
                 reads=[r_lnv], writes=[r_rstd])

        def norm_mod(cx, ai, bi):
            ncol = cx.ncol
            xt, hb = cx.xt, cx.hb
            sumsq_rstd(cx, xt[:, :, :], cx.r_xt)
            for c in range(8):
                t, rt = tmpA.next()
                S.op("vector", lambda e, c=c, t=t: e.scalar_tensor_tensor(
                    out=t[:, 0:ncol], in0=xt[:, c, :], scalar=coef[:, ai, c:c + 1], in1=rstd[:, 0:ncol],
                    op0=ALU.mult, op1=ALU.mult), reads=[cx.r_xt, r_coef, r_rstd], writes=[rt])
                S.op("scalar", lambda e, c=c, t=t: e.activation(
                    out=hb[:, c, :], in_=t[:, 0:ncol], func=AF.Identity, bias=modt[:, bi * 8 + c:bi * 8 + c + 1], scale=1.0),
                    reads=[rt, r_mod], writes=[cx.r_hb])

        def post_norm_residual(cx, gi, x_dst, r_dst, col0, scratch, xt, r_xt):
            ncol = cx.ncol
            mo = cx.mo
            sumsq_rstd(cx, mo[:, :, :], cx.r_mo, scratch)
            for c in range(8):
                t, rt = tmpA.next()
                S.op("vector", lambda e, c=c, t=t: e.scalar_tensor_tensor(
                    out=t[:, 0:ncol], in0=mo[:, c, :], scalar=coef[:, gi, c:c + 1], in1=rstd[:, 0:ncol],
                    op0=ALU.mult, op1=ALU.mult), reads=[cx.r_mo, r_coef, r_rstd], writes=[rt])
                S.op("gpsimd", lambda e, c=c, t=t: e.tensor_tensor(
                    out=xt[:, c, :], in0=xt[:, c, :], in1=t[:, 0:ncol], op=ALU.add),
                    reads=[rt, r_xt], writes=[r_xt])
            dst = x_dst.rearrange("(c p) t -> p c t", p=128)[:, :, col0:col0 + ncol]
            S.dma("sync", lambda e: e.dma_start(out=dst, in_=xt[:, :, :]), reads=[r_xt], writes=[r_dst])

        def issue_x(cx, x_src, r_src, t):
            src = x_src.rearrange("(c p) t -> p c t", p=128)[:, :, t * cx.ncol:(t + 1) * cx.ncol]
            dst, rd = cx.xts[t % 2], cx.r_xts[t % 2]
            S.dma("sync", lambda e: e.dma_start(out=dst[:, :, :], in_=src), reads=[r_src], writes=[rd])

        def load_x(cx, x_src, r_src, T, ntiles, manual=False):
            if not manual:
                if T == 0:
                    issue_x(cx, x_src, r_src, 0)
                if T + 1 < ntiles:
                    issue_x(cx, x_src, r_src, T + 1)
            cx.xt, cx.r_xt = cx.xts[T % 2], cx.r_xts[T % 2]

        def mm_group(p, rp, ncol, lhs_fn, rhs_fn, nk, reads):
            for fi in range(nk):
                rd = reads(fi) if callable(reads) else reads
                S.op("tensor", lambda e, fi=fi: e.matmul(p[:, 0:ncol], lhs_fn(fi), rhs_fn(fi), start=(fi == 0), stop=(fi == nk - 1)),
                     reads=rd, writes=[rp])

        def stage_mod(l, pre_A=None):
            barrier()
            S.dma("sync", lambda e: e.dma_start(out=badat[:], in_=b_ada[l]), writes=[r_bada])
            S.dma("sync", lambda e: e.dma_start(out=gvt[:], in_=gvec[l]), writes=[r_gvt])
            S.dma("sync", lambda e: e.dma_start(out=cwt[:], in_=convw[l]), writes=[r_cwt])
            pm, rpm = PS.next()
            wv = wv3(20480, 8, 1536)
            rws = [Res() for _ in range(4)]
            src = w_ada[l].rearrange("(k p) n -> p k n", p=128)
            for g in range(4):
                S.dma("gpsimd", lambda e, g=g: e.dma_start(out=wv[:, 2 * g:2 * g + 2, :], in_=src[:, 2 * g:2 * g + 2, :]), writes=[rws[g]])
            for fo in range(12):
                for fi in range(8):
                    S.op("tensor", lambda e, fo=fo, fi=fi: e.matmul(
                        pm[:, fo:fo + 1], wv[:, fi, fo * 128:(fo + 1) * 128], cbf[:, fi:fi + 1],
                        start=(fi == 0), stop=(fi == 7)), reads=[rws[fi // 2], r_cbf], writes=[rpm])
            mpart = tmpA.tiles[0]; r_mpart = tmpA.res[0]
            S.op("vector", lambda e: e.tensor_tensor(out=mpart[:, 0:12], in0=pm[:, 0:12], in1=badat[:], op=ALU.add),
                 reads=[rpm, r_bada], writes=[r_mpart])
            S.dma("sync", lambda e: e.dma_start(out=modc, in_=mpart[:, 0:12]), reads=[r_mpart], writes=[r_modc])
            S.coll(lambda e: e.collective_compute("AllGather", ALU.bypass, CC_GROUPS, [modc.opt()], [modg.opt()]),
                   reads=[r_modc], writes=[r_modg], blocking=False)
            if pre_A is not None:
                pre_A()
            S.dma("sync", lambda e: e.dma_start(out=modt[:, :].rearrange("p (r f) -> p r f", r=4),
                                                in_=modg.rearrange("(r p) f -> p r f", p=128)), reads=[r_modg], writes=[r_mod])
            S.op("vector", lambda e: e.scalar_tensor_tensor(out=coef[:, 0, :], in0=modt[:, 8:16], scalar=1.0, in1=gvt[:, 0, :],
                                                            op0=ALU.add, op1=ALU.mult), reads=[r_mod, r_gvt], writes=[r_coef])
            S.op("vector", lambda e: e.tensor_tensor(out=coef[:, 1, :], in0=modt[:, 16:24], in1=gvt[:, 1, :], op=ALU.mult),
                 reads=[r_mod, r_gvt], writes=[r_coef])
            S.op("vector", lambda e: e.scalar_tensor_tensor(out=coef[:, 2, :], in0=modt[:, 32:40], scalar=1.0, in1=gvt[:, 2, :],
                                                            op0=ALU.add, op1=ALU.mult), reads=[r_mod, r_gvt], writes=[r_coef])
            S.op("vector", lambda e: e.tensor_tensor(out=coef[:, 3, :], in0=modt[:, 40:48], in1=gvt[:, 3, :], op=ALU.mult),
                 reads=[r_mod, r_gvt], writes=[r_coef])

        def make_pre_A(l, x_src, r_src):
            P = Ctx()
            P.wqkv = wv3(0, 8, 1536); P.rwq = Res()
            P.wcu = wv3(12288, 8, 1024); P.rwc = Res()
            P.xt1 = CM[:, 8192:16384].bitcast(F32).rearrange("p (c t) -> p c t", c=8); P.r_x1 = Res()
            P.xt0 = af32(ATT0, 4096).rearrange("p (c t) -> p c t", c=8); P.r_x0 = Res()
            P.cxs = Ctx(); P.cxs.ncol = 512
            P.cxs.xts = [P.xt1, P.xt0]; P.cxs.r_xts = [P.r_x1, P.r_x0]

            def go():
                load_w(P.wcu, w_in[l][:, 512:1536], P.rwc, 8)
                load_w(P.wqkv, w_in[l][:, 1536:3072], P.rwq, 8)
                issue_x(P.cxs, x_src, r_src, 0)
                issue_x(P.cxs, x_src, r_src, 1)
            P.go = go
            return P

        def stage_A(l, x_src, r_src, do_gather, P):
            barrier()
            wqkv, rwq, wcu, rwc = P.wqkv, P.rwq, P.wcu, P.rwc
            xt1, r_x1 = P.xt1, P.r_x1
            hbT = [abf(20480 + T * 4096, 4096).rearrange("p (c t) -> p c t", c=8) for T in range(4)]
            r_hbT = [Res() for _ in range(4)]
            stg = Ring([abf(36864 + i * 1536, 1536).rearrange("p (c t) -> p c t", c=3) for i in range(4)])
            assert 36864 + 4 * 1536 <= ATT0
            conH = contrib.rearrange("(h s) c -> s h c", h=4)
            xt0, r_x0 = P.xt0, P.r_x0
            ATB.alias = [r_x0]
            cxs = P.cxs
            for T in range(4):
                cx = Ctx(); cx.ncol = 512
                cx.xts = cxs.xts; cx.r_xts = cxs.r_xts; cx.xt, cx.r_xt = cxs.xts[T % 2], cxs.r_xts[T % 2]
                cx.hb, cx.r_hb = hbT[T], r_hbT[T]
                norm_mod(cx, 0, 0)
                if T + 2 < 4:
                    issue_x(cxs, x_src, r_src, T + 2)
                hb = cx.hb
                p, rp = PS.next()
                hview = hb.rearrange("p c (b t) -> p c b t", t=128)
                for fo in range(8):
                    for fi in range(8):
                        S.op("tensor", lambda e, fo=fo, fi=fi, p=p, hview=hview: e.matmul(
                            p[:, fo * 8:(fo + 1) * 8].rearrange("p (b t) -> p b t", t=2), wcu[:, fi, fo * 128:(fo + 1) * 128],
                            hview[:, fi, :, 126:128], start=(fi == 0), stop=(fi == 7)), reads=[rwc, cx.r_hb], writes=[rp])
                S.op("scalar", lambda e, p=p: e.activation(out=hh[:, :, :].rearrange("p a b -> p (a b)"), in_=p[:, 0:64], func=AF.Identity),
                     reads=[rp], writes=[r_hh])
                S.op("vector", lambda e, T=T: e.tensor_tensor(
                    out=halo_acc[:, :, T * 4:(T + 1) * 4, :], in0=hh[:, 0:4, :].rearrange("p c (b t) -> p c b t", t=2),
                    in1=hh[:, 4:8, :].rearrange("p c (b t) -> p c b t", t=2), op=ALU.mult), reads=[r_hh], writes=[r_halo])
            S.dma("sync", lambda e: e.dma_start(out=hcon, in_=halo_acc[:, :, :, :].rearrange("p c i t -> p (c i t)")),
                  reads=[r_halo], writes=[r_hcon])
            for hp in range(4):
                for T in range(4):
                    hb, r_hb = hbT[T], r_hbT[T]
                    sg, rsg = stg.next()
                    for which in range(2):
                        fo = which * 4 + hp
                        p, rp = PS.next()
                        mm_group(p, rp, 512, lambda fi, fo=fo: wqkv[:, fi, fo * 128:(fo + 1) * 128], lambda fi, hb=hb: hb[:, fi, :], 8, [rwq, r_hb])
                        evac(which, sg[:, which, :], p[:, :], [rp], [rsg])
                    p, rp = PS.next()
                    for blk in range(4):
                        for fi in range(8):
                            S.op("tensor", lambda e, blk=blk, fi=fi, p=p, hb=hb, hp=hp: e.matmul(
                                p[:, blk * 128:(blk + 1) * 128], hb[:, fi, blk * 128:(blk + 1) * 128],
                                wqkv[:, fi, 1024 + hp * 128:1024 + (hp + 1) * 128], start=(fi == 0), stop=(fi == 7)),
                                reads=[rwq, r_hb], writes=[rp])
                    evac(1, sg[:, 2, :], p[:, :], [rp], [rsg])
                    S.dma("sync", lambda e, T=T, sg=sg, hp=hp: e.dma_start(out=qTd_w[hp, :, T * 512:(T + 1) * 512], in_=sg[:, 0, :]),
                          reads=[rsg], writes=[r_qTdh[hp]])
                    S.dma("sync", lambda e, T=T, sg=sg, hp=hp: e.dma_start(out=conH[0:128, hp, T * 512:(T + 1) * 512], in_=sg[:, 1, :]),
                          reads=[rsg], writes=[r_chp[hp], r_contrib])
                    S.dma("sync", lambda e, T=T, sg=sg, hp=hp: e.dma_start(
                        out=conH[128:256, hp, T * 512:(T + 1) * 512], in_=sg[:, 2, :]),
                        reads=[rsg], writes=[r_chp[hp], r_contrib])
                if do_gather and hp == 0:
                    issue_gather(0)
                if do_gather and hp == 1:
                    attn_prefetch(qTd_w, "gpsimd")
            if do_gather:
                S.coll(lambda e: e.collective_compute("AllGather", ALU.bypass, CC_GROUPS, [hcon.opt()], [hgath.opt()]),
                       reads=[r_hcon], writes=[r_hgath], blocking=False)

        def load_post_weights(l):
            W = Ctx()
            W.wbcu = wv3(0, 8, 1536); W.wgt = wv3(12288, 8, 2048)
            W.wpc = wv3(28672, 4, 1024); W.wpa = wv3(32768, 4, 1024); W.wo = wv3(36864, 8, 1024)
            W.rw0, W.rw1, W.rw2, W.rw3 = Res(), Res(), Res(), Res()
            todo = []

            def add(dst_view, src_rows_ap, rw, kchunks):
                src = src_rows_ap.rearrange("(k p) n -> p k n", p=128)
                step = max(1, kchunks // 4)
                for k0 in range(0, kchunks, step):
                    k1 = min(kchunks, k0 + step)
                    todo.append(lambda k0=k0, k1=k1: S.dma(
                        "gpsimd", lambda e: e.dma_start(out=dst_view[:, k0:k1, :], in_=src[:, k0:k1, :]), writes=[rw]))
            add(W.wbcu, w_in[l][:, 0:1536], W.rw0, 8)
            add(W.wgt, w_in[l][:, 3072:5120], W.rw1, 8)
            add(W.wpc, w_pc[l], W.rw2, 4)
            add(W.wpa, w_pa[l], W.rw2, 4)
            add(W.wo, w_out[l], W.rw3, 8)
            return W, todo

        ATB = Ctx()
        ATB.KT = abf(ATT0, 8192).rearrange("p (r t) -> p r t", r=4); ATB.r_KT = Res()
        ATB.Vs = [abf(ATT0 + 8192 + i * 8192, 8192).rearrange("p (r i c) -> p r i c", r=4, i=16) for i in range(2)]
        ATB.r_V = [Res(), Res()]
        ATB.maskt = abf(ATT0 + 24576, 8192).rearrange("p (k t) -> p k t", k=16); ATB.r_mask = Res()
        ATB.qpt = abf(ATT0 + 32768, 2048); ATB.r_qpt = Res()
        ATB.r_KTp = [Res() for _ in range(4)]
        ATB.r_qp = [Res() for _ in range(4)]

        def attn_load_V(hp, q="sync"):
            G4 = gath.rearrange("(h r s) c -> h r s c", h=4, r=4)
            pb = hp % 2
            for r in range(4):
                S.dma(q, lambda e, r=r: e.dma_start(
                    out=ATB.Vs[pb][:, r, :, :], in_=G4[hp, r, 128:256, :].rearrange("p (i c) -> p i c", c=128)),
                    reads=[r_gath, r_gaths[hp]], writes=[ATB.r_V[pb]])

        def attn_load_KTpart(hp, j, q="sync"):
            G4 = gath.rearrange("(h r s) c -> h r s c", h=4, r=4)
            S.dma(q, lambda e: e.dma_start(
                out=ATB.KT[:, :, j * 512:(j + 1) * 512], in_=G4[hp, :, 0:128, j * 512:(j + 1) * 512].rearrange("r p t -> p r t")),
                reads=[r_gath, r_gaths[hp]], writes=[ATB.r_KTp[j]] + getattr(ATB, "alias", []))

        def attn_load_qpart(hp, j, qsrc, q="sync"):
            S.dma(q, lambda e: e.dma_start(out=ATB.qpt[:, j * 512:(j + 1) * 512], in_=qsrc[hp][:, j * 512:(j + 1) * 512]),
                  reads=[r_qTd, r_qTdh[hp]], writes=[ATB.r_qp[j]])

        def attn_load_KQ(hp, qsrc, q="sync"):
            for j in range(4):
                attn_load_KTpart(hp, j, q)
                attn_load_qpart(hp, j, qsrc, q)

        def attn_prefetch(qsrc, q="sync"):
            S.dma(q, lambda e: e.dma_start(out=ATB.maskt, in_=maskd), writes=[ATB.r_mask])
            attn_load_KQ(0, qsrc, q)
            attn_load_V(0, q)
            ATB.prefetched = True

        def stage_attn(l, qsrc, fusedA):
            barrier()
            W, wtodo = load_post_weights(l)

            G4 = gath.rearrange("(h r s) c -> h r s c", h=4, r=4)

            KT, r_KT, Vs, r_V = ATB.KT, ATB.r_KT, ATB.Vs, ATB.r_V
            maskt, r_mask, qpt, r_qpt = ATB.maskt, ATB.r_mask, ATB.qpt, ATB.r_qpt
            ost = [abf(ATT0 + 34816 + i * 2048, 2048) for i in range(2)]; r_ost = [Res(), Res()]
            assert ATT0 + 38912 <= ARN
            if not getattr(ATB, "prefetched", False):
                attn_prefetch(qsrc)
            ATB.prefetched = False
            load_V = attn_load_V

            def load_KQ(hp):
                attn_load_KQ(hp, qsrc)

            steps = []
            for hp in range(4):
                for I in range(4):
                    kbs = list(range(16 * I + 15, -1, -1))
                    for n, kb in enumerate(kbs):
                        band = kb >= 16 * I
                        n0 = 128 * ((kb - 16 * I) // 4) if band else 0
                        steps.append(dict(hp=hp, I=I, kb=kb, band=band, n0=n0, first=(n == 0), last=(n == len(kbs) - 1)))

            def ps2(b):
                return PSALL[:, b * 512:(b + 2) * 512].rearrange("p (e t) -> p e t", e=2)

            ZP = Ring([ps2(0), ps2(2)])
            CP = Ring([ps2(4)])
            OB = ps2(6); r_OB = Res()
            Ering = Ring([c2(i) for i in range(0, 4)])
            SPring = Ring([c2(i) for i in range(4, 8)])
            Aring = Ring([c2(i) for i in range(8, 12)])
            Dring = Ring([c2(i) for i in range(12, 14)])
            ACring = Ring([c2(i) for i in range(14, 16)])
            st = {}
            cur_acc = [None]

            def Zpart(k):
                s = steps[k]
                hp, I, kb, n0 = s["hp"], s["I"], s["kb"], s["n0"]
                if I == 1 and s["first"] and hp + 1 < 4:
                    if fusedA:
                        issue_gather(hp + 1)
                    load_V(hp + 1)
                z, rz = ZP.next()
                r, i = kb % 4, kb // 4
                for e_ in range(2):
                    S.op("tensor", lambda e, e_=e_: e.matmul(
                        z[:, e_, n0:512], KT[64 * e_:64 * e_ + 64, r, i * 128:(i + 1) * 128],
                        qpt[64 * e_:64 * e_ + 64, I * 512 + n0:(I + 1) * 512], start=True, stop=True),
                        reads=[ATB.r_KTp[kb // 16], ATB.r_qp[I]], writes=[rz])
                if I == 3 and hp + 1 < 4:
                    if s["first"]:
                        for j in range(3):
                            attn_load_qpart(hp + 1, j, qsrc)
                    if kb % 16 == 0:
                        attn_load_KTpart(hp + 1, kb // 16)
                    if s["last"]:
                        attn_load_qpart(hp + 1, 3, qsrc)
                E, rE = Ering.next()
                S.op("scalar", lambda e: e.activation(out=E[:, :, n0:512], in_=z[:, :, n0:512], func=AF.Exp, scale=0.125),
                     reads=[rz], writes=[rE])
                if s["band"]:
                    kr = kb - 16 * I
                    for e_ in range(2):
                        S.op("vector", lambda e, e_=e_: e.tensor_tensor(out=E[:, e_, n0:512], in0=E[:, e_, n0:512],
                                                                         in1=maskt[:, kr, n0:512], op=ALU.mult),
                             reads=[rE, r_mask], writes=[rE])
                st[k] = dict(E=E, rE=rE)

            def SPpart(k):
                s = steps[k]; b = st[k]; n0 = s["n0"]
                SP, rSP = SPring.next()
                S.op("scalar", lambda e: e.activation(out=SP[:, :, n0:512], in_=b["E"][:, :, n0:512], func=AF.Ln, bias=1.0, scale=1.0),
                     reads=[b["rE"]], writes=[rSP])
                b["SP"] = SP; b["rSP"] = rSP

            def Cpart(k):
                s = steps[k]; b = st[k]; n0 = s["n0"]; first = s["first"]
                c, rc = CP.next()
                if first:
                    acc, racc = ACring.next()
                    S.op("gpsimd", lambda e: e.memset(acc[:, :, :], 0.0), writes=[racc])
                    cur_acc[0] = (acc, racc)
                acc, racc = cur_acc[0]
                for e_ in range(2):
                    S.op("tensor", lambda e, e_=e_: e.matmul(c[:, e_, n0:512], tri[:], b["SP"][:, e_, n0:512], start=True, stop=first),
                         reads=[r_tri, b["rSP"]], writes=[rc])
                    if not first:
                        S.op("tensor", lambda e, e_=e_: e.matmul(c[:, e_, n0:512], ones[:], acc[:, e_, n0:512], start=False, stop=True),
                             reads=[r_ones, racc], writes=[rc])
                Dt, rD = Dring.next()
                S.op("scalar", lambda e: e.activation(out=Dt[:, :, n0:512], in_=c[:, :, n0:512], func=AF.Exp, scale=-1.0),
                     reads=[rc], writes=[rD])
                b["D"] = Dt; b["rD"] = rD

            def Rest(k):
                s = steps[k]; b = st[k]; n0 = s["n0"]
                acc, racc = cur_acc[0]
                if not s["last"]:
                    nacc, rnacc = ACring.next()
                    if n0 > 0:
                        S.op("gpsimd", lambda e: e.memset(nacc[:, :, 0:n0], 0.0), writes=[rnacc])
                    S.op("gpsimd", lambda e: e.tensor_tensor(out=nacc[:, :, n0:512], in0=acc[:, :, n0:512], in1=b["SP"][:, :, n0:512],
                                                             op=ALU.add), reads=[racc, b["rSP"]], writes=[rnacc])
                    cur_acc[0] = (nacc, rnacc)
                A, rA = Aring.next()
                S.op("vector", lambda e: e.tensor_tensor(out=A[:, :, n0:512], in0=b["E"][:, :, n0:512], in1=b["D"][:, :, n0:512], op=ALU.mult),
                     reads=[b["rE"], b["rD"]], writes=[rA])
                b["A"] = A; b["rA"] = rA

            def AVpart(k):
                s = steps[k]; b = st.pop(k); n0 = s["n0"]; hp = s["hp"]; pb = hp % 2
                kb = s["kb"]; r, i = kb % 4, kb // 4
                if s["first"]:
                    for e_ in range(2):
                        S.op("tensor", lambda e, e_=e_: e.matmul(OB[:, e_, :], zerosw[:], maskt[:, 0, :], start=True, stop=False),
                             reads=[r_zw, r_mask], writes=[r_OB])
                last = s["last"]
                for e_ in range(2):
                    S.op("tensor", lambda e, e_=e_: e.matmul(OB[:, e_, n0:512], Vs[pb][:, r, i, :], b["A"][:, e_, n0:512], start=False, stop=last),
                         reads=[r_V[pb], b["rA"]], writes=[r_OB])
                if last:
                    I = s["I"]
                    for e_ in range(2):
                        S.op("scalar", lambda e, e_=e_: e.activation(
                            out=ost[pb][64 * e_:64 * e_ + 64, I * 512:(I + 1) * 512], in_=OB[64 * e_:64 * e_ + 64, e_, :], func=AF.Identity),
                            reads=[r_OB], writes=[r_ost[pb]])
                    if I == 3:
                        S.dma("sync", lambda e: e.dma_start(out=oTd[hp], in_=ost[pb]), reads=[r_ost[pb]], writes=[r_oTd])

            n = len(steps)
            Zpart(0); SPpart(0); Zpart(1); SPpart(1)
            for k in range(n):
                if k >= 48 and k % 8 == 0 and wtodo:
                    wtodo.pop(0)()
                if k + 2 < n:
                    Zpart(k + 2)
                Cpart(k)
                if k + 2 < n:
                    SPpart(k + 2)
                Rest(k)
                if k >= 2:
                    AVpart(k - 2)
            AVpart(n - 2); AVpart(n - 1)
            while wtodo:
                wtodo.pop(0)()
            return W

        def new_W1P():
            P = Ctx(); P.rw1 = [Res() for _ in range(8)]; P.issued = set()
            return P

        def issue_W1(l, g, dead_res, P=None):
            P = P or cur_W1P[0]
            dst = wv3(g * 4096, 8, 512)
            S.dma("gpsimd", lambda e: e.dma_start(
                out=dst, in_=w1[l][:, g * 512:(g + 1) * 512].rearrange("(k p) n -> p k n", p=128)),
                writes=[P.rw1[g]] + list(dead_res))
            P.issued.add(g)

        cur_W1P = [None]

        def stage_post(l, x_src, r_src, x_dst, r_dst, W):
            barrier()
            cur_W1P[0] = new_W1P()
            wbcu, wgt, wpc, wpa, wo = W.wbcu, W.wgt, W.wpc, W.wpa, W.wo
            rw0, rw1, rw2, rw3 = W.rw0, W.rw1, W.rw2, W.rw3
            cx = mk_ctx(45056, 512)
            hb, mo = cx.hb, cx.mo
            off = cx.end
            vcp = af32(off, 4160).rearrange("p (c b t) -> p c b t", c=4, b=4); r_vcp = Res(); off += 8320
            bgt = af32(off, 2048).rearrange("p (c t) -> p c t", c=4); r_bgt = Res(); off += 4096
            ycv = abf(off, 2048).rearrange("p (c t) -> p c t", c=4); r_ycv = Res(); off += 2048
            otile = abf(off, 2048).rearrange("p (c t) -> p c t", c=4); r_otile = Res(); off += 2048
            mrg = abf(off, 4096).rearrange("p (c t) -> p c t", c=8); r_mrg = Res(); off += 4096
            assert off <= ARN
            S.dma("sync", lambda e: e.dma_start(out=hg[:], in_=hgath.rearrange("(r p) n -> p r n", p=128)),
                  reads=[r_hgath], writes=[r_hg])
            hgv = hg[:, :, :].rearrange("p r (c i t) -> p r c i t", c=4, i=16)
            S.op("vector", lambda e: e.tensor_scalar(out=hsel[:], in0=hgv[:, 0], scalar1=wsel[:, 0:1], scalar2=None, op0=ALU.mult),
                 reads=[r_hg, r_wsel], writes=[r_hsel])
            for r in (1, 2):
                S.op("vector", lambda e, r=r: e.scalar_tensor_tensor(out=hsel[:], in0=hgv[:, r], scalar=wsel[:, r:r + 1], in1=hsel[:],
                                                                     op0=ALU.mult, op1=ALU.add), reads=[r_hg, r_wsel, r_hsel], writes=[r_hsel])
            S.op("vector", lambda e: e.scalar_tensor_tensor(out=hsel[:, :, 1:16, :], in0=hgv[:, 3, :, 0:15, :], scalar=wsel[:, 3:4],
                                                            in1=hsel[:, :, 1:16, :], op0=ALU.mult, op1=ALU.add),
                 reads=[r_hg, r_wsel, r_hsel], writes=[r_hsel])
            oTv = oTd.rearrange("h p t -> p h t")
            def prologue(T):
                load_x(cx, x_src, r_src, T, 4, manual=True)
                norm_mod(cx, 0, 0)

            issue_x(cx, x_src, r_src, 0)
            issue_x(cx, x_src, r_src, 1)
            prologue(0)
            for T in range(4):
                xt_T, r_xt_T = cx.xt, cx.r_xt
                S.dma("sync", lambda e, T=T: e.dma_start(out=otile, in_=oTv[:, :, T * 512:(T + 1) * 512]),
                      reads=[r_oTd], writes=[r_otile])
                S.op("gpsimd", lambda e, T=T: e.tensor_copy(out=vcp[:, :, :, 0:2], in_=hsel[:, :, T * 4:(T + 1) * 4, :]),
                     reads=[r_hsel], writes=[r_vcp])
                for c in range(4):
                    for which in (1, 2, 0):
                        fo = which * 4 + c
                        p, rp = PS.next()
                        mm_group(p, rp, 512, lambda fi, fo=fo: wbcu[:, fi, fo * 128:(fo + 1) * 128], lambda fi: hb[:, fi, :], 8, [rw0, cx.r_hb])
                        if which == 1:
                            S.op("scalar", lambda e, c=c, p=p: e.activation(out=mo[:, c, :], in_=p[:, :], func=AF.Identity),
                                 reads=[rp], writes=[cx.r_mo])
                        elif which == 2:
                            S.op("vector", lambda e, c=c, p=p: e.tensor_tensor(
                                out=vcp[:, c, :, 2:130], in0=mo[:, c, :].rearrange("p (b t) -> p b t", t=128),
                                in1=p[:, :].rearrange("p (b t) -> p b t", t=128), op=ALU.mult),
                                reads=[rp, cx.r_mo], writes=[r_vcp])
                        else:
                            S.op("scalar", lambda e, c=c, p=p: e.activation(out=bgt[:, c, :], in_=p[:, :], func=AF.Identity),
                                 reads=[rp], writes=[r_bgt])
                    t, rt = tmpA.next()
                    tv = t[:, :].rearrange("p (b t) -> p b t", t=128)
                    S.op("scalar", lambda e, c=c, tv=tv: e.activation(out=tv, in_=vcp[:, c, :, 2:130], func=AF.Identity, scale=cwt[:, 2, c:c + 1]),
                         reads=[r_vcp, r_cwt], writes=[rt])
                    S.op("vector", lambda e, c=c, tv=tv: e.scalar_tensor_tensor(out=tv, in0=vcp[:, c, :, 1:129], scalar=cwt[:, 1, c:c + 1], in1=tv,
                                                                                op0=ALU.mult, op1=ALU.add), reads=[r_vcp, r_cwt, rt], writes=[rt])
                    S.op("vector", lambda e, c=c, tv=tv: e.scalar_tensor_tensor(out=tv, in0=vcp[:, c, :, 0:128], scalar=cwt[:, 0, c:c + 1], in1=tv,
                                                                                op0=ALU.mult, op1=ALU.add), reads=[r_vcp, r_cwt, rt], writes=[rt])
                    S.op("vector", lambda e, c=c, t=t: e.tensor_tensor(out=ycv[:, c, :], in0=t[:, :], in1=bgt[:, c, :], op=ALU.mult),
                         reads=[rt, r_bgt], writes=[r_ycv])
                if T == 3:
                    for g in range(3):
                        issue_W1(l, g, [rw0])
                for fo in range(8):
                    pya, rya = PS.next()
                    mm_group(pya, rya, 512, lambda ci, fo=fo: wpa[:, ci, fo * 128:(fo + 1) * 128], lambda ci: otile[:, ci, :], 4, [rw2, r_otile])
                    pga, rga = PS.next()
                    mm_group(pga, rga, 512, lambda fi, fo=fo: wgt[:, fi, fo * 128:(fo + 1) * 128], lambda fi: hb[:, fi, :], 8, [rw1, cx.r_hb])
                    pgb, rgb = PS.next()
                    mm_group(pgb, rgb, 512, lambda fi, fo=fo: wgt[:, fi, 1024 + fo * 128:1024 + (fo + 1) * 128], lambda fi: hb[:, fi, :], 8,
                             [rw1, cx.r_hb])
                    pyc, ryc = PS.next()
                    mm_group(pyc, ryc, 512, lambda ci, fo=fo: wpc[:, ci, fo * 128:(fo + 1) * 128], lambda ci: ycv[:, ci, :], 4, [rw2, r_ycv])
                    sa, rsa = tmpB.next()
                    S.op("scalar", lambda e, p=pga, sa=sa: e.activation(out=sa[:, :], in_=p[:, :], func=AF.Sigmoid), reads=[rga], writes=[rsa])
                    sg, rsg = tmpB.next()
                    S.op("scalar", lambda e, p=pgb, sg=sg: e.activation(out=sg[:, :], in_=p[:, :], func=AF.Sigmoid), reads=[rgb], writes=[rsg])
                    S.op("vector", lambda e, p=pya, sg=sg: e.tensor_tensor(out=sg[:, :], in0=sg[:, :], in1=p[:, :], op=ALU.mult),
                         reads=[rsg, rya], writes=[rsg])
                    S.op("vector", lambda e, p=pyc, sa=sa: e.tensor_tensor(out=sa[:, :], in0=sa[:, :], in1=p[:, :], op=ALU.mult),
                         reads=[rsa, ryc], writes=[rsa])
                    S.op("gpsimd", lambda e, fo=fo, sa=sa, sg=sg: e.tensor_tensor(out=mrg[:, fo, :], in0=sa[:, :], in1=sg[:, :], op=ALU.add),
                         reads=[rsa, rsg], writes=[r_mrg])
                if T == 3:
                    for g in range(3, 7):
                        issue_W1(l, g, [rw1])
                    issue_W1(l, 7, [rw2])
                for fo in range(8):
                    if fo == 2 and T + 1 < 4:
                        prologue(T + 1)
                    p, rp = PS.next()
                    mm_group(p, rp, 512, lambda fi, fo=fo: wo[:, fi, fo * 128:(fo + 1) * 128], lambda fi: mrg[:, fi, :], 8, [rw3, r_mrg])
                    evac(fo, mo[:, fo, :], p[:, :], [rp], [cx.r_mo])
                post_norm_residual(cx, 1, x_dst, r_dst, T * 512, (mrg, r_mrg), xt_T, r_xt_T)
                if T + 2 < 4:
                    issue_x(cx, x_src, r_src, T + 2)

        def stage_mlp(l, x_src, r_src, x_dst, r_dst, W1P):
            barrier()
            W2v = wv3(32768, 32, 1024)
            W1c = [wv3(g * 4096, 8, 512) for g in range(8)]
            rw1 = W1P.rw1
            rw2 = [Res() for _ in range(8)]
            for g in range(8):
                if g not in W1P.issued:
                    issue_W1(l, g, [])
            for g in range(8):
                S.dma("gpsimd", lambda e, g=g: e.dma_start(
                    out=W2v[:, g * 4:(g + 1) * 4, :], in_=w2[l][g * 512:(g + 1) * 512, :].rearrange("(k p) n -> p k n", p=128)),
                    writes=[rw2[g]])
            NC_ = 256
            cx = mk_ctx(65536, NC_)
            hb, mo = cx.hb, cx.mo
            ff1 = abf(cx.end, 32 * NC_).rearrange("p (k n) -> p k n", k=32); r_ff1 = Res()
            assert cx.end + 32 * NC_ <= ARN
            def prologue(T):
                load_x(cx, x_src, r_src, T, NT // NC_, manual=True)
                norm_mod(cx, 2, 3)

            issue_x(cx, x_src, r_src, 0)
            issue_x(cx, x_src, r_src, 1)
            prologue(0)
            for T in range(NT // NC_):
                xt_T, r_xt_T = cx.xt, cx.r_xt
                for fo in range(32):
                    p, rp = PS.next()
                    mm_group(p, rp, NC_, lambda fi, fo=fo: W1c[fo // 4][:, fi, (fo % 4) * 128:(fo % 4 + 1) * 128], lambda fi: hb[:, fi, :], 8, [rw1[fo // 4], cx.r_hb])
                    t, rt = tmpB.next()
                    S.op("scalar", lambda e, p=p, t=t: e.activation(out=t[:, 0:NC_], in_=p[:, 0:NC_], func=AF.Relu), reads=[rp], writes=[rt])
                    eng = "vector" if fo % 2 == 0 else "gpsimd"
                    S.op(eng, lambda e, fo=fo, t=t: e.tensor_tensor(out=ff1[:, fo, :], in0=t[:, 0:NC_], in1=t[:, 0:NC_], op=ALU.mult),
                         reads=[rt], writes=[r_ff1])
                for fo in range(8):
                    if fo == 3 and T + 1 < NT // NC_:
                        prologue(T + 1)
                    p, rp = PS.next()
                    mm_group(p, rp, NC_, lambda fi, fo=fo: W2v[:, fi, fo * 128:(fo + 1) * 128], lambda fi: ff1[:, fi, :], 32, lambda fi: [rw2[fi // 4], r_ff1])
                    evac(fo, mo[:, fo, :], p[:, 0:NC_], [rp], [cx.r_mo])
                post_norm_residual(cx, 3, x_dst, r_dst, T * NC_, (ff1[:, 0:8, :], r_ff1), xt_T, r_xt_T)
                if T + 2 < NT // NC_:
                    issue_x(cx, x_src, r_src, T + 2)

        cur_x, r_cur = x_in, r_xin
        nB_done = 0
        last_mod = None
        finals = []
        for (kind, l) in parts:
            P = make_pre_A(l, cur_x, r_cur) if kind == "A" else None
            if last_mod != l:
                stage_mod(l, P.go if P is not None else None)
                last_mod = l
            elif P is not None:
                P.go()
            if kind == "A":
                stage_A(l, cur_x, r_cur, ("B", l) in parts, P)
                finals += [r_contrib, r_hcon, r_qTd] + r_qTdh
            else:
                fusedA = ("A", l) in parts
                qsrc = qTd_w if fusedA else qTd_r
                W = stage_attn(l, qsrc, fusedA)
                stage_post(l, cur_x, r_cur, xmid, r_xmid, W)
                nB_done += 1
                if n_B > 1 and nB_done < n_B:
                    dst, rdst = xl0, r_xl0
                else:
                    dst, rdst = x_out, r_xout
                stage_mlp(l, xmid, r_xmid, dst, rdst, cur_W1P[0])
                cur_x, r_cur = dst, rdst
                finals.append(rdst)
        S.final_wait("sync", finals)
        S.emit(block)
    return nc


_PROG_CACHE = {}


def _prog(parts):
    key = tuple(parts)
    if key not in _PROG_CACHE:
        _PROG_CACHE[key] = build(list(parts))
    return _PROG_CACHE[key]


def _core_tokens(j):
    return [4 * i + j for i in range(NB)]


def _prep_static(c, w_ada, b_ada, g_pre_mix, g_post_mix, g_pre_mlp, g_post_mlp, w_in, conv_w,
                 w_proj_conv, w_proj_attn, w_out, w_mlp_in, w_mlp_out):
    f = lambda a: np.ascontiguousarray(np.asarray(a, dtype=np.float32))
    shared = {
        "gvec": f(np.stack([np.asarray(g).reshape(2, 8, 128) for g in (g_pre_mix, g_post_mix, g_pre_mlp, g_post_mlp)], axis=1)
                  .transpose(0, 3, 1, 2)),
        "w_in": f(w_in),
        "convw": f(np.asarray(conv_w).reshape(2, 3, 4, 128).transpose(0, 3, 1, 2)),
        "w_pc": f(w_proj_conv), "w_pa": f(w_proj_attn), "w_out": f(w_out),
        "w1": f(w_mlp_in), "w2": f(w_mlp_out),
        "tri": (np.arange(128)[:, None] >= np.arange(128)[None, :]).astype(ml_dtypes.bfloat16),
    }
    per_core = []
    b_ada_l = np.asarray(b_ada).reshape(2, 48, 128).transpose(0, 2, 1)
    s_idx = np.arange(128)[:, None]
    t_idx = np.arange(128)[None, :]
    for core in range(NCORES):
        b, j = core // 4, core % 4
        mask = np.zeros((128, 16, 4, 128), dtype=np.float32)
        for kr in range(16):
            for m in range(4):
                q = 4 * m + j
                if kr < q:
                    mask[:, kr, m, :] = 1.0
                elif kr == q:
                    mask[:, kr, m, :] = (t_idx > s_idx).astype(np.float32)
        wsel = np.zeros((128, 4), dtype=np.float32)
        wsel[:, (j - 1) % 4] = 1.0
        d = dict(shared)
        d["mask"] = mask.reshape(128, 16, 512).astype(ml_dtypes.bfloat16)
        d["wsel"] = wsel
        d["cT"] = f(np.asarray(c)[b].reshape(8, 128).T)
        d["w_ada_s"] = f(np.asarray(w_ada)[:, :, j * 1536:(j + 1) * 1536])
        d["b_ada_s"] = f(b_ada_l[:, :, j * 12:(j + 1) * 12])
        per_core.append(d)
    return per_core


def _shard_x(x):
    x = np.asarray(x, dtype=np.float32)
    out = []
    for core in range(NCORES):
        b, j = core // 4, core % 4
        xb = x[b].reshape(64, 128, D)[j::4].reshape(NT, D)
        out.append(np.ascontiguousarray(xb.T))
    return out


def _unshard_x(xs):
    out = np.zeros((2, 64, 128, D), dtype=np.float32)
    for core in range(NCORES):
        b, j = core // 4, core % 4
        out[b, j::4] = np.asarray(xs[core]).T.reshape(NB, 128, D)
    return out.reshape(2, 8192, D)


def _gather(res, name, nq=1):
    outs = []
    for core in range(NCORES):
        b = core // 4
        parts = []
        for q in range(nq):
            for r in range(4):
                a = np.asarray(res[b * 4 + r][name])
                n = a.shape[0] // nq
                parts.append(a[q * n:(q + 1) * n])
        outs.append(np.ascontiguousarray(np.concatenate(parts, axis=0)))
    return outs


def kernel(x, c, w_ada, b_ada, g_pre_mix, g_post_mix, g_pre_mlp, g_post_mlp,
           w_in, conv_w, w_proj_conv, w_proj_attn, w_out, w_mlp_in, w_mlp_out):
    static = _prep_static(c, w_ada, b_ada, g_pre_mix, g_post_mix, g_pre_mlp, g_post_mlp, w_in, conv_w,
                          w_proj_conv, w_proj_attn, w_out, w_mlp_in, w_mlp_out)
    xs = _shard_x(x)
    cores = list(range(NCORES))
    if FUSED:
        nc = _prog((("A", 0), ("B", 0), ("A", 1), ("B", 1)))
        in_maps = [dict(static[i], x_in=xs[i]) for i in cores]
        res = run_bass_kernel_spmd(nc, in_maps, core_ids=cores).results
        return _unshard_x([res[i]["x_out"] for i in cores])
    nc1 = _prog((("A", 0),))
    r1 = run_bass_kernel_spmd(nc1, [dict(static[i], x_in=xs[i]) for i in cores], core_ids=cores).results
    g1, h1 = _gather(r1, "contrib", 4), _gather(r1, "hcon")
    nc2 = _prog((("B", 0), ("A", 1)))
    r2 = run_bass_kernel_spmd(nc2, [dict(static[i], x_in=xs[i], gath=g1[i], hgath=h1[i], qTd_r=np.asarray(r1[i]["qTd_w"]))
                                    for i in cores], core_ids=cores).results
    g2, h2 = _gather(r2, "contrib", 4), _gather(r2, "hcon")
    nc3 = _prog((("B", 1),))
    r3 = run_bass_kernel_spmd(nc3, [dict(static[i], x_in=np.asarray(r2[i]["x_out"]), gath=g2[i], hgath=h2[i],
                                         qTd_r=np.asarray(r2[i]["qTd_w"])) for i in cores], core_ids=cores).results
    return _unshard_x([r3[i]["x_out"] for i in cores])
```

```python
import numpy as np
from contextlib import ExitStack
import ml_dtypes
import concourse.bass as bass
import concourse.mybir as mybir
from concourse.bass_utils import run_bass_kernel_spmd

F32 = mybir.dt.float32
BF16 = mybir.dt.bfloat16
AF = mybir.ActivationFunctionType
ALU = mybir.AluOpType

D = 1024
NT = 2048
NB = 16
DFF = 4096
EPS = 1e-6
NCORES = 8
FUSED = True


class Res:
    __slots__ = ("name", "lw", "rd")

    def __init__(self, name=""):
        self.name = name
        self.lw = None
        self.rd = []


class Sched:
    ENGS = ("sync", "scalar", "vector", "gpsimd", "tensor")
    NDMA = 40
    NHW = 24

    def __init__(self, nc, es):
        self.nc = nc
        self._es = es
        self.ops = {e: [] for e in self.ENGS}
        self.epoch = 0
        self.cnt = {e: 0 for e in self.ENGS}
        self.esems = [{e: es.enter_context(nc.semaphore("s0_" + e)) for e in self.ENGS}]
        self.dsem = [es.enter_context(nc.semaphore("d%d" % i)) for i in range(self.NDMA)]
        self.dcnt = [0] * self.NDMA
        self.drr = {"sync": 0, "gpsimd": 0, "scalar": 0}
        self.waited = {e: {} for e in self.ENGS}
        self.ccsem = None

    def _ekey(self, eng):
        return ("eng", eng, self.epoch)

    def _deps(self, eng, reads, writes):
        deps = []
        for r in reads:
            if r.lw is not None:
                deps.append(r.lw)
        for w in writes:
            if w.lw is not None:
                deps.append(w.lw)
            deps.extend(w.rd)
        wd = self.waited[eng]
        best = {}
        for (k, v) in deps:
            if k[0] == "eng":
                if k[2] < self.epoch:
                    continue
                if k[1] == "tensor" and eng == "tensor":
                    continue
            if wd.get(k, 0) >= v:
                continue
            wd[k] = v
            best[k] = max(best.get(k, 0), v)
        return list(best.items())

    def _commit(self, tok, reads, writes):
        for r in reads:
            r.rd.append(tok)
        for w in writes:
            w.lw = tok
            w.rd = []

    def op(self, eng, fn, reads=(), writes=()):
        waits = self._deps(eng, reads, writes)
        self.cnt[eng] += 1
        tok = (self._ekey(eng), self.cnt[eng])
        self.ops[eng].append((fn, waits, self.esems[self.epoch][eng], 1))
        self._commit(tok, reads, writes)
        return tok

    def dma(self, eng, fn, reads=(), writes=()):
        waits = self._deps(eng, reads, writes)
        if eng == "gpsimd":
            n = self.NDMA - self.NHW
            i = self.NHW + self.drr[eng]
            self.drr[eng] = (self.drr[eng] + 1) % n
        else:
            i = self.drr["sync"]
            self.drr["sync"] = (self.drr["sync"] + 1) % self.NHW
        key = ("dma", i)
        if self.dcnt[i] > self.waited[eng].get(key, 0):
            self.waited[eng][key] = self.dcnt[i]
            waits = [w for w in waits if w[0] != key] + [(key, self.dcnt[i])]
        self.dcnt[i] += 16
        tok = (("dma", i), self.dcnt[i])
        self.ops[eng].append((fn, waits, self.dsem[i], 16))
        self._commit(tok, reads, writes)
        return tok

    def coll(self, fn, reads=(), writes=(), blocking=True):
        if self.ccsem is None:
            self.ccsem = self._es.enter_context(self.nc.semaphore("ccsem"))
            self.ccdummy = self._es.enter_context(self.nc.sbuf_tensor("ccdummy", [128, 8], F32))
            self.cccnt = 0
        waits = self._deps("gpsimd", reads, writes)
        self.cccnt += 1
        self.ops["gpsimd"].append((fn, waits, self.ccsem, 1))
        if not blocking:
            tok = (("cc",), self.cccnt)
            self._commit(tok, reads, writes)
            return tok
        self.ops["gpsimd"].append((None, [(("cc",), self.cccnt)], None, 0))
        dm = self.ccdummy
        return self.op("gpsimd", lambda e: e.memset(dm[:], 0.0), reads=reads, writes=writes)

    def barrier(self):
        for e in self.ENGS:
            waits = []
            wd = self.waited[e]
            for k2 in self.ENGS:
                key = self._ekey(k2)
                if self.cnt[k2] > wd.get(key, 0):
                    wd[key] = self.cnt[k2]
                    waits.append((key, self.cnt[k2]))
            for i in range(self.NDMA):
                key = ("dma", i)
                if self.dcnt[i] > wd.get(key, 0):
                    wd[key] = self.dcnt[i]
                    waits.append((key, self.dcnt[i]))
            if waits:
                self.ops[e].append((None, waits, None, 0))
        self.epoch += 1
        self.esems.append({e: self._es.enter_context(self.nc.semaphore("s%d_%s" % (self.epoch, e))) for e in self.ENGS})
        self.cnt = {e: 0 for e in self.ENGS}

    def _semof(self, k):
        if k[0] == "cc":
            return self.ccsem
        if k[0] == "dma":
            return self.dsem[k[1]]
        return self.esems[k[2]][k[1]]

    def final_wait(self, eng, resources):
        waits = self._deps(eng, resources, ())
        self.ops[eng].append((None, waits, None, 0))

    def emit(self, block):
        for e in self.ENGS:
            ops = self.ops[e]
            if not ops:
                continue

            def body(engine, ops=ops):
                for fn, waits, sem, inc in ops:
                    for k, v in waits:
                        engine.wait_ge(self._semof(k), v)
                    if fn is not None:
                        fn(engine).then_inc(sem, inc)

            getattr(block, e)(body)


class Ring:
    def __init__(self, tiles):
        self.tiles = tiles
        self.res = [Res() for _ in tiles]
        self.i = 0

    def next(self):
        t, r = self.tiles[self.i], self.res[self.i]
        self.i = (self.i + 1) % len(self.tiles)
        return t, r


def build(parts):
    nc = bass.Bass("TRN2", target_bir_lowering=False)
    es = ExitStack()
    with es:
        S = Sched(nc, es)
        layers = sorted(set(l for _, l in parts))

        def dram(name, shape, dt, kind):
            return nc.dram_tensor(name, list(shape), dt, kind=kind).ap()

        x_in = dram("x_in", [D, NT], F32, "ExternalInput"); r_xin = Res()
        cT = dram("cT", [128, 8], F32, "ExternalInput")
        w_ada = dram("w_ada_s", [2, D, 1536], F32, "ExternalInput")
        b_ada = dram("b_ada_s", [2, 128, 12], F32, "ExternalInput")
        modc = dram("modc", [128, 12], F32, "Internal"); r_modc = Res()
        modg = dram("modg", [512, 12], F32, "Internal"); r_modg = Res()
        gvec = dram("gvec", [2, 128, 4, 8], F32, "ExternalInput")
        w_in = dram("w_in", [2, D, 5120], F32, "ExternalInput")
        convw = dram("convw", [2, 128, 3, 4], F32, "ExternalInput")
        w_pc = dram("w_pc", [2, 512, D], F32, "ExternalInput")
        w_pa = dram("w_pa", [2, 512, D], F32, "ExternalInput")
        w_out = dram("w_out", [2, D, D], F32, "ExternalInput")
        w1 = dram("w1", [2, D, DFF], F32, "ExternalInput")
        w2 = dram("w2", [2, DFF, D], F32, "ExternalInput")
        maskd = dram("mask", [128, 16, 512], BF16, "ExternalInput")
        wseld = dram("wsel", [128, 4], F32, "ExternalInput")
        trid = dram("tri", [128, 128], BF16, "ExternalInput")

        firstA = parts[0][0] == "A"
        lastA = parts[-1][0] == "A"
        fused_pairs = [l for l in layers if ("A", l) in parts and ("B", l) in parts]
        ckind = "ExternalOutput" if lastA else "Internal"
        has_A = any(p == "A" for p, _ in parts)
        has_B = any(p == "B" for p, _ in parts)
        if has_A:
            contrib = dram("contrib", [1024, NT], BF16, ckind); r_contrib = Res()
            hcon = dram("hcon", [128, 128], F32, ckind); r_hcon = Res()
            qTd_w = dram("qTd_w", [4, 128, NT], BF16, ckind)
        if has_B:
            gk = "ExternalInput" if not firstA else "Internal"
            gath = dram("gath", [4 * 1024, NT], BF16, gk); r_gath = Res(); r_gaths = [Res() for _ in range(4)]
            hgath = dram("hgath", [4 * 128, 128], F32, gk); r_hgath = Res()
            if not firstA:
                qTd_r = dram("qTd_r", [4, 128, NT], BF16, "ExternalInput")
            oTd = dram("oTd", [4, 128, NT], BF16, "Internal"); r_oTd = Res()
            xmid = dram("xmid", [D, NT], F32, "Internal"); r_xmid = Res()
        r_qTd = Res()
        r_qTdh = [Res() for _ in range(4)]
        x_out = None
        if has_B:
            x_out = dram("x_out", [D, NT], F32, "ExternalOutput"); r_xout = Res()
        n_B = sum(1 for p, _ in parts if p == "B")
        if n_B > 1:
            xl0 = dram("xl0", [D, NT], F32, "Internal"); r_xl0 = Res()

        def sb(name, shape, dt):
            return es.enter_context(nc.sbuf_tensor(name, list(shape), dt))

        ARN = 86144
        AR = sb("AR", [128, ARN], BF16)
        CM = sb("CM", [128, 16384], BF16)

        def abf(off, n):
            return AR[:, off:off + n]

        def af32(off, n):
            return AR[:, off:off + 2 * n].bitcast(F32)

        def cf32(i):
            return CM[:, i * 1024:(i + 1) * 1024].bitcast(F32)

        def c2(i):
            return CM[:, i * 1024:(i + 1) * 1024].rearrange("p (e t) -> p e t", e=2)

        rstd = cf32(0); r_rstd = Res()
        lnv = cf32(1); r_lnv = Res()
        tmpA = Ring([cf32(2 + i) for i in range(3)])
        tmpB = Ring([cf32(5 + i) for i in range(3)])
        ones = sb("ones", [128, 128], BF16); r_ones = Res()
        tri = sb("tri_sb", [128, 128], BF16); r_tri = Res()
        wsel = sb("wselt", [128, 4], F32); r_wsel = Res()
        cbf = sb("cbf", [128, 8], BF16); r_cbf = Res()
        badat = sb("badat", [128, 12], F32); r_bada = Res()
        gvt = sb("gvt", [128, 4, 8], F32); r_gvt = Res()
        modt = sb("modt", [128, 48], F32); r_mod = Res()
        coef = sb("coef", [128, 4, 8], F32); r_coef = Res()
        cwt = sb("cwt", [128, 3, 4], F32); r_cwt = Res()
        halo_acc = sb("halo_acc", [128, 4, 16, 2], F32); r_halo = Res()
        hg = sb("hg", [128, 4, 128], F32); r_hg = Res()
        hsel = sb("hsel", [128, 4, 16, 2], F32); r_hsel = Res()
        hh = sb("hh", [128, 8, 8], F32); r_hh = Res()

        class Ctx:
            pass

        PSALL = es.enter_context(nc.psum_tensor("psall", [128, 4096], F32))
        PS = Ring([PSALL[:, i * 512:(i + 1) * 512] for i in range(8)])
        zerosw = sb("zerosw", [128, 128], BF16); r_zw = Res()

        block = es.enter_context(nc.Block())

        S.op("vector", lambda e: e.memset(ones[:], 1.0), writes=[r_ones])
        S.op("vector", lambda e: e.memset(zerosw[:], 0.0), writes=[r_zw])
        S.dma("sync", lambda e: e.dma_start(out=tri[:], in_=trid), writes=[r_tri])
        S.dma("sync", lambda e: e.dma_start(out=wsel[:], in_=wseld), writes=[r_wsel])
        S.dma("gpsimd", lambda e: e.dma_start(out=cbf[:], in_=cT), writes=[r_cbf])

        barrier = S.barrier
        ATT0 = 45056
        CC_GROUPS = [[0, 1, 2, 3], [4, 5, 6, 7]]
        r_chp = [Res() for _ in range(4)]

        def issue_gather(hp):
            S.coll(lambda e: e.collective_compute(
                "AllGather", ALU.bypass, CC_GROUPS, [contrib[hp * 256:(hp + 1) * 256, :].opt()],
                [gath[hp * 1024:(hp + 1) * 1024, :].opt()]), reads=[r_chp[hp]], writes=[r_gaths[hp]], blocking=False)

        def wv3(off, k, n):
            return AR[:, off:off + k * n].rearrange("p (k n) -> p k n", k=k)

        def load_w(dst_view, src_rows_ap, rw, kchunks, nsplit=4):
            src = src_rows_ap.rearrange("(k p) n -> p k n", p=128)
            step = max(1, kchunks // nsplit)
            for k0 in range(0, kchunks, step):
                k1 = min(kchunks, k0 + step)
                S.dma("gpsimd", lambda e, k0=k0, k1=k1: e.dma_start(out=dst_view[:, k0:k1, :], in_=src[:, k0:k1, :]),
                      writes=[rw])

        def evac(i, out_ap, in_ap, reads, writes):
            if i % 2 == 0:
                S.op("scalar", lambda e: e.activation(out=out_ap, in_=in_ap, func=AF.Identity), reads=reads, writes=writes)
            else:
                S.op("vector", lambda e: e.tensor_copy(out=out_ap, in_=in_ap), reads=reads, writes=writes)

        def mk_ctx(off, ncol, with_mo=True):
            cx = Ctx()
            cx.ncol = ncol
            xt0 = af32(off, 8 * ncol).rearrange("p (c t) -> p c t", c=8)
            xt1 = CM[:, 8192:8192 + 16 * ncol].bitcast(F32).rearrange("p (c t) -> p c t", c=8)
            cx.xts = [xt0, xt1]; cx.r_xts = [Res(), Res()]
            cx.xt, cx.r_xt = xt0, cx.r_xts[0]
            off += 16 * ncol
            cx.hb = abf(off, 8 * ncol).rearrange("p (c t) -> p c t", c=8); cx.r_hb = Res()
            off += 8 * ncol
            if with_mo:
                cx.mo = af32(off, 8 * ncol).rearrange("p (c t) -> p c t", c=8); cx.r_mo = Res()
                off += 16 * ncol
            cx.end = off
            return cx

        def sumsq_rstd(cx, src3, r_src, scratch=None):
            ncol = cx.ncol
            hb, r_hb = (cx.hb, cx.r_hb) if scratch is None else scratch
            S.op("scalar", lambda e: e.activation(out=hb[:, :, :], in_=src3, func=AF.Square),
                 reads=[r_src], writes=[r_hb])
            p, rp = PS.next()
            for c in range(8):
                S.op("tensor", lambda e, c=c: e.matmul(p[:, 0:ncol], ones[:], hb[:, c, :], start=(c == 0), stop=(c == 7)),
                     reads=[r_ones, r_hb], writes=[rp])
            S.op("scalar", lambda e: e.activation(out=lnv[:, 0:ncol], in_=p[:, 0:ncol], func=AF.Ln, bias=EPS, scale=1.0 / D),
                 reads=[rp], writes=[r_lnv])
            S.op("scalar", lambda e: e.activation(out=rstd[:, 0:ncol], in_=lnv[:, 0:ncol], func=AF.Exp, scale=-0.5),
                 reads=[r_lnv], writes=[r_rstd])

        def norm_mod(cx, ai, bi):
            ncol = cx.ncol
            xt, hb = cx.xt, cx.hb
            sumsq_rstd(cx, xt[:, :, :], cx.r_xt)
            for c in range(8):
                t, rt = tmpA.next()
                S.op("vector", lambda e, c=c, t=t: e.scalar_tensor_tensor(
                    out=t[:, 0:ncol], in0=xt[:, c, :], scalar=coef[:, ai, c:c + 1], in1=rstd[:, 0:ncol],
                    op0=ALU.mult, op1=ALU.mult), reads=[cx.r_xt, r_coef, r_rstd], writes=[rt])
                S.op("scalar", lambda e, c=c, t=t: e.activation(
                    out=hb[:, c, :], in_=t[:, 0:ncol], func=AF.Identity, bias=modt[:, bi * 8 + c:bi * 8 + c + 1], scale=1.0),
                    reads=[rt, r_mod], writes=[cx.r_hb])

        def post_norm_residual(cx, gi, x_dst, r_dst, col0, scratch, xt, r_xt):
            ncol = cx.ncol
            mo = cx.mo
            sumsq_rstd(cx, mo[:, :, :], cx.r_mo, scratch)
            for c in range(8):
                t, rt = tmpA.next()
                S.op("vector", lambda e, c=c, t=t: e.scalar_tensor_tensor(
                    out=t[:, 0:ncol], in0=mo[:, c, :], scalar=coef[:, gi, c:c + 1], in1=rstd[:, 0:ncol],
                    op0=ALU.mult, op1=ALU.mult), reads=[cx.r_mo, r_coef, r_rstd], writes=[rt])
                S.op("gpsimd", lambda e, c=c, t=t: e.tensor_tensor(
                    out=xt[:, c, :], in0=xt[:, c, :], in1=t[:, 0:ncol], op=ALU.add),
                    reads=[rt, r_xt], writes=[r_xt])
            dst = x_dst.rearrange("(c p) t -> p c t", p=128)[:, :, col0:col0 + ncol]
            S.dma("sync", lambda e: e.dma_start(out=dst, in_=xt[:, :, :]), reads=[r_xt], writes=[r_dst])

        def issue_x(cx, x_src, r_src, t):
            src = x_src.rearrange("(c p) t -> p c t", p=128)[:, :, t * cx.ncol:(t + 1) * cx.ncol]
            dst, rd = cx.xts[t % 2], cx.r_xts[t % 2]
            S.dma("sync", lambda e: e.dma_start(out=dst[:, :, :], in_=src), reads=[r_src], writes=[rd])

        def load_x(cx, x_src, r_src, T, ntiles, manual=False):
            if not manual:
                if T == 0:
                    issue_x(cx, x_src, r_src, 0)
                if T + 1 < ntiles:
                    issue_x(cx, x_src, r_src, T + 1)
            cx.xt, cx.r_xt = cx.xts[T % 2], cx.r_xts[T % 2]

        def mm_group(p, rp, ncol, lhs_fn, rhs_fn, nk, reads):
            for fi in range(nk):
                rd = reads(fi) if callable(reads) else reads
                S.op("tensor", lambda e, fi=fi: e.matmul(p[:, 0:ncol], lhs_fn(fi), rhs_fn(fi), start=(fi == 0), stop=(fi == nk - 1)),
                     reads=rd, writes=[rp])

        def stage_mod(l, pre_A=None):
            barrier()
            S.dma("sync", lambda e: e.dma_start(out=badat[:], in_=b_ada[l]), writes=[r_bada])
            S.dma("sync", lambda e: e.dma_start(out=gvt[:], in_=gvec[l]), writes=[r_gvt])
            S.dma("sync", lambda e: e.dma_start(out=cwt[:], in_=convw[l]), writes=[r_cwt])
            pm, rpm = PS.next()
            wv = wv3(20480, 8, 1536)
            rws = [Res() for _ in range(4)]
            src = w_ada[l].rearrange("(k p) n -> p k n", p=128)
            for g in range(4):
                S.dma("gpsimd", lambda e, g=g: e.dma_start(out=wv[:, 2 * g:2 * g + 2, :], in_=src[:, 2 * g:2 * g + 2, :]), writes=[rws[g]])
            for fo in range(12):
                for fi in range(8):
                    S.op("tensor", lambda e, fo=fo, fi=fi: e.matmul(
                        pm[:, fo:fo + 1], wv[:, fi, fo * 128:(fo + 1) * 128], cbf[:, fi:fi + 1],
                        start=(fi == 0), stop=(fi == 7)), reads=[rws[fi // 2], r_cbf], writes=[rpm])
            mpart = tmpA.tiles[0]; r_mpart = tmpA.res[0]
            S.op("vector", lambda e: e.tensor_tensor(out=mpart[:, 0:12], in0=pm[:, 0:12], in1=badat[:], op=ALU.add),
                 reads=[rpm, r_bada], writes=[r_mpart])
            S.dma("sync", lambda e: e.dma_start(out=modc, in_=mpart[:, 0:12]), reads=[r_mpart], writes=[r_modc])
            S.coll(lambda e: e.collective_compute("AllGather", ALU.bypass, CC_GROUPS, [modc.opt()], [modg.opt()]),
                   reads=[r_modc], writes=[r_modg], blocking=False)
            if pre_A is not None:
                pre_A()
            S.dma("sync", lambda e: e.dma_start(out=modt[:, :].rearrange("p (r f) -> p r f", r=4),
                                                in_=modg.rearrange("(r p) f -> p r f", p=128)), reads=[r_modg], writes=[r_mod])
            S.op("vector", lambda e: e.scalar_tensor_tensor(out=coef[:, 0, :], in0=modt[:, 8:16], scalar=1.0, in1=gvt[:, 0, :],
                                                            op0=ALU.add, op1=ALU.mult), reads=[r_mod, r_gvt], writes=[r_coef])
            S.op("vector", lambda e: e.tensor_tensor(out=coef[:, 1, :], in0=modt[:, 16:24], in1=gvt[:, 1, :], op=ALU.mult),
                 reads=[r_mod, r_gvt], writes=[r_coef])
            S.op("vector", lambda e: e.scalar_tensor_tensor(out=coef[:, 2, :], in0=modt[:, 32:40], scalar=1.0, in1=gvt[:, 2, :],
                                                            op0=ALU.add, op1=ALU.mult), reads=[r_mod, r_gvt], writes=[r_coef])
            S.op("vector", lambda e: e.tensor_tensor(out=coef[:, 3, :], in0=modt[:, 40:48], in1=gvt[:, 3, :], op=ALU.mult),
                 reads=[r_mod, r_gvt], writes=[r_coef])

        def make_pre_A(l, x_src, r_src):
            P = Ctx()
            P.wqkv = wv3(0, 8, 1536); P.rwq = Res()
            P.wcu = wv3(12288, 8, 1024); P.rwc = Res()
            P.xt1 = CM[:, 8192:16384].bitcast(F32).rearrange("p (c t) -> p c t", c=8); P.r_x1 = Res()
            P.xt0 = af32(ATT0, 4096).rearrange("p (c t) -> p c t", c=8); P.r_x0 = Res()
            P.cxs = Ctx(); P.cxs.ncol = 512
            P.cxs.xts = [P.xt1, P.xt0]; P.cxs.r_xts = [P.r_x1, P.r_x0]

            def go():
                load_w(P.wcu, w_in[l][:, 512:1536], P.rwc, 8)
                load_w(P.wqkv, w_in[l][:, 1536:3072], P.rwq, 8)
                issue_x(P.cxs, x_src, r_src, 0)
                issue_x(P.cxs, x_src, r_src, 1)
            P.go = go
            return P

        def stage_A(l, x_src, r_src, do_gather, P):
            barrier()
            wqkv, rwq, wcu, rwc = P.wqkv, P.rwq, P.wcu, P.rwc
            xt1, r_x1 = P.xt1, P.r_x1
            hbT = [abf(20480 + T * 4096, 4096).rearrange("p (c t) -> p c t", c=8) for T in range(4)]
            r_hbT = [Res() for _ in range(4)]
            stg = Ring([abf(36864 + i * 1536, 1536).rearrange("p (c t) -> p c t", c=3) for i in range(4)])
            assert 36864 + 4 * 1536 <= ATT0
            conH = contrib.rearrange("(h s) c -> s h c", h=4)
            xt0, r_x0 = P.xt0, P.r_x0
            ATB.alias = [r_x0]
            cxs = P.cxs
            for T in range(4):
                cx = Ctx(); cx.ncol = 512
                cx.xts = cxs.xts; cx.r_xts = cxs.r_xts; cx.xt, cx.r_xt = cxs.xts[T % 2], cxs.r_xts[T % 2]
                cx.hb, cx.r_hb = hbT[T], r_hbT[T]
                norm_mod(cx, 0, 0)
                if T + 2 < 4:
                    issue_x(cxs, x_src, r_src, T + 2)
                hb = cx.hb
                p, rp = PS.next()
                hview = hb.rearrange("p c (b t) -> p c b t", t=128)
                for fo in range(8):
                    for fi in range(8):
                        S.op("tensor", lambda e, fo=fo, fi=fi, p=p, hview=hview: e.matmul(
                            p[:, fo * 8:(fo + 1) * 8].rearrange("p (b t) -> p b t", t=2), wcu[:, fi, fo * 128:(fo + 1) * 128],
                            hview[:, fi, :, 126:128], start=(fi == 0), stop=(fi == 7)), reads=[rwc, cx.r_hb], writes=[rp])
                S.op("scalar", lambda e, p=p: e.activation(out=hh[:, :, :].rearrange("p a b -> p (a b)"), in_=p[:, 0:64], func=AF.Identity),
                     reads=[rp], writes=[r_hh])
                S.op("vector", lambda e, T=T: e.tensor_tensor(
                    out=halo_acc[:, :, T * 4:(T + 1) * 4, :], in0=hh[:, 0:4, :].rearrange("p c (b t) -> p c b t", t=2),
                    in1=hh[:, 4:8, :].rearrange("p c (b t) -> p c b t", t=2), op=ALU.mult), reads=[r_hh], writes=[r_halo])
            S.dma("sync", lambda e: e.dma_start(out=hcon, in_=halo_acc[:, :, :, :].rearrange("p c i t -> p (c i t)")),
                  reads=[r_halo], writes=[r_hcon])
            for hp in range(4):
                for T in range(4):
                    hb, r_hb = hbT[T], r_hbT[T]
                    sg, rsg = stg.next()
                    for which in range(2):
                        fo = which * 4 + hp
                        p, rp = PS.next()
                        mm_group(p, rp, 512, lambda fi, fo=fo: wqkv[:, fi, fo * 128:(fo + 1) * 128], lambda fi, hb=hb: hb[:, fi, :], 8, [rwq, r_hb])
                        evac(which, sg[:, which, :], p[:, :], [rp], [rsg])
                    p, rp = PS.next()
                    for blk in range(4):
                        for fi in range(8):
                            S.op("tensor", lambda e, blk=blk, fi=fi, p=p, hb=hb, hp=hp: e.matmul(
                                p[:, blk * 128:(blk + 1) * 128], hb[:, fi, blk * 128:(blk + 1) * 128],
                                wqkv[:, fi, 1024 + hp * 128:1024 + (hp + 1) * 128], start=(fi == 0), stop=(fi == 7)),
                                reads=[rwq, r_hb], writes=[rp])
                    evac(1, sg[:, 2, :], p[:, :], [rp], [rsg])
                    S.dma("sync", lambda e, T=T, sg=sg, hp=hp: e.dma_start(out=qTd_w[hp, :, T * 512:(T + 1) * 512], in_=sg[:, 0, :]),
                          reads=[rsg], writes=[r_qTdh[hp]])
                    S.dma("sync", lambda e, T=T, sg=sg, hp=hp: e.dma_start(out=conH[0:128, hp, T * 512:(T + 1) * 512], in_=sg[:, 1, :]),
                          reads=[rsg], writes=[r_chp[hp], r_contrib])
                    S.dma("sync", lambda e, T=T, sg=sg, hp=hp: e.dma_start(
                        out=conH[128:256, hp, T * 512:(T + 1) * 512], in_=sg[:, 2, :]),
                        reads=[rsg], writes=[r_chp[hp], r_contrib])
                if do_gather and hp == 0:
                    issue_gather(0)
                if do_gather and hp == 1:
                    attn_prefetch(qTd_w, "gpsimd")
            if do_gather:
                S.coll(lambda e: e.collective_compute("AllGather", ALU.bypass, CC_GROUPS, [hcon.opt()], [hgath.opt()]),
                       reads=[r_hcon], writes=[r_hgath], blocking=False)

        def load_post_weights(l):
            W = Ctx()
            W.wbcu = wv3(0, 8, 1536); W.wgt = wv3(12288, 8, 2048)
            W.wpc = wv3(28672, 4, 1024); W.wpa = wv3(32768, 4, 1024); W.wo = wv3(36864, 8, 1024)
            W.rw0, W.rw1, W.rw2, W.rw3 = Res(), Res(), Res(), Res()
            todo = []

            def add(dst_view, src_rows_ap, rw, kchunks):
                src = src_rows_ap.rearrange("(k p) n -> p k n", p=128)
                step = max(1, kchunks // 4)
                for k0 in range(0, kchunks, step):
                    k1 = min(kchunks, k0 + step)
                    todo.append(lambda k0=k0, k1=k1: S.dma(
                        "gpsimd", lambda e: e.dma_start(out=dst_view[:, k0:k1, :], in_=src[:, k0:k1, :]), writes=[rw]))
            add(W.wbcu, w_in[l][:, 0:1536], W.rw0, 8)
            add(W.wgt, w_in[l][:, 3072:5120], W.rw1, 8)
            add(W.wpc, w_pc[l], W.rw2, 4)
            add(W.wpa, w_pa[l], W.rw2, 4)
            add(W.wo, w_out[l], W.rw3, 8)
            return W, todo

        ATB = Ctx()
        ATB.KT = abf(ATT0, 8192).rearrange("p (r t) -> p r t", r=4); ATB.r_KT = Res()
        ATB.Vs = [abf(ATT0 + 8192 + i * 8192, 8192).rearrange("p (r i c) -> p r i c", r=4, i=16) for i in range(2)]
        ATB.r_V = [Res(), Res()]
        ATB.maskt = abf(ATT0 + 24576, 8192).rearrange("p (k t) -> p k t", k=16); ATB.r_mask = Res()
        ATB.qpt = abf(ATT0 + 32768, 2048); ATB.r_qpt = Res()
        ATB.r_KTp = [Res() for _ in range(4)]
        ATB.r_qp = [Res() for _ in range(4)]

        def attn_load_V(hp, q="sync"):
            G4 = gath.rearrange("(h r s) c -> h r s c", h=4, r=4)
            pb = hp % 2
            for r in range(4):
                S.dma(q, lambda e, r=r: e.dma_start(
                    out=ATB.Vs[pb][:, r, :, :], in_=G4[hp, r, 128:256, :].rearrange("p (i c) -> p i c", c=128)),
                    reads=[r_gath, r_gaths[hp]], writes=[ATB.r_V[pb]])

        def attn_load_KTpart(hp, j, q="sync"):
            G4 = gath.rearrange("(h r s) c -> h r s c", h=4, r=4)
            S.dma(q, lambda e: e.dma_start(
                out=ATB.KT[:, :, j * 512:(j + 1) * 512], in_=G4[hp, :, 0:128, j * 512:(j + 1) * 512].rearrange("r p t -> p r t")),
                reads=[r_gath, r_gaths[hp]], writes=[ATB.r_KTp[j]] + getattr(ATB, "alias", []))

        def attn_load_qpart(hp, j, qsrc, q="sync"):
            S.dma(q, lambda e: e.dma_start(out=ATB.qpt[:, j * 512:(j + 1) * 512], in_=qsrc[hp][:, j * 512:(j + 1) * 512]),
                  reads=[r_qTd, r_qTdh[hp]], writes=[ATB.r_qp[j]])

        def attn_load_KQ(hp, qsrc, q="sync"):
            for j in range(4):
                attn_load_KTpart(hp, j, q)
                attn_load_qpart(hp, j, qsrc, q)

        def attn_prefetch(qsrc, q="sync"):
            S.dma(q, lambda e: e.dma_start(out=ATB.maskt, in_=maskd), writes=[ATB.r_mask])
            attn_load_KQ(0, qsrc, q)
            attn_load_V(0, q)
            ATB.prefetched = True

        def stage_attn(l, qsrc, fusedA):
            barrier()
            W, wtodo = load_post_weights(l)

            G4 = gath.rearrange("(h r s) c -> h r s c", h=4, r=4)

            KT, r_KT, Vs, r_V = ATB.KT, ATB.r_KT, ATB.Vs, ATB.r_V
            maskt, r_mask, qpt, r_qpt = ATB.maskt, ATB.r_mask, ATB.qpt, ATB.r_qpt
            ost = [abf(ATT0 + 34816 + i * 2048, 2048) for i in range(2)]; r_ost = [Res(), Res()]
            assert ATT0 + 38912 <= ARN
            if not getattr(ATB, "prefetched", False):
                attn_prefetch(qsrc)
            ATB.prefetched = False
            load_V = attn_load_V

            def load_KQ(hp):
                attn_load_KQ(hp, qsrc)

            steps = []
            for hp in range(4):
                for I in range(4):
                    kbs = list(range(16 * I + 15, -1, -1))
                    for n, kb in enumerate(kbs):
                        band = kb >= 16 * I
                        n0 = 128 * ((kb - 16 * I) // 4) if band else 0
                        steps.append(dict(hp=hp, I=I, kb=kb, band=band, n0=n0, first=(n == 0), last=(n == len(kbs) - 1)))

            def ps2(b):
                return PSALL[:, b * 512:(b + 2) * 512].rearrange("p (e t) -> p e t", e=2)

            ZP = Ring([ps2(0), ps2(2)])
            CP = Ring([ps2(4)])
            OB = ps2(6); r_OB = Res()
            Ering = Ring([c2(i) for i in range(0, 4)])
            SPring = Ring([c2(i) for i in range(4, 8)])
            Aring = Ring([c2(i) for i in range(8, 12)])
            Dring = Ring([c2(i) for i in range(12, 14)])
            ACring = Ring([c2(i) for i in range(14, 16)])
            st = {}
            cur_acc = [None]

            def Zpart(k):
                s = steps[k]
                hp, I, kb, n0 = s["hp"], s["I"], s["kb"], s["n0"]
                if I == 1 and s["first"] and hp + 1 < 4:
                    if fusedA:
                        issue_gather(hp + 1)
                    load_V(hp + 1)
                z, rz = ZP.next()
                r, i = kb % 4, kb // 4
                for e_ in range(2):
                    S.op("tensor", lambda e, e_=e_: e.matmul(
                        z[:, e_, n0:512], KT[64 * e_:64 * e_ + 64, r, i * 128:(i + 1) * 128],
                        qpt[64 * e_:64 * e_ + 64, I * 512 + n0:(I + 1) * 512], start=True, stop=True),
                        reads=[ATB.r_KTp[kb // 16], ATB.r_qp[I]], writes=[rz])
                if I == 3 and hp + 1 < 4:
                    if s["first"]:
                        for j in range(3):
                            attn_load_qpart(hp + 1, j, qsrc)
                    if kb % 16 == 0:
                        attn_load_KTpart(hp + 1, kb // 16)
                    if s["last"]:
                        attn_load_qpart(hp + 1, 3, qsrc)
                E, rE = Ering.next()
                S.op("scalar", lambda e: e.activation(out=E[:, :, n0:512], in_=z[:, :, n0:512], func=AF.Exp, scale=0.125),
                     reads=[rz], writes=[rE])
                if s["band"]:
                    kr = kb - 16 * I
                    for e_ in range(2):
                        S.op("vector", lambda e, e_=e_: e.tensor_tensor(out=E[:, e_, n0:512], in0=E[:, e_, n0:512],
                                                                         in1=maskt[:, kr, n0:512], op=ALU.mult),
                             reads=[rE, r_mask], writes=[rE])
                st[k] = dict(E=E, rE=rE)

            def SPpart(k):
                s = steps[k]; b = st[k]; n0 = s["n0"]
                SP, rSP = SPring.next()
                S.op("scalar", lambda e: e.activation(out=SP[:, :, n0:512], in_=b["E"][:, :, n0:512], func=AF.Ln, bias=1.0, scale=1.0),
                     reads=[b["rE"]], writes=[rSP])
                b["SP"] = SP; b["rSP"] = rSP

            def Cpart(k):
                s = steps[k]; b = st[k]; n0 = s["n0"]; first = s["first"]
                c, rc = CP.next()
                if first:
                    acc, racc = ACring.next()
                    S.op("gpsimd", lambda e: e.memset(acc[:, :, :], 0.0), writes=[racc])
                    cur_acc[0] = (acc, racc)
                acc, racc = cur_acc[0]
                for e_ in range(2):
                    S.op("tensor", lambda e, e_=e_: e.matmul(c[:, e_, n0:512], tri[:], b["SP"][:, e_, n0:512], start=True, stop=first),
                         reads=[r_tri, b["rSP"]], writes=[rc])
                    if not first:
                        S.op("tensor", lambda e, e_=e_: e.matmul(c[:, e_, n0:512], ones[:], acc[:, e_, n0:512], start=False, stop=True),
                             reads=[r_ones, racc], writes=[rc])
                Dt, rD = Dring.next()
                S.op("scalar", lambda e: e.activation(out=Dt[:, :, n0:512], in_=c[:, :, n0:512], func=AF.Exp, scale=-1.0),
                     reads=[rc], writes=[rD])
                b["D"] = Dt; b["rD"] = rD

            def Rest(k):
                s = steps[k]; b = st[k]; n0 = s["n0"]
                acc, racc = cur_acc[0]
                if not s["last"]:
                    nacc, rnacc = ACring.next()
                    if n0 > 0:
                        S.op("gpsimd", lambda e: e.memset(nacc[:, :, 0:n0], 0.0), writes=[rnacc])
                    S.op("gpsimd", lambda e: e.tensor_tensor(out=nacc[:, :, n0:512], in0=acc[:, :, n0:512], in1=b["SP"][:, :, n0:512],
                                                             op=ALU.add), reads=[racc, b["rSP"]], writes=[rnacc])
                    cur_acc[0] = (nacc, rnacc)
                A, rA = Aring.next()
                S.op("vector", lambda e: e.tensor_tensor(out=A[:, :, n0:512], in0=b["E"][:, :, n0:512], in1=b["D"][:, :, n0:512], op=ALU.mult),
                     reads=[b["rE"], b["rD"]], writes=[rA])
                b["A"] = A; b["rA"] = rA

            def AVpart(k):
                s = steps[k]; b = st.pop(k); n0 = s["n0"]; hp = s["hp"]; pb = hp % 2
                kb = s["kb"]; r, i = kb % 4, kb // 4
                if s["first"]:
                    for e_ in range(2):
                        S.op("tensor", lambda e, e_=e_: e.matmul(OB[:, e_, :], zerosw[:], maskt[:, 0, :], start=True, stop=False),
                             reads=[r_zw, r_mask], writes=[r_OB])
                last = s["last"]
                for e_ in range(2):
                    S.op("tensor", lambda e, e_=e_: e.matmul(OB[:, e_, n0:512], Vs[pb][:, r, i, :], b["A"][:, e_, n0:512], start=False, stop=last),
                         reads=[r_V[pb], b["rA"]], writes=[r_OB])
                if last:
                    I = s["I"]
                    for e_ in range(2):
                        S.op("vector", lambda e, e_=e_: e.tensor_copy(
                            out=ost[pb][64 * e_:64 * e_ + 64, I * 512:(I + 1) * 512], in_=OB[64 * e_:64 * e_ + 64, e_, :]),
                            reads=[r_OB], writes=[r_ost[pb]])
                    if I == 3:
                        S.dma("sync", lambda e: e.dma_start(out=oTd[hp], in_=ost[pb]), reads=[r_ost[pb]], writes=[r_oTd])

            n = len(steps)
            Zpart(0); SPpart(0); Zpart(1); SPpart(1)
            for k in range(n):
                if k >= 48 and k % 8 == 0 and wtodo:
                    wtodo.pop(0)()
                if k + 2 < n:
                    Zpart(k + 2)
                Cpart(k)
                if k + 2 < n:
                    SPpart(k + 2)
                Rest(k)
                if k >= 2:
                    AVpart(k - 2)
            AVpart(n - 2); AVpart(n - 1)
            while wtodo:
                wtodo.pop(0)()
            return W

        def new_W1P():
            P = Ctx(); P.rw1 = [Res() for _ in range(8)]; P.issued = set()
            P.rw2 = [Res() for _ in range(8)]; P.issued2 = set()
            return P

        def issue_W2(l, g, dead_res, P=None):
            P = P or cur_W1P[0]
            dst = wv3(32768 + g * 4096, 4, 1024)
            S.dma("gpsimd", lambda e: e.dma_start(
                out=dst, in_=w2[l][g * 512:(g + 1) * 512, :].rearrange("(k p) n -> p k n", p=128)),
                writes=[P.rw2[g]] + list(dead_res))
            P.issued2.add(g)

        def issue_W1(l, g, dead_res, P=None):
            P = P or cur_W1P[0]
            dst = wv3(g * 4096, 8, 512)
            S.dma("gpsimd", lambda e: e.dma_start(
                out=dst, in_=w1[l][:, g * 512:(g + 1) * 512].rearrange("(k p) n -> p k n", p=128)),
                writes=[P.rw1[g]] + list(dead_res))
            P.issued.add(g)

        cur_W1P = [None]

        def stage_post(l, x_src, r_src, x_dst, r_dst, W):
            barrier()
            cur_W1P[0] = new_W1P()
            wbcu, wgt, wpc, wpa, wo = W.wbcu, W.wgt, W.wpc, W.wpa, W.wo
            rw0, rw1, rw2, rw3 = W.rw0, W.rw1, W.rw2, W.rw3
            cx = mk_ctx(45056, 512)
            hb, mo = cx.hb, cx.mo
            off = cx.end
            vcp = af32(off, 4160).rearrange("p (c b t) -> p c b t", c=4, b=4); r_vcp = Res(); off += 8320
            bgt = af32(off, 2048).rearrange("p (c t) -> p c t", c=4); r_bgt = Res(); off += 4096
            ycv = abf(off, 2048).rearrange("p (c t) -> p c t", c=4); r_ycv = Res(); off += 2048
            otile = abf(off, 2048).rearrange("p (c t) -> p c t", c=4); r_otile = Res(); off += 2048
            mrg = abf(off, 4096).rearrange("p (c t) -> p c t", c=8); r_mrg = Res(); off += 4096
            assert off <= ARN
            S.dma("sync", lambda e: e.dma_start(out=hg[:], in_=hgath.rearrange("(r p) n -> p r n", p=128)),
                  reads=[r_hgath], writes=[r_hg])
            hgv = hg[:, :, :].rearrange("p r (c i t) -> p r c i t", c=4, i=16)
            S.op("vector", lambda e: e.tensor_scalar(out=hsel[:], in0=hgv[:, 0], scalar1=wsel[:, 0:1], scalar2=None, op0=ALU.mult),
                 reads=[r_hg, r_wsel], writes=[r_hsel])
            for r in (1, 2):
                S.op("vector", lambda e, r=r: e.scalar_tensor_tensor(out=hsel[:], in0=hgv[:, r], scalar=wsel[:, r:r + 1], in1=hsel[:],
                                                                     op0=ALU.mult, op1=ALU.add), reads=[r_hg, r_wsel, r_hsel], writes=[r_hsel])
            S.op("vector", lambda e: e.scalar_tensor_tensor(out=hsel[:, :, 1:16, :], in0=hgv[:, 3, :, 0:15, :], scalar=wsel[:, 3:4],
                                                            in1=hsel[:, :, 1:16, :], op0=ALU.mult, op1=ALU.add),
                 reads=[r_hg, r_wsel, r_hsel], writes=[r_hsel])
            oTv = oTd.rearrange("h p t -> p h t")
            def prologue(T):
                load_x(cx, x_src, r_src, T, 4, manual=True)
                norm_mod(cx, 0, 0)

            issue_x(cx, x_src, r_src, 0)
            issue_x(cx, x_src, r_src, 1)
            prologue(0)
            for T in range(4):
                xt_T, r_xt_T = cx.xt, cx.r_xt
                S.dma("sync", lambda e, T=T: e.dma_start(out=otile, in_=oTv[:, :, T * 512:(T + 1) * 512]),
                      reads=[r_oTd], writes=[r_otile])
                S.op("gpsimd", lambda e, T=T: e.tensor_copy(out=vcp[:, :, :, 0:2], in_=hsel[:, :, T * 4:(T + 1) * 4, :]),
                     reads=[r_hsel], writes=[r_vcp])
                for c in range(4):
                    for which in (1, 2, 0):
                        fo = which * 4 + c
                        p, rp = PS.next()
                        mm_group(p, rp, 512, lambda fi, fo=fo: wbcu[:, fi, fo * 128:(fo + 1) * 128], lambda fi: hb[:, fi, :], 8, [rw0, cx.r_hb])
                        if which == 1:
                            S.op("scalar", lambda e, c=c, p=p: e.activation(out=mo[:, c, :], in_=p[:, :], func=AF.Identity),
                                 reads=[rp], writes=[cx.r_mo])
                        elif which == 2:
                            S.op("vector", lambda e, c=c, p=p: e.tensor_tensor(
                                out=vcp[:, c, :, 2:130], in0=mo[:, c, :].rearrange("p (b t) -> p b t", t=128),
                                in1=p[:, :].rearrange("p (b t) -> p b t", t=128), op=ALU.mult),
                                reads=[rp, cx.r_mo], writes=[r_vcp])
                        else:
                            S.op("scalar", lambda e, c=c, p=p: e.activation(out=bgt[:, c, :], in_=p[:, :], func=AF.Identity),
                                 reads=[rp], writes=[r_bgt])
                    t, rt = tmpA.next()
                    tv = t[:, :].rearrange("p (b t) -> p b t", t=128)
                    S.op("scalar", lambda e, c=c, tv=tv: e.activation(out=tv, in_=vcp[:, c, :, 2:130], func=AF.Identity, scale=cwt[:, 2, c:c + 1]),
                         reads=[r_vcp, r_cwt], writes=[rt])
                    S.op("vector", lambda e, c=c, tv=tv: e.scalar_tensor_tensor(out=tv, in0=vcp[:, c, :, 1:129], scalar=cwt[:, 1, c:c + 1], in1=tv,
                                                                                op0=ALU.mult, op1=ALU.add), reads=[r_vcp, r_cwt, rt], writes=[rt])
                    S.op("vector", lambda e, c=c, tv=tv: e.scalar_tensor_tensor(out=tv, in0=vcp[:, c, :, 0:128], scalar=cwt[:, 0, c:c + 1], in1=tv,
                                                                                op0=ALU.mult, op1=ALU.add), reads=[r_vcp, r_cwt, rt], writes=[rt])
                    S.op("vector", lambda e, c=c, t=t: e.tensor_tensor(out=ycv[:, c, :], in0=t[:, :], in1=bgt[:, c, :], op=ALU.mult),
                         reads=[rt, r_bgt], writes=[r_ycv])
                if T == 3:
                    for g in range(3):
                        issue_W1(l, g, [rw0])
                for fo in range(8):
                    pya, rya = PS.next()
                    mm_group(pya, rya, 512, lambda ci, fo=fo: wpa[:, ci, fo * 128:(fo + 1) * 128], lambda ci: otile[:, ci, :], 4, [rw2, r_otile])
                    pga, rga = PS.next()
                    mm_group(pga, rga, 512, lambda fi, fo=fo: wgt[:, fi, fo * 128:(fo + 1) * 128], lambda fi: hb[:, fi, :], 8, [rw1, cx.r_hb])
                    pgb, rgb = PS.next()
                    mm_group(pgb, rgb, 512, lambda fi, fo=fo: wgt[:, fi, 1024 + fo * 128:1024 + (fo + 1) * 128], lambda fi: hb[:, fi, :], 8,
                             [rw1, cx.r_hb])
                    pyc, ryc = PS.next()
                    mm_group(pyc, ryc, 512, lambda ci, fo=fo: wpc[:, ci, fo * 128:(fo + 1) * 128], lambda ci: ycv[:, ci, :], 4, [rw2, r_ycv])
                    sa, rsa = tmpB.next()
                    S.op("scalar", lambda e, p=pga, sa=sa: e.activation(out=sa[:, :], in_=p[:, :], func=AF.Sigmoid), reads=[rga], writes=[rsa])
                    sg, rsg = tmpB.next()
                    S.op("scalar", lambda e, p=pgb, sg=sg: e.activation(out=sg[:, :], in_=p[:, :], func=AF.Sigmoid), reads=[rgb], writes=[rsg])
                    S.op("vector", lambda e, p=pya, sg=sg: e.tensor_tensor(out=sg[:, :], in0=sg[:, :], in1=p[:, :], op=ALU.mult),
                         reads=[rsg, rya], writes=[rsg])
                    S.op("vector", lambda e, p=pyc, sa=sa: e.tensor_tensor(out=sa[:, :], in0=sa[:, :], in1=p[:, :], op=ALU.mult),
                         reads=[rsa, ryc], writes=[rsa])
                    S.op("gpsimd", lambda e, fo=fo, sa=sa, sg=sg: e.tensor_tensor(out=mrg[:, fo, :], in0=sa[:, :], in1=sg[:, :], op=ALU.add),
                         reads=[rsa, rsg], writes=[r_mrg])
                if T == 3:
                    for g in range(3, 7):
                        issue_W1(l, g, [rw1])
                    issue_W1(l, 7, [rw2])
                    issue_W2(l, 0, [rw2])
                for fo in range(8):
                    if fo == 2 and T + 1 < 4:
                        prologue(T + 1)
                    p, rp = PS.next()
                    mm_group(p, rp, 512, lambda fi, fo=fo: wo[:, fi, fo * 128:(fo + 1) * 128], lambda fi: mrg[:, fi, :], 8, [rw3, r_mrg])
                    evac(fo, mo[:, fo, :], p[:, :], [rp], [cx.r_mo])
                if T == 3:
                    issue_W2(l, 1, [rw3])
                    issue_W2(l, 2, [rw3])
                post_norm_residual(cx, 1, x_dst, r_dst, T * 512, (mrg, r_mrg), xt_T, r_xt_T)
                if T + 2 < 4:
                    issue_x(cx, x_src, r_src, T + 2)

        def stage_mlp(l, x_src, r_src, x_dst, r_dst, W1P):
            barrier()
            W2v = wv3(32768, 32, 1024)
            W1c = [wv3(g * 4096, 8, 512) for g in range(8)]
            rw1 = W1P.rw1
            rw2 = W1P.rw2
            for g in range(8):
                if g not in W1P.issued:
                    issue_W1(l, g, [])
            for g in range(8):
                if g not in W1P.issued2:
                    issue_W2(l, g, [], W1P)
            NC_ = 256
            cx = mk_ctx(65536, NC_)
            hb, mo = cx.hb, cx.mo
            ff1 = abf(cx.end, 32 * NC_).rearrange("p (k n) -> p k n", k=32); r_ff1 = Res()
            assert cx.end + 32 * NC_ <= ARN
            def prologue(T):
                load_x(cx, x_src, r_src, T, NT // NC_, manual=True)
                norm_mod(cx, 2, 3)

            issue_x(cx, x_src, r_src, 0)
            issue_x(cx, x_src, r_src, 1)
            prologue(0)
            for T in range(NT // NC_):
                xt_T, r_xt_T = cx.xt, cx.r_xt
                for fo in range(32):
                    p, rp = PS.next()
                    mm_group(p, rp, NC_, lambda fi, fo=fo: W1c[fo // 4][:, fi, (fo % 4) * 128:(fo % 4 + 1) * 128], lambda fi: hb[:, fi, :], 8, [rw1[fo // 4], cx.r_hb])
                    t, rt = tmpB.next()
                    S.op("scalar", lambda e, p=p, t=t: e.activation(out=t[:, 0:NC_], in_=p[:, 0:NC_], func=AF.Relu), reads=[rp], writes=[rt])
                    eng = "vector" if fo % 2 == 0 else "gpsimd"
                    S.op(eng, lambda e, fo=fo, t=t: e.tensor_tensor(out=ff1[:, fo, :], in0=t[:, 0:NC_], in1=t[:, 0:NC_], op=ALU.mult),
                         reads=[rt], writes=[r_ff1])
                for fo in range(8):
                    if fo == 3 and T + 1 < NT // NC_:
                        prologue(T + 1)
                    p, rp = PS.next()
                    mm_group(p, rp, NC_, lambda fi, fo=fo: W2v[:, fi, fo * 128:(fo + 1) * 128], lambda fi: ff1[:, fi, :], 32, lambda fi: [rw2[fi // 4], r_ff1])
                    evac(fo, mo[:, fo, :], p[:, 0:NC_], [rp], [cx.r_mo])
                post_norm_residual(cx, 3, x_dst, r_dst, T * NC_, (ff1[:, 0:8, :], r_ff1), xt_T, r_xt_T)
                if T + 2 < NT // NC_:
                    issue_x(cx, x_src, r_src, T + 2)

        cur_x, r_cur = x_in, r_xin
        nB_done = 0
        last_mod = None
        finals = []
        for (kind, l) in parts:
            P = make_pre_A(l, cur_x, r_cur) if kind == "A" else None
            if last_mod != l:
                stage_mod(l, P.go if P is not None else None)
                last_mod = l
            elif P is not None:
                P.go()
            if kind == "A":
                stage_A(l, cur_x, r_cur, ("B", l) in parts, P)
                finals += [r_contrib, r_hcon, r_qTd] + r_qTdh
            else:
                fusedA = ("A", l) in parts
                qsrc = qTd_w if fusedA else qTd_r
                W = stage_attn(l, qsrc, fusedA)
                stage_post(l, cur_x, r_cur, xmid, r_xmid, W)
                nB_done += 1
                if n_B > 1 and nB_done < n_B:
                    dst, rdst = xl0, r_xl0
                else:
                    dst, rdst = x_out, r_xout
                stage_mlp(l, xmid, r_xmid, dst, rdst, cur_W1P[0])
                cur_x, r_cur = dst, rdst
                finals.append(rdst)
        S.final_wait("sync", finals)
        S.emit(block)
    return nc


_PROG_CACHE = {}


def _prog(parts):
    key = tuple(parts)
    if key not in _PROG_CACHE:
        _PROG_CACHE[key] = build(list(parts))
    return _PROG_CACHE[key]


def _core_tokens(j):
    return [4 * i + j for i in range(NB)]


def _prep_static(c, w_ada, b_ada, g_pre_mix, g_post_mix, g_pre_mlp, g_post_mlp, w_in, conv_w,
                 w_proj_conv, w_proj_attn, w_out, w_mlp_in, w_mlp_out):
    f = lambda a: np.ascontiguousarray(np.asarray(a, dtype=np.float32))
    shared = {
        "gvec": f(np.stack([np.asarray(g).reshape(2, 8, 128) for g in (g_pre_mix, g_post_mix, g_pre_mlp, g_post_mlp)], axis=1)
                  .transpose(0, 3, 1, 2)),
        "w_in": f(w_in),
        "convw": f(np.asarray(conv_w).reshape(2, 3, 4, 128).transpose(0, 3, 1, 2)),
        "w_pc": f(w_proj_conv), "w_pa": f(w_proj_attn), "w_out": f(w_out),
        "w1": f(w_mlp_in), "w2": f(w_mlp_out),
        "tri": (np.arange(128)[:, None] >= np.arange(128)[None, :]).astype(ml_dtypes.bfloat16),
    }
    per_core = []
    b_ada_l = np.asarray(b_ada).reshape(2, 48, 128).transpose(0, 2, 1)
    s_idx = np.arange(128)[:, None]
    t_idx = np.arange(128)[None, :]
    for core in range(NCORES):
        b, j = core // 4, core % 4
        mask = np.zeros((128, 16, 4, 128), dtype=np.float32)
        for kr in range(16):
            for m in range(4):
                q = 4 * m + j
                if kr < q:
                    mask[:, kr, m, :] = 1.0
                elif kr == q:
                    mask[:, kr, m, :] = (t_idx > s_idx).astype(np.float32)
        wsel = np.zeros((128, 4), dtype=np.float32)
        wsel[:, (j - 1) % 4] = 1.0
        d = dict(shared)
        d["mask"] = mask.reshape(128, 16, 512).astype(ml_dtypes.bfloat16)
        d["wsel"] = wsel
        d["cT"] = f(np.asarray(c)[b].reshape(8, 128).T)
        d["w_ada_s"] = f(np.asarray(w_ada)[:, :, j * 1536:(j + 1) * 1536])
        d["b_ada_s"] = f(b_ada_l[:, :, j * 12:(j + 1) * 12])
        per_core.append(d)
    return per_core


def _shard_x(x):
    x = np.asarray(x, dtype=np.float32)
    out = []
    for core in range(NCORES):
        b, j = core // 4, core % 4
        xb = x[b].reshape(64, 128, D)[j::4].reshape(NT, D)
        out.append(np.ascontiguousarray(xb.T))
    return out


def _unshard_x(xs):
    out = np.zeros((2, 64, 128, D), dtype=np.float32)
    for core in range(NCORES):
        b, j = core // 4, core % 4
        out[b, j::4] = np.asarray(xs[core]).T.reshape(NB, 128, D)
    return out.reshape(2, 8192, D)


def _gather(res, name, nq=1):
    outs = []
    for core in range(NCORES):
        b = core // 4
        parts = []
        for q in range(nq):
            for r in range(4):
                a = np.asarray(res[b * 4 + r][name])
                n = a.shape[0] // nq
                parts.append(a[q * n:(q + 1) * n])
        outs.append(np.ascontiguousarray(np.concatenate(parts, axis=0)))
    return outs


def kernel(x, c, w_ada, b_ada, g_pre_mix, g_post_mix, g_pre_mlp, g_post_mlp,
           w_in, conv_w, w_proj_conv, w_proj_attn, w_out, w_mlp_in, w_mlp_out):
    static = _prep_static(c, w_ada, b_ada, g_pre_mix, g_post_mix, g_pre_mlp, g_post_mlp, w_in, conv_w,
                          w_proj_conv, w_proj_attn, w_out, w_mlp_in, w_mlp_out)
    xs = _shard_x(x)
    cores = list(range(NCORES))
    if FUSED:
        nc = _prog((("A", 0), ("B", 0), ("A", 1), ("B", 1)))
        in_maps = [dict(static[i], x_in=xs[i]) for i in cores]
        res = run_bass_kernel_spmd(nc, in_maps, core_ids=cores).results
        return _unshard_x([res[i]["x_out"] for i in cores])
    nc1 = _prog((("A", 0),))
    r1 = run_bass_kernel_spmd(nc1, [dict(static[i], x_in=xs[i]) for i in cores], core_ids=cores).results
    g1, h1 = _gather(r1, "contrib", 4), _gather(r1, "hcon")
    nc2 = _prog((("B", 0), ("A", 1)))
    r2 = run_bass_kernel_spmd(nc2, [dict(static[i], x_in=xs[i], gath=g1[i], hgath=h1[i], qTd_r=np.asarray(r1[i]["qTd_w"]))
                                    for i in cores], core_ids=cores).results
    g2, h2 = _gather(r2, "contrib", 4), _gather(r2, "hcon")
    nc3 = _prog((("B", 1),))
    r3 = run_bass_kernel_spmd(nc3, [dict(static[i], x_in=np.asarray(r2[i]["x_out"]), gath=g2[i], hgath=h2[i],
                                         qTd_r=np.asarray(r2[i]["qTd_w"])) for i in cores], core_ids=cores).results
    return _unshard_x([r3[i]["x_out"] for i in cores])
```
